# Optimizing a Trainium2 kernel written in Bass

```python
import math
import jax, jax.numpy as jnp
from jax import lax
import numpy as np

D_MODEL = 1024
BATCH = 8
SEQ = 4096
DEPTH = 2

CHUNK = 64
ATT_HEADS = 8
ATT_HEAD_DIM = D_MODEL // 16
ATT_WIDTH = ATT_HEADS * ATT_HEAD_DIM
LEFT_CHUNKS = 8
BAND = (LEFT_CHUNKS + 1) * CHUNK
MAX_REL = 128
RET_HEADS = 8
RET_QK_DIM = D_MODEL // 16
RET_V_DIM = D_MODEL // 8
RET_QK_WIDTH = RET_HEADS * RET_QK_DIM
RET_V_WIDTH = RET_HEADS * RET_V_DIM
ROPE_BASE = 10000.0
SSM_WIDTH = D_MODEL // 2
SSM_GROUP = 16
SSM_GROUPS = SSM_WIDTH // SSM_GROUP
SSM_STATE = 64
DT_MIN = 1e-3
DT_MAX = 1e-1
D_FF = 2816
CONV_WIDTH = 3
N_BRANCH = 3
IN_SIZES = (ATT_WIDTH, ATT_WIDTH, ATT_WIDTH, RET_QK_WIDTH, RET_QK_WIDTH, RET_V_WIDTH, RET_V_WIDTH,
            SSM_WIDTH, N_BRANCH * D_MODEL)
IN_OFFSETS = tuple(sum(IN_SIZES[:i]) for i in range(len(IN_SIZES) + 1))
IN_WIDTH = IN_OFFSETS[-1]
DEEPNORM_ALPHA = (2.0 * DEPTH) ** 0.25
DEEPNORM_BETA = (8.0 * DEPTH) ** -0.25
LN_EPS = 1e-5

kernel_name = "hybrid_chunk_causal_gated_merge_encoder"


def layer_norm(x, g, b):
    xf = x.astype(jnp.float32)
    mu = jnp.mean(xf, -1, keepdims=True)
    var = jnp.mean(jnp.square(xf - mu), -1, keepdims=True)
    y = (xf - mu) * lax.rsqrt(var + LN_EPS) * g.astype(jnp.float32) + b.astype(jnp.float32)
    return y.astype(x.dtype)


def in_cols(w, i):
    return w[:, IN_OFFSETS[i]:IN_OFFSETS[i + 1]]


def rotary(x):
    S, d = x.shape[1], x.shape[-1]
    inv = ROPE_BASE ** (-jnp.arange(0, d, 2, dtype=jnp.float32) / d)
    ang = jnp.arange(S, dtype=jnp.float32)[:, None] * inv[None, :]
    cos = jnp.cos(ang)[None, :, None, :]
    sin = jnp.sin(ang)[None, :, None, :]
    xf = x.astype(jnp.float32)
    x1, x2 = xf[..., : d // 2], xf[..., d // 2:]
    return jnp.concatenate([x1 * cos - x2 * sin, x1 * sin + x2 * cos], axis=-1)


def band_attention(q, k, v, rel_bias):
    B, S, H, dh = q.shape
    n_chunks = S // CHUNK
    pad = ((0, 0), (LEFT_CHUNKS * CHUNK, 0), (0, 0), (0, 0))
    kp = jnp.pad(k, pad).reshape(B, n_chunks + LEFT_CHUNKS, CHUNK, H, dh)
    vp = jnp.pad(v, pad).reshape(B, n_chunks + LEFT_CHUNKS, CHUNK, H, dh)
    k_band = jnp.concatenate([kp[:, j:j + n_chunks] for j in range(LEFT_CHUNKS + 1)], axis=2)
    v_band = jnp.concatenate([vp[:, j:j + n_chunks] for j in range(LEFT_CHUNKS + 1)], axis=2)
    qc = q.reshape(B, n_chunks, CHUNK, H, dh)
    s = jnp.einsum('bcqhd,bckhd->bhcqk', qc, k_band).astype(jnp.float32) * (dh ** -0.5)
    q_pos = jnp.arange(CHUNK) + LEFT_CHUNKS * CHUNK
    k_pos = jnp.arange(BAND)
    rel_idx = jnp.clip(q_pos[:, None] - k_pos[None, :], -MAX_REL, MAX_REL) + MAX_REL
    bias = rel_bias[:, rel_idx].astype(jnp.float32)
    key_chunk = jnp.arange(n_chunks)[:, None] - LEFT_CHUNKS + (k_pos // CHUNK)[None, :]
    valid = key_chunk >= 0
    s = jnp.where(valid[None, None, :, None, :], s + bias[:, None], -1e30)
    p = jax.nn.softmax(s, axis=-1)
    out = jnp.einsum('bhcqk,bckhd->bcqhd', p.astype(v.dtype), v_band)
    return out.reshape(B, S, H * dh)


def chunk_retention(q, k, v):
    B, S, H, dk = q.shape
    dv = v.shape[-1]
    n_chunks = S // CHUNK
    L = CHUNK
    f32 = jnp.float32
    log_g = jnp.log(1.0 - jnp.power(2.0, -5.0 - jnp.arange(H, dtype=f32)))
    pos = jnp.arange(L, dtype=f32)
    inner_decay = jnp.exp(log_g[:, None, None] * jnp.abs(pos[:, None] - pos[None, :]))
    qc = (rotary(q) * (dk ** -0.5)).reshape(B, n_chunks, L, H, dk)
    kc = rotary(k).reshape(B, n_chunks, L, H, dk)
    vc = v.astype(f32).reshape(B, n_chunks, L, H, dv)
    s = jnp.einsum('bclhd,bcmhd->bchlm', qc, kc) * inner_decay[None, None]
    inner = jnp.einsum('bchlm,bcmhe->bclhe', s, vc)
    k_decay = jnp.exp(log_g[None, :] * (L - 1 - pos)[:, None])
    chunk_kv = jnp.einsum('bclhd,bclhe->bchde', kc * k_decay[None, None, :, :, None], vc)
    g_chunk = jnp.exp(log_g * L)[None, :, None, None]

    def step(state, kv):
        return state * g_chunk + kv, state

    init = jnp.zeros((B, H, dk, dv), f32)
    _, prev = lax.scan(step, init, jnp.swapaxes(chunk_kv, 0, 1))
    prev = jnp.swapaxes(prev, 0, 1)
    q_decay = jnp.exp(log_g[None, :] * (pos + 1.0)[:, None])
    cross = jnp.einsum('bclhd,bchde->bclhe', qc, prev) * q_decay[None, None, :, :, None]
    out = (inner + cross).reshape(B, S, H, dv)
    mu = jnp.mean(out, -1, keepdims=True)
    var = jnp.mean(jnp.square(out - mu), -1, keepdims=True)
    out = (out - mu) * lax.rsqrt(var + LN_EPS)
    return out.reshape(B, S, H * dv)


def s5_mixer(u, lam_re, lam_im, log_step, b_re, b_im, c_re, c_im, d_skip, w_glu):
    B, S, _ = u.shape
    f32 = jnp.float32
    ug = u.astype(f32).reshape(B, S, SSM_GROUPS, SSM_GROUP)
    lam = lax.complex(lam_re.astype(f32), lam_im.astype(f32))
    step = jnp.exp(log_step.astype(f32))[:, None]
    lam_bar = jnp.exp(lam * step)
    b = lax.complex(b_re.astype(f32), b_im.astype(f32))
    b_bar = ((lam_bar - 1.0) / lam)[..., None] * b
    bu = jnp.einsum('gpk,bsgk->bsgp', b_bar, ug.astype(jnp.complex64))
    a = jnp.broadcast_to(lam_bar, bu.shape)

    def combine(e1, e2):
        a1, x1 = e1
        a2, x2 = e2
        return a1 * a2, a2 * x1 + x2

    _, states = lax.associative_scan(combine, (a, bu), axis=1)
    c = lax.complex(c_re.astype(f32), c_im.astype(f32))
    y = jnp.real(jnp.einsum('gkp,bsgp->bsgk', c, states)) + d_skip.astype(f32).reshape(SSM_GROUPS, SSM_GROUP) * ug
    y = jax.nn.gelu(y.reshape(B, S, SSM_WIDTH))
    y = y * jax.nn.sigmoid(y @ w_glu.astype(f32))
    return y.astype(u.dtype)


def conv_ffn(x, w_up, conv_w, conv_b, w_down):
    h = x @ w_up
    ch = h.shape[-1]
    h = lax.conv_general_dilated(h, conv_w[:, None, :], window_strides=(1,), padding=[(CONV_WIDTH - 1, 0)],
                                 dimension_numbers=('NWC', 'WIO', 'NWC'), feature_group_count=ch) + conv_b
    a, g = h[..., :D_FF], h[..., D_FF:]
    return (jax.nn.gelu(a) * g) @ w_down


def setup_inputs(seed: int = 0) -> dict:
    key = jax.random.key(seed)
    ks = jax.random.split(key, 32)
    f32 = jnp.float32
    nrm = lambda k, shape, scale: jax.random.normal(k, shape, f32) * scale
    L, D = DEPTH, D_MODEL
    G, P, K = SSM_GROUPS, SSM_STATE, SSM_GROUP
    inp = {}
    inp['x'] = jax.random.normal(ks[0], (BATCH, SEQ, D), f32)
    inp['ln_in_g'] = 1.0 + nrm(ks[1], (D,), 0.01)
    inp['ln_in_b'] = nrm(ks[2], (D,), 0.01)
    inp['w_in'] = nrm(ks[3], (L, D, IN_WIDTH), D ** -0.5)
    inp['rel_bias'] = nrm(ks[4], (L, ATT_HEADS, 2 * MAX_REL + 1), 0.1)
    inp['w_proj_a'] = nrm(ks[5], (L, ATT_WIDTH, D), ATT_WIDTH ** -0.5)
    inp['w_proj_b'] = nrm(ks[6], (L, RET_V_WIDTH, D), RET_V_WIDTH ** -0.5)
    inp['w_proj_c'] = nrm(ks[7], (L, SSM_WIDTH, D), SSM_WIDTH ** -0.5)
    inp['lam_re'] = -0.5 + nrm(ks[8], (L, G, P), 0.01)
    inp['lam_im'] = math.pi * jnp.broadcast_to(jnp.arange(P, dtype=f32), (L, G, P)) + nrm(ks[9], (L, G, P), 0.01)
    inp['log_step'] = jax.random.uniform(ks[10], (L, G), f32, math.log(DT_MIN), math.log(DT_MAX))
    inp['b_re'] = nrm(ks[11], (L, G, P, K), (2 * K) ** -0.5)
    inp['b_im'] = nrm(ks[12], (L, G, P, K), (2 * K) ** -0.5)
    inp['c_re'] = nrm(ks[13], (L, G, K, P), (2 * P) ** -0.5)
    inp['c_im'] = nrm(ks[14], (L, G, K, P), (2 * P) ** -0.5)
    inp['d_skip'] = nrm(ks[15], (L, SSM_WIDTH), 1.0)
    inp['w_glu'] = nrm(ks[16], (L, SSM_WIDTH, SSM_WIDTH), SSM_WIDTH ** -0.5)
    inp['w_o'] = nrm(ks[17], (L, D, D), D ** -0.5 * DEEPNORM_BETA)
    inp['ln1_g'] = 1.0 + nrm(ks[18], (L, D), 0.01)
    inp['ln1_b'] = nrm(ks[19], (L, D), 0.01)
    inp['w_up'] = nrm(ks[20], (L, D, 2 * D_FF), D ** -0.5)
    inp['conv_w'] = nrm(ks[21], (L, CONV_WIDTH, 2 * D_FF), CONV_WIDTH ** -0.5)
    inp['conv_b'] = nrm(ks[22], (L, 2 * D_FF), 0.01)
    inp['w_down'] = nrm(ks[23], (L, D_FF, D), D_FF ** -0.5 * DEEPNORM_BETA)
    inp['ln2_g'] = 1.0 + nrm(ks[24], (L, D), 0.01)
    inp['ln2_b'] = nrm(ks[25], (L, D), 0.01)
    return inp


def reference(x, ln_in_g, ln_in_b, w_in, rel_bias, w_proj_a, w_proj_b, w_proj_c, lam_re, lam_im, log_step,
              b_re, b_im, c_re, c_im, d_skip, w_glu, w_o, ln1_g, ln1_b, w_up, conv_w, conv_b, w_down,
              ln2_g, ln2_b):
    B, S, D = x.shape
    h = layer_norm(x, ln_in_g, ln_in_b)
    for l in range(DEPTH):
        w = w_in[l]
        qa = (h @ in_cols(w, 0)).reshape(B, S, ATT_HEADS, ATT_HEAD_DIM)
        ka = (h @ in_cols(w, 1)).reshape(B, S, ATT_HEADS, ATT_HEAD_DIM)
        va = (h @ in_cols(w, 2)).reshape(B, S, ATT_HEADS, ATT_HEAD_DIM)
        pa = band_attention(qa, ka, va, rel_bias[l]) @ w_proj_a[l]
        qb = (h @ in_cols(w, 3)).reshape(B, S, RET_HEADS, RET_QK_DIM)
        kb = (h @ in_cols(w, 4)).reshape(B, S, RET_HEADS, RET_QK_DIM)
        vb = (h @ in_cols(w, 5)).reshape(B, S, RET_HEADS, RET_V_DIM)
        ret = chunk_retention(qb, kb, vb).astype(h.dtype) * jax.nn.silu(h @ in_cols(w, 6))
        pb = ret @ w_proj_b[l]
        uc = h @ in_cols(w, 7)
        pc = s5_mixer(uc, lam_re[l], lam_im[l], log_step[l], b_re[l], b_im[l], c_re[l], c_im[l],
                      d_skip[l], w_glu[l]) @ w_proj_c[l]
        gates = jax.nn.sigmoid(h @ in_cols(w, 8)).reshape(B, S, N_BRANCH, D)
        merged = gates[:, :, 0] * pa + gates[:, :, 1] * pb + gates[:, :, 2] * pc
        h = layer_norm(DEEPNORM_ALPHA * h + merged @ w_o[l], ln1_g[l], ln1_b[l])
        h = layer_norm(DEEPNORM_ALPHA * h + conv_ffn(h, w_up[l], conv_w[l], conv_b[l], w_down[l]),
                       ln2_g[l], ln2_b[l])
    return h
```

```python
import contextlib
import math
import numpy as np
import concourse.bass as bass
import concourse.mybir as mybir
from concourse.bass_utils import run_bass_kernel_spmd

F32 = mybir.dt.float32
BF16 = mybir.dt.bfloat16
I32 = mybir.dt.int32
AF = mybir.ActivationFunctionType
ALU = mybir.AluOpType
AX = mybir.AxisListType

SAME_ENGINE_SYNC = True
N_DMA_SEMS = 12
N_SW_SEMS = 4
DEBUG = False
STOP_AFTER = None
LIMIT_TB = None
RET_CUT = None
LAYERS = (0, 1)
MM_LAZY_INC = True

T = 4096
D = 1024
NTT = 32
NTB = 8
NL = 2
ALPHA = (2.0 * NL) ** 0.25
LN_EPS = 1e-5
TWO_PI = 2.0 * math.pi


class Buf:
    __slots__ = ("name", "w", "r", "excl")

    def __init__(self, name="", excl=False):
        self.name = name
        self.w = None
        self.r = {}
        self.excl = excl


class Sched:
    ENG = ("pe", "act", "dve", "pool", "sp")

    def __init__(self, nc, stack):
        self.nc = nc
        self.streams = {e: [] for e in self.ENG}
        self.count = {e: 0 for e in self.ENG}
        self.seen = {e: {} for e in self.ENG}
        self.csem = {e: stack.enter_context(nc.semaphore("c_" + e)) for e in self.ENG}
        self.dsem = [stack.enter_context(nc.semaphore("d%d" % i)) for i in range(N_DMA_SEMS)]
        self.duse = [0] * N_DMA_SEMS
        self.dnext = 0
        self.dnext_sw = 0
        self.ninst = 0

    def _sem_of(self, key):
        return self.csem[key[1]] if key[0] == "e" else self.dsem[key[1]]

    def _add_wait(self, eng, k, v):
        seen = self.seen[eng]
        if seen.get(k, 0) >= v:
            return
        seen[k] = v
        sem = self._sem_of(k)
        self.streams[eng].append(lambda e: e.wait_ge(sem, v))
        self.ninst += 1

    def _waits(self, eng, reads, writes):
        deps = {}
        for b in reads:
            if b.w is not None:
                k, v = b.w
                if deps.get(k, 0) < v:
                    deps[k] = v
            if b.excl:
                for k, v in b.r.items():
                    if k != ("e", eng) and deps.get(k, 0) < v:
                        deps[k] = v
        for b in writes:
            if b.w is not None:
                k, v = b.w
                if deps.get(k, 0) < v:
                    deps[k] = v
            for k, v in b.r.items():
                if deps.get(k, 0) < v:
                    deps[k] = v
        for k, v in deps.items():
            if k == ("e", eng) and (eng == "pe" or not SAME_ENGINE_SYNC):
                continue
            self._add_wait(eng, k, v)

    def _commit(self, tok, reads, writes):
        k, v = tok
        for b in reads:
            if b.r.get(k, 0) < v:
                b.r[k] = v
        for b in writes:
            b.w = tok
            b.r = {}

    def op(self, eng, fn, reads=(), writes=(), inc=True):
        self._waits(eng, reads, writes)
        if not inc:
            self.streams[eng].append(lambda e: fn(e))
            self.ninst += 1
            tok = (("e", eng), self.count[eng] + 1)
            self._commit(tok, reads, writes)
            return tok
        self.count[eng] += 1
        n = self.count[eng]
        sem = self.csem[eng]
        self.streams[eng].append(lambda e: fn(e).then_inc(sem, 1))
        self.ninst += 1
        tok = (("e", eng), n)
        self._commit(tok, reads, writes)
        return tok

    def dma(self, eng, out, in_, reads=(), writes=(), **kw):
        if eng == "pool":
            s = N_DMA_SEMS - N_SW_SEMS + self.dnext_sw
            self.dnext_sw = (self.dnext_sw + 1) % N_SW_SEMS
        else:
            s = self.dnext
            self.dnext = (self.dnext + 1) % (N_DMA_SEMS - N_SW_SEMS)
        k = ("d", s)
        prev = 16 * self.duse[s]
        if prev > 0:
            self._add_wait(eng, k, prev)
        self._waits(eng, reads, writes)
        self.duse[s] += 1
        sem = self.dsem[s]
        self.streams[eng].append(lambda e: e.dma_start(out=out, in_=in_, **kw).then_inc(sem, 16))
        self.ninst += 1
        tok = (k, 16 * self.duse[s])
        self._commit(tok, reads, writes)
        return tok

    def flush(self):
        toks = [(("d", s), 16 * self.duse[s]) for s in range(N_DMA_SEMS) if self.duse[s]]
        toks += [(("e", e), self.count[e]) for e in self.ENG if self.count[e]]
        for e in self.ENG:
            for k, v in toks:
                if k == ("e", e):
                    continue
                self._add_wait(e, k, v)
        nc = self.nc
        streams = self.streams
        with nc.Block() as block:
            @block.tensor
            def _(eng):
                for f in streams["pe"]:
                    f(eng)

            @block.scalar
            def _(eng):
                for f in streams["act"]:
                    f(eng)

            @block.vector
            def _(eng):
                for f in streams["dve"]:
                    f(eng)

            @block.gpsimd
            def _(eng):
                for f in streams["pool"]:
                    f(eng)

            @block.sync
            def _(eng):
                for f in streams["sp"]:
                    f(eng)
        self.streams = {e: [] for e in self.ENG}


def mm(S, out, lhsT, rhs, start, stop, rd, wr):
    S.op("pe", lambda e: e.matmul(out, lhsT, rhs, start=start, stop=stop), rd, wr, inc=(stop or not MM_LAZY_INC))


def trp(S, out, in_, ident, rd, wr):
    S.op("pe", lambda e: e.transpose(out, in_, ident), rd, wr)


def act(S, out, in_, func, rd, wr, bias=0.0, scale=1.0):
    S.op("act", lambda e: e.activation(out=out, in_=in_, func=func, bias=bias, scale=scale), rd, wr)


def tt(S, eng, out, in0, in1, op, rd, wr):
    S.op(eng, lambda e: e.tensor_tensor(out=out, in0=in0, in1=in1, op=op), rd, wr)


def ts(S, eng, out, in0, s1, s2, op0, op1, rd, wr):
    S.op(eng, lambda e: e.tensor_scalar(out=out, in0=in0, scalar1=s1, scalar2=s2, op0=op0, op1=op1), rd, wr)


def ts1(S, eng, out, in0, s1, op0, rd, wr):
    S.op(eng, lambda e: e.tensor_scalar(out=out, in0=in0, scalar1=s1, scalar2=None, op0=op0), rd, wr)


def stt(S, eng, out, in0, scalar, in1, op0, op1, rd, wr):
    S.op(eng, lambda e: e.scalar_tensor_tensor(out=out, in0=in0, scalar=scalar, in1=in1, op0=op0, op1=op1), rd, wr)


def cp(S, eng, out, in_, rd, wr):
    if eng == "act":
        S.op("act", lambda e: e.copy(out=out, in_=in_), rd, wr)
    else:
        S.op(eng, lambda e: e.tensor_copy(out=out, in_=in_), rd, wr)


def mset(S, eng, ap, val, wr):
    S.op(eng, lambda e: e.memset(ap, val), (), wr)


def recip(S, out, in_, rd, wr):
    S.op("dve", lambda e: e.reciprocal(out=out, in_=in_), rd, wr)


class Ctx:
    pass


def sbt(K, st, name, shape, dtype):
    K.uid += 1
    return st.enter_context(K.nc.sbuf_tensor("%s_%d" % (name, K.uid), list(shape), dtype))


def sincos(K, S, st, ang, shape, b, want_cos):
    yy = sbt(K, st, "sc_y", shape, F32)
    ki = sbt(K, st, "sc_k", shape, I32)
    kf = sbt(K, st, "sc_kf", shape, F32)
    out = sbt(K, st, "sc_o", shape, F32)
    off = (math.pi / 2.0 if want_cos else 0.0)
    ts1(S, "dve", yy[:], ang, off, ALU.add, [b], [b])
    ts1(S, "dve", ki[:], yy[:], 1.0 / TWO_PI, ALU.mult, [b], [b])
    cp(S, "dve", kf[:], ki[:], [b], [b])
    stt(S, "dve", yy[:], kf[:], -TWO_PI, yy[:], ALU.mult, ALU.add, [b], [b])
    ts(S, "dve", kf[:], yy[:], math.pi, -TWO_PI, ALU.is_gt, ALU.mult, [b], [b])
    tt(S, "dve", yy[:], yy[:], kf[:], ALU.add, [b], [b])
    ts(S, "dve", kf[:], yy[:], -math.pi, TWO_PI, ALU.is_lt, ALU.mult, [b], [b])
    tt(S, "dve", yy[:], yy[:], kf[:], ALU.add, [b], [b])
    ts(S, "dve", yy[:], yy[:], math.pi, -math.pi, ALU.min, ALU.max, [b], [b])
    act(S, out[:], yy[:], AF.Sin, [b], [b])
    return out


def ln_scratch(K, st):
    sc = Ctx()
    sc.i = 0
    sc.st = [sbt(K, st, "ln_st", [128, 12], F32) for _ in range(2)]
    sc.mv = [sbt(K, st, "ln_mv", [128, 2], F32) for _ in range(2)]
    sc.sd = [sbt(K, st, "ln_sd", [128, 1], F32) for _ in range(2)]
    sc.hn = [sbt(K, st, "ln_hn", [128, 1024], F32) for _ in range(2)]
    sc.b = [Buf() for _ in range(2)]
    sc.bh = [Buf() for _ in range(2)]
    sc.gt = sbt(K, st, "ln_g", [128, 1024], F32)
    sc.bt = sbt(K, st, "ln_b", [128, 1024], F32)
    sc.bgb = Buf()
    return sc


def ln_load_gb(K, S, sc, g_ap, b_ap):
    S.dma("sp", sc.gt[:], g_ap.partition_broadcast(128), (), [sc.bgb])
    S.dma("sp", sc.bt[:], b_ap.partition_broadcast(128), (), [sc.bgb])


def ln_apply(K, S, sc, src, src_buf, tt_i, dst_dram, dst_buf, ps, ps_buf, final_out=None, final_buf=None):
    i = sc.i
    sc.i = 1 - i
    stt_, mv, sd, hn, b, bh = sc.st[i], sc.mv[i], sc.sd[i], sc.hn[i], sc.b[i], sc.bh[i]
    S.op("dve", lambda e: e.bn_stats(out=stt_[:, 0:6], in_=src[:, 0:512]), [src_buf], [b])
    S.op("dve", lambda e: e.bn_stats(out=stt_[:, 6:12], in_=src[:, 512:1024]), [src_buf], [b])
    S.op("dve", lambda e: e.bn_aggr(out=mv[:, 0:2], in_=stt_[:, 0:12]), [b], [b])
    ts1(S, "dve", sd[:], mv[:, 1:2], LN_EPS, ALU.add, [b], [b])
    act(S, sd[:], sd[:], AF.Sqrt, [b], [b])
    recip(S, sd[:], sd[:], [b], [b])
    stt(S, "dve", mv[:, 1:2], mv[:, 0:1], -1.0, sd[:, 0:1], ALU.mult, ALU.mult, [b], [b])
    S.op("act", lambda e: e.activation(out=hn[:], in_=src, func=AF.Identity, bias=mv[:, 1:2], scale=sd[:, 0:1]), [src_buf, b], [bh])
    tt(S, "dve", hn[:], hn[:], sc.gt[:], ALU.mult, [bh, sc.bgb], [bh])
    tt(S, "dve", hn[:], hn[:], sc.bt[:], ALU.add, [bh, sc.bgb], [bh])
    rows = slice(tt_i * 128, (tt_i + 1) * 128)
    if dst_dram is not None:
        S.dma("sp", dst_dram[rows, :], hn[:], [bh], [dst_buf])
    if final_out is not None:
        S.dma("sp", final_out[rows, :], hn[:], [bh], [final_buf])
        return
    for k in range(8):
        trp(S, ps[:, k * 128:(k + 1) * 128], hn[:, k * 128:(k + 1) * 128], K.identf[:], [bh], [ps_buf])
    cp(S, "act", K.HT[:, :, rows], ps.rearrange("p (k t) -> p k t", k=8), [ps_buf], [K.HTb[tt_i // 4]])


def stage_entry(K, S):
    with contextlib.ExitStack() as st:
        sc = ln_scratch(K, st)
        ln_load_gb(K, S, sc, K.inp["ln_in_g"], K.inp["ln_in_b"])
        xin = [sbt(K, st, "xin", [128, 1024], F32) for _ in range(2)]
        bx = [Buf() for _ in range(2)]
        for t in range(NTT):
            i = t % 2
            S.dma("sp", xin[i][:], K.inp["x"][t * 128:(t + 1) * 128, :], (), [bx[i]])
            ln_apply(K, S, sc, xin[i][:], bx[i], t, K.hres, K.b_hres[t // 4], K.PS[t % 2][:, :], K.PSb[t % 2])
        S.flush()


def build_bias(K, S, st, l, biasT, b_bias):
    Fd = K.Fd
    bF = Buf()
    rb = K.inp["rel_bias"][l]
    with contextlib.ExitStack() as st2:
        fsb = sbt(K, st2, "fsb", [8, 768], F32)
        bfs = Buf()
        S.dma("sp", fsb[:, 0:256], rb[:, 1:257], (), [bfs])
        cp(S, "dve", fsb[:, 256:768], fsb[:, 255:256].broadcast_to([8, 512]), [bfs], [bfs])
        S.dma("sp", Fd[:, :], fsb[:], [bfs], [bF])
        Tk = sbt(K, st2, "Tk", [128, 8, 5, 128], F32)
        bT = Buf()
        for h in range(8):
            for jr in range(5):
                src = bass.AP(tensor=Fd.tensor, offset=h * 768 + 512 - 128 * jr, ap=[[1, 128], [1, 128]])
                S.dma("sp", Tk[:, h, jr, :], src, [bF], [bT])
        for h in range(8):
            p0 = K.PS[3][:, 0:512]
            p1 = K.PS[3][:, 512:640]
            mm(S, p0, K.Jf[:], Tk[:, h, 0:4, :].rearrange("p j q -> p (j q)"), True, True, [bT], [K.PSb[3]])
            mm(S, p1, K.Jf[:], Tk[:, h, 4, :], True, True, [bT], [K.PSb[3]])
            cp(S, "act", biasT[:, h, :, :].rearrange("p j q -> p (j q)"), K.PS[3][:, 0:640], [K.PSb[3]], [b_bias])
            mset(S, "pool", biasT[64:128, h, 4, 0:64], -30000.0, [b_bias])
            mset(S, "pool", biasT[0:64, h, 0, 64:128], -30000.0, [b_bias])
        S.flush()


def load_w(K, S, dst, src, buf):
    S.dma("pool", dst, src, (), [buf])


def stage_attn(K, S, l):
    with contextlib.ExitStack() as st:
        w_in = K.inp["w_in"][l].rearrange("(k p) c -> p k c", p=128)
        wqkv = sbt(K, st, "wqkv", [128, 8, 1536], BF16)
        wg0 = sbt(K, st, "wg0", [128, 8, 1024], BF16)
        wpa = sbt(K, st, "wpa", [128, 4, 1024], BF16)
        bw = Buf()
        for k in range(8):
            load_w(K, S, wqkv[:, k, :], w_in[:, k, 0:1536], bw)
        bw_g = Buf()
        bw_p = Buf()
        for k in range(8):
            load_w(K, S, wg0[:, k, :], w_in[:, k, 5120:6144], bw_g)
        for k_ in range(4):
            load_w(K, S, wpa[:, k_, :], K.inp["w_proj_a"][l].rearrange("(k p) c -> p k c", p=128)[:, k_, :], bw_p)
        biasT = sbt(K, st, "biasT", [128, 8, 5, 128], F32)
        b_bias = Buf()
        build_bias(K, S, st, l, biasT, b_bias)
        if STOP_AFTER == "bias":
            S.dma("sp", K.dbg1[:, :], biasT[:].rearrange("p h j q -> p (h j q)"), [b_bias], [Buf()])
            S.flush()
            return
        qT = [sbt(K, st, "qT", [128, 4, 512], BF16) for _ in range(2)]
        b_qT = [Buf() for _ in range(2)]
        kring = sbt(K, st, "kring", [128, 4, 1024], BF16)
        b_kr = [Buf() for _ in range(2)]
        vring = sbt(K, st, "vring", [128, 8, 8, 65], BF16)
        b_vr = [Buf() for _ in range(8)]
        mset(S, "pool", vring[:], 1.0, b_vr)
        sbf = [sbt(K, st, "sbf", [128, 640], F32) for _ in range(2)]
        pT = [sbt(K, st, "pT", [128, 640], BF16) for _ in range(2)]
        b_sbf = [Buf() for _ in range(2)]
        b_pT = [Buf() for _ in range(2)]
        att = [sbt(K, st, "att", [128, 512], F32) for _ in range(2)]
        rs = [sbt(K, st, "rs", [128, 8], F32) for _ in range(2)]
        b_att = [Buf() for _ in range(2)]
        attT = [sbt(K, st, "attT", [128, 4, 512], BF16) for _ in range(2)]
        b_attT = [Buf() for _ in range(2)]
        gsb = [sbt(K, st, "gsb", [128, 512], F32) for _ in range(2)]
        b_gsb = [Buf() for _ in range(2)]
        mrg1 = sbt(K, st, "mrg", [128, 8, 512], BF16)
        mrg = [mrg1, mrg1]
        b_mrg1 = Buf()
        b_mrg = [b_mrg1, b_mrg1]
        psS = [K.PS[0], K.PS[1]]
        b_psS = [K.PSb[0], K.PSb[1]]
        psO = [K.PS[2][:, 0:512], K.PS[2][:, 512:1024]]
        b_psO = [K.PSh[4], K.PSh[5]]
        psM = [K.PS[3][:, 0:512], K.PS[3][:, 512:1024]]
        b_psM = [K.PSh[6], K.PSh[7]]
        K.PSh[6].w = K.PSb[3].w
        K.PSh[7].w = K.PSb[3].w
        K.PSh[6].r = dict(K.PSb[3].r)
        K.PSh[7].r = dict(K.PSb[3].r)
        mi = 0
        for tb in range(NTB if LIMIT_TB is None else LIMIT_TB):
            blk = slice(tb * 512, (tb + 1) * 512)
            qs = tb % 2
            hb = K.HTb[tb]
            for ot in range(8):
                m = mi % 2
                mi += 1
                for k in range(8):
                    mm(S, psM[m], wqkv[:, k, ot * 128:(ot + 1) * 128], K.HT[:, k, blk], k == 0, k == 7, [bw, hb], [b_psM[m]])
                if ot < 4:
                    act(S, qT[qs][:, ot, :], psM[m], AF.Copy, [b_psM[m]], [b_qT[qs]], scale=0.125)
                else:
                    cp(S, "dve", kring[:, ot - 4, qs * 512:(qs + 1) * 512], psM[m], [b_psM[m]], [b_kr[qs]])
            for t4 in range(4):
                t = tb * 4 + t4
                slot = t % 8
                m = mi % 2
                mi += 1
                for k in range(8):
                    mm(S, psM[m], K.HT[:, k, t * 128:(t + 1) * 128], wqkv[:, k, 1024:1536], k == 0, k == 7, [bw, hb], [b_psM[m]])
                cp(S, "act", vring[:, slot, :, 0:64], psM[m].rearrange("p (h d) -> p h d", h=8), [b_psM[m]], [b_vr[slot]])
            for s4 in range(4):
                sc_ = tb * 4 + s4
                nk = min(sc_, 4) + 1
                jr0 = 5 - nk
                ai = sc_ % 2
                for h in range(8):
                    hp, tq = h % 2, h // 2
                    x = h % 2
                    prt = slice(hp * 64, (hp + 1) * 64)
                    for j in range(nk):
                        kt = sc_ - (nk - 1) + j
                        slot = kt % 8
                        mm(S, psS[x][:, j * 128:(j + 1) * 128], kring[prt, tq, slot * 128:(slot + 1) * 128],
                           qT[qs][prt, tq, s4 * 128:(s4 + 1) * 128], True, True, [b_kr[slot // 4], b_qT[qs]], [b_psS[x]])
                    n = nk * 128
                    tt(S, "dve", sbf[x][:, 0:n], psS[x][:, 0:n],
                       biasT[:, h, jr0:5, :].rearrange("p j q -> p (j q)"), ALU.add, [b_psS[x], b_bias], [b_sbf[x]])
                    act(S, pT[x][:, 0:n], sbf[x][:, 0:n], AF.Exp, [b_sbf[x]], [b_pT[x]])
                    for j in range(nk):
                        kt = sc_ - (nk - 1) + j
                        slot = kt % 8
                        mm(S, psO[h // 4][:, (h % 4) * 65:(h % 4) * 65 + 65], pT[x][:, j * 128:(j + 1) * 128],
                           vring[:, slot, h, :], j == 0, j == nk - 1, [b_pT[x], b_vr[slot]], [b_psO[h // 4]])
                for half in range(2):
                    pv = psO[half][:, 0:260].rearrange("p (h e) -> p h e", h=4)
                    recip(S, rs[ai][:, half * 4:(half + 1) * 4], pv[:, :, 64], [b_psO[half]], [b_att[ai]])
                    tt(S, "dve", att[ai][:, half * 256:(half + 1) * 256].rearrange("p (h d) -> p h d", h=4), pv[:, :, 0:64],
                       rs[ai][:, half * 4:(half + 1) * 4].unsqueeze(2).broadcast_to([128, 4, 64]), ALU.mult,
                       [b_psO[half], b_att[ai]], [b_att[ai]])
                m = mi % 2
                mi += 1
                for f in range(4):
                    trp(S, psM[m][:, f * 128:(f + 1) * 128], att[ai][:, f * 128:(f + 1) * 128], K.identf[:], [b_att[ai]], [b_psM[m]])
                cp(S, "act", attT[qs][:, :, s4 * 128:(s4 + 1) * 128], psM[m].rearrange("p (f t) -> p f t", f=4), [b_psM[m]], [b_attT[qs]])
            for ot in range(8):
                g = ot % 2
                m = mi % 2
                mi += 1
                for k in range(8):
                    mm(S, psM[m], wg0[:, k, ot * 128:(ot + 1) * 128], K.HT[:, k, blk], k == 0, k == 7, [bw_g, hb], [b_psM[m]])
                act(S, gsb[g][:], psM[m], AF.Sigmoid, [b_psM[m]], [b_gsb[g]])
                m = mi % 2
                mi += 1
                for f in range(4):
                    mm(S, psM[m], wpa[:, f, ot * 128:(ot + 1) * 128], attT[qs][:, f, :], f == 0, f == 3, [bw_p, b_attT[qs]], [b_psM[m]])
                tt(S, "dve", mrg[qs][:, ot, :], psM[m], gsb[g][:], ALU.mult, [b_psM[m], b_gsb[g]], [b_mrg[qs]])
            for o2 in range(4):
                S.dma("sp", K.merged[:, 2 * o2:2 * o2 + 2, blk], mrg[qs][:, 2 * o2:2 * o2 + 2, :], [b_mrg[qs]], [K.b_merged[tb]])
        S.flush()


def stage_ret(K, S, l):
    with contextlib.ExitStack() as st:
        w_in = K.inp["w_in"][l].rearrange("(k p) c -> p k c", p=128)
        wqk = sbt(K, st, "wqk", [128, 8, 1024], BF16)
        wv = sbt(K, st, "wv", [128, 8, 1024], BF16)
        wgr = sbt(K, st, "wgr", [128, 8, 1024], BF16)
        wg1 = sbt(K, st, "wg1", [128, 8, 1024], BF16)
        wpb = sbt(K, st, "wpb", [128, 8, 1024], BF16)
        bw = Buf()
        for k in range(8):
            load_w(K, S, wqk[:, k, :], w_in[:, k, 1536:2560], bw)
        bw_v, bw_gr, bw_g1, bw_pb = Buf(), Buf(), Buf(), Buf()
        for k in range(8):
            load_w(K, S, wv[:, k, :], w_in[:, k, 2560:3584], bw_v)
        for k in range(8):
            load_w(K, S, wgr[:, k, :], w_in[:, k, 3584:4608], bw_gr)
        for k in range(8):
            load_w(K, S, wg1[:, k, :], w_in[:, k, 6144:7168], bw_g1)
        for k_ in range(8):
            load_w(K, S, wpb[:, k_, :], K.inp["w_proj_b"][l].rearrange("(k p) c -> p k c", p=128)[:, k_, :], bw_pb)
        cosT = sbt(K, st, "cosT", [128, 32, 32], F32)
        sinT = sbt(K, st, "sinT", [128, 32, 32], F32)
        DT = sbt(K, st, "DT", [128, 8, 128], F32)
        qdecT = sbt(K, st, "qdecT", [128, 4, 128], F32)
        kdec = sbt(K, st, "kdec", [128, 8], F32)
        g128 = sbt(K, st, "g128", [128, 4], F32)
        btab = Buf()
        logg = [math.log(1.0 - 2.0 ** (-5.0 - h)) for h in range(8)]
        with contextlib.ExitStack() as st2:
            ii = sbt(K, st2, "ii", [128, 128], I32)
            ff = sbt(K, st2, "ff", [128, 128], F32)
            posf = sbt(K, st2, "posf", [128, 32], F32)
            invf = sbt(K, st2, "invf", [128, 32], F32)
            ang = sbt(K, st2, "ang", [128, 32, 32], F32)
            bt2 = Buf()
            S.op("pool", lambda e: e.iota(ii[:, 0:32], pattern=[[128, 32]], base=0, channel_multiplier=1), (), [bt2])
            cp(S, "dve", posf[:], ii[:, 0:32], [bt2], [bt2])
            for i in range(32):
                mset(S, "pool", invf[:, i:i + 1], float(np.float32(10000.0 ** (-2.0 * i / 64.0))), [bt2])
            tt(S, "dve", ang[:], posf[:].unsqueeze(2).broadcast_to([128, 32, 32]),
               invf[:].unsqueeze(1).broadcast_to([128, 32, 32]), ALU.mult, [bt2], [bt2])
            c_ = sincos(K, S, st2, ang[:], [128, 32, 32], bt2, True)
            cp(S, "dve", cosT[:], c_[:], [bt2], [btab])
            s_ = sincos(K, S, st2, ang[:], [128, 32, 32], bt2, False)
            cp(S, "dve", sinT[:], s_[:], [bt2], [btab])
            S.op("pool", lambda e: e.iota(ii[:], pattern=[[1, 128]], base=0, channel_multiplier=-1), [bt2], [bt2])
            cp(S, "dve", ff[:], ii[:], [bt2], [bt2])
            ts1(S, "dve", ang[:, 0:4, :].rearrange("p a b -> p (a b)"), ff[:], -1.0, ALU.mult, [bt2], [bt2])
            tt(S, "dve", ff[:], ff[:], ang[:, 0:4, :].rearrange("p a b -> p (a b)"), ALU.max, [bt2], [bt2])
            for h in range(8):
                act(S, DT[:, h, :], ff[:], AF.Exp, [bt2], [btab], scale=logg[h])
            mset(S, "pool", DT[64:128, :, 0:64], 0.0, [btab])
            S.op("pool", lambda e: e.iota(ii[:], pattern=[[1, 128]], base=1, channel_multiplier=0), [bt2], [bt2])
            cp(S, "dve", ff[:], ii[:], [bt2], [bt2])
            for h in range(8):
                prt = slice((h % 2) * 64, (h % 2) * 64 + 64)
                act(S, qdecT[prt, h // 2, :], ff[prt, :], AF.Exp, [bt2], [btab], scale=logg[h])
                mset(S, "pool", g128[prt, h // 2:h // 2 + 1], math.exp(128.0 * logg[h]), [btab])
            S.op("pool", lambda e: e.iota(ii[:, 0:1], pattern=[[0, 1]], base=127, channel_multiplier=-1), [bt2], [bt2])
            cp(S, "dve", ff[:, 0:1], ii[:, 0:1], [bt2], [bt2])
            for h in range(8):
                act(S, kdec[:, h:h + 1], ff[:, 0:1], AF.Exp, [bt2], [btab], scale=logg[h])
            S.flush()
        if STOP_AFTER == "rettab":
            return
        xs1 = sbt(K, st, "xs", [128, 512], F32)
        xs = [xs1, xs1]
        xr = [sbt(K, st, "xr", [128, 512], F32) for _ in range(2)]
        tmp = [sbt(K, st, "rtmp", [128, 8, 32], F32) for _ in range(4)]
        b_x1 = Buf()
        b_x = [b_x1, b_x1]
        kcd = sbt(K, st, "kcd", [128, 512], BF16)
        qcT = sbt(K, st, "qcT", [128, 4, 128], BF16)
        qcdT = sbt(K, st, "qcdT", [128, 4, 128], BF16)
        kcT = sbt(K, st, "kcT", [128, 4, 128], BF16)
        vbf = sbt(K, st, "vbf", [128, 1024], BF16)
        b_qk = Buf()
        b_v = Buf()
        sTb = sbt(K, st, "sTb", [128, 1024], BF16)
        b_sT = Buf()
        state = sbt(K, st, "state", [128, 4, 128], F32)
        state_bf = sbt(K, st, "state_bf", [128, 4, 128], BF16)
        b_state = Buf()
        b_sbf = Buf()
        ret = sbt(K, st, "ret", [128, 1024], F32)
        gs = sbt(K, st, "gs", [128, 1024], F32)
        sq = gs
        stat = sbt(K, st, "stat", [128, 6, 8], F32)
        b_ret = Buf()
        b_gs = Buf()
        b_stat = Buf()
        retT1 = sbt(K, st, "retT", [128, 8, 512], BF16)
        retT = [retT1, retT1]
        b_retT1 = Buf()
        b_retT = [b_retT1, b_retT1]
        gsb1 = sbt(K, st, "gsbB", [128, 512], F32)
        gsb = [gsb1, gsb1]
        b_gsb1 = Buf()
        b_gsb = [b_gsb1, b_gsb1]
        mprev = sbt(K, st, "mprev", [128, 8, 512], BF16)
        mrg = mprev
        b_mprev = Buf()
        b_mrg = b_mprev
        psS, b_pS = K.PS[0], K.PSb[0]
        psOo, b_pO = K.PS[1], K.PSb[1]
        psK, b_pK = K.PS[2], K.PSb[2]
        psM = [K.PS[3][:, 0:512], K.PS[3][:, 512:1024]]
        b_psM = [K.PSh[6], K.PSh[7]]
        mset(S, "pool", state[:], 0.0, [b_state])
        mset(S, "pool", state_bf[:], 0.0, [b_sbf])
        mi = 0
        for tb in range(NTB if LIMIT_TB is None else LIMIT_TB):
            blk = slice(tb * 512, (tb + 1) * 512)
            hb = K.HTb[tb]
            rs_ = tb % 2
            for o2 in range(4):
                S.dma("sp", mprev[:, 2 * o2:2 * o2 + 2, :], K.merged[:, 2 * o2:2 * o2 + 2, blk], [K.b_merged[tb]], [b_mprev])
            for s4 in range(4):
                t = tb * 4 + s4
                tok = slice(t * 128, (t + 1) * 128)
                C = cosT[:, t, :].unsqueeze(1).broadcast_to([128, 8, 32])
                Sn = sinT[:, t, :].unsqueeze(1).broadcast_to([128, 8, 32])
                for qk in range(2):
                    m = mi % 2
                    mi += 1
                    for k in range(8):
                        mm(S, psM[m], K.HT[:, k, tok], wqk[:, k, qk * 512:(qk + 1) * 512], k == 0, k == 7, [bw, hb], [b_psM[m]])
                    cp(S, "act", xs[qk][:], psM[m], [b_psM[m]], [b_x[qk]])
                    x3 = xs[qk][:].rearrange("p (h d) -> p h d", h=8)
                    o3 = xr[qk][:].rearrange("p (h d) -> p h d", h=8)
                    x1, x2 = x3[:, :, 0:32], x3[:, :, 32:64]
                    tt(S, "dve", tmp[0][:], x1, C, ALU.mult, [b_x[qk], btab], [b_x[qk]])
                    tt(S, "dve", tmp[1][:], x2, Sn, ALU.mult, [b_x[qk], btab], [b_x[qk]])
                    tt(S, "dve", o3[:, :, 0:32], tmp[0][:], tmp[1][:], ALU.subtract, [b_x[qk]], [b_x[qk]])
                    tt(S, "dve", tmp[2][:], x1, Sn, ALU.mult, [b_x[qk], btab], [b_x[qk]])
                    tt(S, "dve", tmp[3][:], x2, C, ALU.mult, [b_x[qk], btab], [b_x[qk]])
                    tt(S, "dve", o3[:, :, 32:64], tmp[2][:], tmp[3][:], ALU.add, [b_x[qk]], [b_x[qk]])
                if RET_CUT == 1:
                    S.flush()
                    return
                tt(S, "dve", kcd[:].rearrange("p (h d) -> p h d", h=8), xr[1][:].rearrange("p (h d) -> p h d", h=8),
                   kdec[:].unsqueeze(2).broadcast_to([128, 8, 64]), ALU.mult, [b_x[1], btab], [b_qk])
                m = mi % 2
                mi += 1
                for f in range(4):
                    trp(S, psM[m][:, f * 128:(f + 1) * 128], xr[0][:, f * 128:(f + 1) * 128], K.identf[:], [b_x[0]], [b_psM[m]])
                act(S, qcT[:].rearrange("p f t -> p (f t)"), psM[m], AF.Copy, [b_psM[m]], [b_qk], scale=0.125)
                stt(S, "dve", qcdT[:].rearrange("p f t -> p (f t)"), psM[m], 0.125, qdecT[:].rearrange("p f t -> p (f t)"),
                    ALU.mult, ALU.mult, [b_psM[m], btab], [b_qk])
                m = mi % 2
                mi += 1
                for f in range(4):
                    trp(S, psM[m][:, f * 128:(f + 1) * 128], xr[1][:, f * 128:(f + 1) * 128], K.identf[:], [b_x[1]], [b_psM[m]])
                cp(S, "act", kcT[:].rearrange("p f t -> p (f t)"), psM[m], [b_psM[m]], [b_qk])
                if RET_CUT == 2:
                    S.flush()
                    return
                for half in range(2):
                    m = mi % 2
                    mi += 1
                    for k in range(8):
                        mm(S, psM[m], K.HT[:, k, tok], wv[:, k, half * 512:(half + 1) * 512], k == 0, k == 7, [bw_v, hb], [b_psM[m]])
                    cp(S, "act", vbf[:, half * 512:(half + 1) * 512], psM[m], [b_psM[m]], [b_v])
                if RET_CUT == 3:
                    S.flush()
                    return
                for h in range(8):
                    prt = slice((h % 2) * 64, (h % 2) * 64 + 64)
                    c0 = (h % 2) * 512 + (h // 2) * 128
                    mm(S, psS[:, c0:c0 + 128], kcT[prt, h // 2, :], qcT[prt, h // 2, :], True, True, [b_qk], [b_pS])
                if RET_CUT == 40:
                    S.flush()
                    return
                for half in range(2):
                    cs = slice(half * 512, (half + 1) * 512)
                    tt(S, "dve", sTb[:, cs].rearrange("p (h l) -> p h l", h=4), psS[:, cs].rearrange("p (h l) -> p h l", h=4),
                       DT[:, half::2, :], ALU.mult, [b_pS, btab], [b_sT])
                if RET_CUT == 4:
                    S.flush()
                    return
                for h in range(8):
                    prt = slice((h % 2) * 64, (h % 2) * 64 + 64)
                    hs = slice(h * 128, (h + 1) * 128)
                    c0 = (h % 2) * 512 + (h // 2) * 128
                    mm(S, psOo[:, hs], sTb[:, c0:c0 + 128], vbf[:, hs], True, False, [b_sT, b_v], [b_pO])
                    mm(S, psOo[:, hs], qcdT[prt, h // 2, :], state_bf[prt, h // 2, :], False, True, [b_qk, b_sbf], [b_pO])
                if RET_CUT == 5:
                    S.flush()
                    return
                for h in range(8):
                    hs = slice(h * 128, (h + 1) * 128)
                    pr = (h // 2) * 128
                    mm(S, psK[:, hs], kcd[:, pr:pr + 128], vbf[:, hs], True, True, [b_qk, b_v], [b_pK])
                for h in range(8):
                    prt = slice((h % 2) * 64, (h % 2) * 64 + 64)
                    hs = slice(h * 128, (h + 1) * 128)
                    stt(S, "dve", state[prt, h // 2, :], state[prt, h // 2, :], g128[prt, h // 2:h // 2 + 1], psK[prt, hs],
                        ALU.mult, ALU.add, [b_state, b_pK, btab], [b_state])
                cp(S, "act", state_bf[:], state[:], [b_state], [b_sbf])
                if RET_CUT == 6:
                    S.flush()
                    return
                o3 = psOo[:, :].rearrange("p (h e) -> p h e", h=8)
                S.op("dve", lambda e, o3=o3: e.tensor_reduce(out=stat[:, 0, :], in_=o3, axis=AX.X, op=ALU.add), [b_pO], [b_stat])
                act(S, sq[:], psOo[:, :], AF.Square, [b_pO], [b_gs])
                S.op("dve", lambda e: e.tensor_reduce(out=stat[:, 1, :], in_=sq[:].rearrange("p (h e) -> p h e", h=8), axis=AX.X, op=ALU.add),
                     [b_gs], [b_stat])
                ts1(S, "dve", stat[:, 2, :], stat[:, 0, :], 1.0 / 128.0, ALU.mult, [b_stat], [b_stat])
                tt(S, "dve", stat[:, 3, :], stat[:, 2, :], stat[:, 2, :], ALU.mult, [b_stat], [b_stat])
                stt(S, "dve", stat[:, 4, :], stat[:, 1, :], 1.0 / 128.0, stat[:, 3, :], ALU.mult, ALU.subtract, [b_stat], [b_stat])
                ts1(S, "dve", stat[:, 4, :], stat[:, 4, :], LN_EPS, ALU.add, [b_stat], [b_stat])
                act(S, stat[:, 4, :], stat[:, 4, :], AF.Sqrt, [b_stat], [b_stat])
                recip(S, stat[:, 5, :], stat[:, 4, :], [b_stat], [b_stat])
                r3 = ret[:].rearrange("p (h e) -> p h e", h=8)
                tt(S, "dve", r3, o3, stat[:, 2, :].unsqueeze(2).broadcast_to([128, 8, 128]), ALU.subtract, [b_pO, b_stat, b_ret], [b_ret])
                tt(S, "dve", r3, r3, stat[:, 5, :].unsqueeze(2).broadcast_to([128, 8, 128]), ALU.mult, [b_ret, b_stat], [b_ret])
                if RET_CUT == 7:
                    S.flush()
                    return
                for half in range(2):
                    m = mi % 2
                    mi += 1
                    for k in range(8):
                        mm(S, psM[m], K.HT[:, k, tok], wgr[:, k, half * 512:(half + 1) * 512], k == 0, k == 7, [bw_gr, hb], [b_psM[m]])
                    act(S, gs[:, half * 512:(half + 1) * 512], psM[m], AF.Silu, [b_psM[m]], [b_gs])
                tt(S, "dve", ret[:], ret[:], gs[:], ALU.mult, [b_ret, b_gs], [b_ret])
                for half in range(2):
                    m = mi % 2
                    mi += 1
                    for f in range(4):
                        c0 = (half * 4 + f) * 128
                        trp(S, psM[m][:, f * 128:(f + 1) * 128], ret[:, c0:c0 + 128], K.identf[:], [b_ret], [b_psM[m]])
                    cp(S, "act", retT[rs_][:, half * 4:(half + 1) * 4, s4 * 128:(s4 + 1) * 128],
                       psM[m].rearrange("p (f t) -> p f t", f=4), [b_psM[m]], [b_retT[rs_]])
            for ot in range(8):
                g = ot % 2
                m = mi % 2
                mi += 1
                for k in range(8):
                    mm(S, psM[m], wg1[:, k, ot * 128:(ot + 1) * 128], K.HT[:, k, blk], k == 0, k == 7, [bw_g1, hb], [b_psM[m]])
                act(S, gsb[g][:], psM[m], AF.Sigmoid, [b_psM[m]], [b_gsb[g]])
                m = mi % 2
                mi += 1
                for f in range(8):
                    mm(S, psM[m], wpb[:, f, ot * 128:(ot + 1) * 128], retT[rs_][:, f, :], f == 0, f == 7, [bw_pb, b_retT[rs_]], [b_psM[m]])
                tt(S, "dve", gsb[g][:], psM[m], gsb[g][:], ALU.mult, [b_psM[m], b_gsb[g]], [b_gsb[g]])
                tt(S, "dve", mrg[:, ot, :], gsb[g][:], mprev[:, ot, :], ALU.add, [b_gsb[g], b_mprev], [b_mrg])
            for o2 in range(4):
                S.dma("sp", K.merged[:, 2 * o2:2 * o2 + 2, blk], mrg[:, 2 * o2:2 * o2 + 2, :], [b_mrg], [K.b_merged[tb]])
        S.flush()


PW = list(range(9)) + [16, 32, 64, 128, 256, 512, 1024, 2048]
PWI = {m: i for i, m in enumerate(PW)}
NPW = len(PW)


def s5_setup(K, S, l, Tt, ARt, AIt, AInt, b_T):
    with contextlib.ExitStack() as st:
        b = Buf()
        raw = sbt(K, st, "raw", [16, 3, 128], F32)
        ls2 = sbt(K, st, "ls2", [16, 2], F32)
        S.dma("sp", raw[:, 0, :], K.inp["lam_re"][l].rearrange("(pi g) p -> pi (g p)", g=2), (), [b])
        S.dma("sp", raw[:, 1, :], K.inp["lam_im"][l].rearrange("(pi g) p -> pi (g p)", g=2), (), [b])
        S.dma("sp", ls2[:], K.inp["log_step"][l].rearrange("(pi g) -> pi g", g=2), (), [b])
        cp(S, "dve", raw[:, 2, :].rearrange("q (g p) -> q g p", g=2), ls2[:].unsqueeze(2).broadcast_to([16, 2, 64]), [b], [b])
        draw = sbt(K, st, "draw", [4, 128], F32)
        S.dma("sp", draw[:], K.inp["d_skip"][l].rearrange("(a q) -> a q", q=128), (), [b])
        ps = K.PS[3]
        pb = K.PSb[3]
        for i in range(3):
            trp(S, ps[:, i * 16:(i + 1) * 16], raw[:, i, :], K.identf[0:16, 0:16], [b], [pb])
        trp(S, ps[:, 48:52], draw[:], K.identf[0:4, 0:4], [b], [pb])
        lam = sbt(K, st, "lam", [128, 4, 16], F32)
        dcol = sbt(K, st, "dcol", [128, 4], F32)
        cp(S, "dve", lam[:, 0:3, :].rearrange("p a b -> p (a b)"), ps[:, 0:48], [pb], [b])
        cp(S, "dve", dcol[:], ps[:, 48:52], [pb], [b])
        lr, li, stp = lam[:, 0, :], lam[:, 1, :], lam[:, 2, :]
        act(S, stp, stp, AF.Exp, [b], [b])
        er = sbt(K, st, "er", [128, 16], F32)
        th = sbt(K, st, "th", [128, 16], F32)
        tt(S, "dve", er[:], lr, stp, ALU.mult, [b], [b])
        tt(S, "dve", th[:], li, stp, ALU.mult, [b], [b])
        angM = sbt(K, st, "angM", [128, NPW + 1, 16], F32)
        magM = sbt(K, st, "magM", [128, NPW, 16], F32)
        for i, m in enumerate(PW):
            ts1(S, "dve", angM[:, i, :], th[:], float(m), ALU.mult, [b], [b])
            act(S, magM[:, i, :], er[:], AF.Exp, [b], [b], scale=float(m))
        ts1(S, "dve", angM[:, NPW, :], th[:], 0.5, ALU.mult, [b], [b])
        cosM = sincos(K, S, st, angM[:], [128, NPW + 1, 16], b, True)
        sinM = sincos(K, S, st, angM[:], [128, NPW + 1, 16], b, False)
        tt(S, "dve", ARt[:], magM[:], cosM[:, 0:NPW, :], ALU.mult, [b], [b_T])
        tt(S, "dve", AIt[:], magM[:], sinM[:, 0:NPW, :], ALU.mult, [b], [b_T])
        ts1(S, "dve", AInt[:], AIt[:], -1.0, ALU.mult, [b_T], [b_T])
        w = sbt(K, st, "wk", [128, 8, 16], F32)
        t0, t1, nr, ni, den, cr, ci, t2 = [w[:, i, :] for i in range(8)]
        ts(S, "dve", t0, er[:], 1.0 / 120.0, 1.0 / 24.0, ALU.mult, ALU.add, [b], [b])
        for cst in (1.0 / 6.0, 0.5, 1.0):
            tt(S, "dve", t0, t0, er[:], ALU.mult, [b], [b])
            ts1(S, "dve", t0, t0, cst, ALU.add, [b], [b])
        tt(S, "dve", t0, t0, er[:], ALU.mult, [b], [b])
        tt(S, "dve", nr, t0, cosM[:, 1, :], ALU.mult, [b], [b])
        tt(S, "dve", t1, sinM[:, NPW, :], sinM[:, NPW, :], ALU.mult, [b], [b])
        stt(S, "dve", nr, t1, -2.0, nr, ALU.mult, ALU.add, [b], [b])
        cp(S, "dve", ni, AIt[:, 1, :], [b, b_T], [b])
        tt(S, "dve", den, lr, lr, ALU.mult, [b], [b])
        tt(S, "dve", t1, li, li, ALU.mult, [b], [b])
        tt(S, "dve", den, den, t1, ALU.add, [b], [b])
        recip(S, den, den, [b], [b])
        tt(S, "dve", cr, nr, lr, ALU.mult, [b], [b])
        tt(S, "dve", t1, ni, li, ALU.mult, [b], [b])
        tt(S, "dve", cr, cr, t1, ALU.add, [b], [b])
        tt(S, "dve", cr, cr, den, ALU.mult, [b], [b])
        tt(S, "dve", ci, ni, lr, ALU.mult, [b], [b])
        tt(S, "dve", t1, nr, li, ALU.mult, [b], [b])
        tt(S, "dve", ci, ci, t1, ALU.subtract, [b], [b])
        tt(S, "dve", ci, ci, den, ALU.mult, [b], [b])
        Bn = sbt(K, st, "Bn", [128, 4, 16, 16], F32)
        for ri, nm in enumerate(("b_re", "b_im")):
            srcB = K.inp[nm][l].rearrange("(pi g) p k -> (g p) pi k", g=2)
            for pi_ in range(16):
                S.dma("sp", Bn[:, ri, pi_, :], srcB[:, pi_, :], (), [b])
        tmpB = sbt(K, st, "tmpB", [128, 16, 16], F32)

        def bc(x):
            return x.unsqueeze(2).broadcast_to([128, 16, 16])
        tt(S, "dve", Bn[:, 2], Bn[:, 0], bc(cr), ALU.mult, [b], [b])
        tt(S, "dve", tmpB[:], Bn[:, 1], bc(ci), ALU.mult, [b], [b])
        tt(S, "dve", Bn[:, 2], Bn[:, 2], tmpB[:], ALU.subtract, [b], [b])
        tt(S, "dve", Bn[:, 3], Bn[:, 1], bc(cr), ALU.mult, [b], [b])
        tt(S, "dve", tmpB[:], Bn[:, 0], bc(ci), ALU.mult, [b], [b])
        tt(S, "dve", Bn[:, 3], Bn[:, 3], tmpB[:], ALU.add, [b], [b])
        WA = sbt(K, st, "WA", [128, 2, 8, 16, 16], F32)
        for jp in range(8):
            pi_ = PWI[7 - jp]
            ar, ai = bc(ARt[:, pi_, :]), bc(AIt[:, pi_, :])
            tt(S, "dve", WA[:, 0, jp], Bn[:, 2], ar, ALU.mult, [b, b_T], [b])
            tt(S, "dve", tmpB[:], Bn[:, 3], ai, ALU.mult, [b, b_T], [b])
            tt(S, "dve", WA[:, 0, jp], WA[:, 0, jp], tmpB[:], ALU.subtract, [b], [b])
            tt(S, "dve", WA[:, 1, jp], Bn[:, 3], ar, ALU.mult, [b, b_T], [b])
            tt(S, "dve", tmpB[:], Bn[:, 2], ai, ALU.mult, [b, b_T], [b])
            tt(S, "dve", WA[:, 1, jp], WA[:, 1, jp], tmpB[:], ALU.add, [b], [b])
        craw = sbt(K, st, "craw", [128, 2, 4, 2, 64], F32)
        for ri, nm in enumerate(("c_re", "c_im")):
            src = K.inp[nm][l].rearrange("(a g) k p -> (g k) a p", g=8)
            for a in range(4):
                S.dma("sp", craw[:, ri, a, 0, :], src[:, a, :], (), [b])
                S.dma("sp", craw[:, ri, a, 1, :], src[:, a, :], (), [b])
        CT = sbt(K, st, "CT", [128, 2, 16, 16], F32)
        for ri in range(2):
            for a in range(4):
                pa_ = K.PS[2][:, (a % 2) * 512:(a % 2) * 512 + 128]
                pba = K.PSh[4 + a % 2]
                trp(S, pa_, craw[:, ri, a].rearrange("p d q -> p (d q)"), K.identf[:], [b], [pba])
                p3 = pa_.rearrange("p (g k) -> p g k", k=16)
                cp(S, "dve", CT[0:64, ri, 4 * a:4 * a + 4, :], p3[0:64, 0::2, :], [pba], [b])
                cp(S, "dve", CT[64:128, ri, 4 * a:4 * a + 4, :], p3[64:128, 1::2, :], [pba], [b])
        VA = sbt(K, st, "VA", [128, 2, 8, 16, 16], F32)
        for j in range(8):
            pi_ = PWI[j + 1]
            ar, ai = bc(ARt[:, pi_, :]), bc(AIt[:, pi_, :])
            tt(S, "dve", VA[:, 0, j], CT[:, 0], ar, ALU.mult, [b, b_T], [b])
            tt(S, "dve", tmpB[:], CT[:, 1], ai, ALU.mult, [b, b_T], [b])
            tt(S, "dve", VA[:, 0, j], VA[:, 0, j], tmpB[:], ALU.subtract, [b], [b])
            tt(S, "dve", VA[:, 1, j], CT[:, 0], ai, ALU.mult, [b, b_T], [b])
            tt(S, "dve", tmpB[:], CT[:, 1], ar, ALU.mult, [b, b_T], [b])
            tt(S, "dve", VA[:, 1, j], VA[:, 1, j], tmpB[:], ALU.add, [b], [b])
            ts1(S, "dve", VA[:, 1, j], VA[:, 1, j], -1.0, ALU.mult, [b], [b])
        Vst = [sbt(K, st, "Vst", [128, 2, 8, 128], BF16) for _ in range(4)]
        Nat = [sbt(K, st, "Nat", [128, 2, 8, 128], F32) for _ in range(4)]
        CTp = sbt(K, st, "CTp", [128, 4, 2, 128], F32)
        Wst = [sbt(K, st, "Wst", [128, 2, 8, 128], BF16) for _ in range(2)]
        b_V = [Buf() for _ in range(4)]
        b_N = [Buf() for _ in range(4)]
        b_W = [Buf() for _ in range(2)]
        b_C = Buf()
        for q in range(4):
            mset(S, "pool", Vst[q][:], 0.0, [b_V[q]])
            mset(S, "pool", Nat[q][:], 0.0, [b_N[q]])
        mset(S, "pool", CTp[:], 0.0, [b_C])
        psT = [K.PS[0], K.PS[1]]
        b_psT = [K.PSb[0], K.PSb[1]]
        wi = 0
        for a in range(4):
            for q in range(4):
                pi_ = 4 * a + q
                lo, hi = slice(0, 64), slice(64, 128)
                c0, c1 = slice(32 * q, 32 * q + 16), slice(32 * q + 16, 32 * q + 32)
                for ri in range(2):
                    cp(S, "dve", Vst[q][lo, ri, :, c0], VA[lo, ri, :, pi_, :], [b], [b_V[q]])
                    cp(S, "dve", Vst[q][hi, ri, :, c1], VA[hi, ri, :, pi_, :], [b], [b_V[q]])
                    cp(S, "dve", Nat[q][lo, ri, :, c0], WA[lo, ri, :, pi_, :], [b], [b_N[q]])
                    cp(S, "dve", Nat[q][hi, ri, :, c1], WA[hi, ri, :, pi_, :], [b], [b_N[q]])
                    cp(S, "dve", CTp[lo, q, ri, c0], CT[lo, ri, pi_, :], [b], [b_C])
                    cp(S, "dve", CTp[hi, q, ri, c1], CT[hi, ri, pi_, :], [b], [b_C])
                ts1(S, "dve", CTp[:, q, 1, :], CTp[:, q, 1, :], -1.0, ALU.mult, [b_C], [b_C])
                S.dma("sp", K.Vd[pi_], Vst[q][:], [b_V[q]], [K.b_Vd])
                ws = wi % 2
                wi += 1
                for ri in range(2):
                    for jh in range(2):
                        x = (ri * 2 + jh) % 2
                        for jj in range(4):
                            trp(S, psT[x][:, jj * 128:(jj + 1) * 128], Nat[q][:, ri, jh * 4 + jj, :], K.identf[:], [b_N[q]], [b_psT[x]])
                        cp(S, "act", Wst[ws][:, ri, jh * 4:jh * 4 + 4, :].rearrange("p j c -> p (j c)"), psT[x][:, 0:512], [b_psT[x]], [b_W[ws]])
                S.dma("sp", K.Wd[pi_], Wst[ws][:], [b_W[ws]], [K.b_Wd])
            for dl in range(8):
                x = dl % 2
                pT_ = K.PS[3][:, x * 512:x * 512 + 128]
                pbT = K.PSh[6 + x]
                n = 0
                for q in range(4):
                    for ri in range(2):
                        mm(S, pT_, Nat[q][:, ri, 7 - dl, :], CTp[:, q, ri, :], n == 0, n == 7, [b_N[q], b_C], [pbT])
                        n += 1
                if dl == 0:
                    stt(S, "dve", Tt[:, a, dl, :], K.identf[:], dcol[:, a:a + 1], pT_, ALU.mult, ALU.add, [pbT, b], [b_T])
                else:
                    cp(S, "dve", Tt[:, a, dl, :], pT_, [pbT], [b_T])
        S.flush()


def stage_s5(K, S, l, yT, b_yT, Tt, ARt, AIt, AInt, b_T):
    with contextlib.ExitStack() as st:
        w_in = K.inp["w_in"][l].rearrange("(k p) c -> p k c", p=128)
        wu = sbt(K, st, "wu", [128, 8, 512], BF16)
        bw = Buf()
        for k in range(8):
            load_w(K, S, wu[:, k, :], w_in[:, k, 4608:5120], bw)
        ucT = sbt(K, st, "ucT", [128, 4, T], BF16)
        b_uc = [Buf() for _ in range(4)]
        psM = [K.PS[3][:, 0:512], K.PS[3][:, 512:1024]]
        b_psM = [K.PSh[6], K.PSh[7]]
        mi = 0
        for tb in range(NTB):
            blk = slice(tb * 512, (tb + 1) * 512)
            for a in range(4):
                m = mi % 2
                mi += 1
                for k in range(8):
                    mm(S, psM[m], wu[:, k, a * 128:(a + 1) * 128], K.HT[:, k, blk], k == 0, k == 7, [bw, K.HTb[tb]], [b_psM[m]])
                cp(S, "act", ucT[:, a, blk], psM[m], [b_psM[m]], [b_uc[a]])
        Vsb = sbt(K, st, "Vsb", [128, 4, 2, 8, 128], BF16)
        Wsb = [sbt(K, st, "Wsb", [128, 2, 8, 128], BF16) for _ in range(2)]
        b_Vsb = Buf()
        b_Wsb = [Buf() for _ in range(2)]
        X = [[sbt(K, st, "X", [128, 512], F32) for _ in range(2)] for _ in range(2)]
        b_X = [[Buf() for _ in range(2)] for _ in range(2)]
        Xp = [sbt(K, st, "Xp", [128, 4, 512], BF16) for _ in range(2)]
        ptmp = sbt(K, st, "ptmp", [128, 512], F32)
        b_ptmp = Buf()
        b_Xp = Buf()
        psA = [K.PS[0][:, 0:512], K.PS[0][:, 512:1024], K.PS[1][:, 0:512], K.PS[1][:, 512:1024]]
        b_psA = [K.PSh[0], K.PSh[1], K.PSh[2], K.PSh[3]]
        for i in range(4):
            b_psA[i].w = K.PSb[i // 2].w
            b_psA[i].r = dict(K.PSb[i // 2].r)
        ai_ = 0
        wi = 0
        for a in range(4):
            for q in range(4):
                S.dma("sp", Vsb[:, q], K.Vd[4 * a + q], [K.b_Vd], [b_Vsb])
            mset(S, "pool", Xp[0][:, :, 0:1], 0.0, [b_Xp])
            mset(S, "pool", Xp[1][:, :, 0:1], 0.0, [b_Xp])
            for q in range(4):
                pi_ = 4 * a + q
                ws = wi % 2
                wi += 1
                S.dma("sp", Wsb[ws][:], K.Wd[pi_], [K.b_Wd], [b_Wsb[ws]])
                for ri in range(2):
                    x = ai_ % 4
                    ai_ += 1
                    for jp in range(8):
                        mm(S, psA[x], Wsb[ws][:, ri, jp, :], ucT[:, a, jp::8], jp == 0, jp == 7, [b_Wsb[ws], b_uc[a]], [b_psA[x]])
                    cp(S, "act", X[0][ri][:], psA[x], [b_psA[x]], [b_X[0][ri]])
                cur = 0
                for lv in range(9):
                    sft = 1 << lv
                    pw = PWI[8 * sft]
                    ar = ARt[:, pw, pi_:pi_ + 1]
                    ai = AIt[:, pw, pi_:pi_ + 1]
                    an = AInt[:, pw, pi_:pi_ + 1]
                    o, n_ = X[cur], X[1 - cur]
                    bo, bn = b_X[cur], b_X[1 - cur]
                    hd, tl, bd = slice(0, sft), slice(sft, 512), slice(0, 512 - sft)
                    stt(S, "dve", n_[0][:, tl], o[0][:, bd], ar, o[0][:, tl], ALU.mult, ALU.add, [bo[0], b_T], [bn[0]])
                    stt(S, "dve", n_[0][:, tl], o[1][:, bd], an, n_[0][:, tl], ALU.mult, ALU.add, [bo[1], b_T], [bn[0]])
                    cp(S, "act", n_[0][:, hd], o[0][:, hd], [bo[0]], [bn[0]])
                    stt(S, "dve", n_[1][:, tl], o[0][:, bd], ai, o[1][:, tl], ALU.mult, ALU.add, [bo[0], bo[1], b_T], [bn[1]])
                    stt(S, "dve", n_[1][:, tl], o[1][:, bd], ar, n_[1][:, tl], ALU.mult, ALU.add, [bo[1], b_T], [bn[1]])
                    cp(S, "act", n_[1][:, hd], o[1][:, hd], [bo[1]], [bn[1]])
                    cur = 1 - cur
                for ri in range(2):
                    cp(S, "act", Xp[ri][:, q, 1:512], X[cur][ri][:, 0:511], [b_X[cur][ri]], [b_Xp])
            for j in range(8):
                x = ai_ % 4
                ai_ += 1
                n = 0
                tot = 8 + j + 1
                for q in range(4):
                    for ri in range(2):
                        mm(S, psA[x], Vsb[:, q, ri, j, :], Xp[ri][:, q, :], n == 0, n == tot - 1, [b_Vsb, b_Xp], [b_psA[x]])
                        n += 1
                for jp in range(j + 1):
                    mm(S, psA[x], Tt[:, a, j - jp, :], ucT[:, a, jp::8], n == 0, n == tot - 1, [b_T, b_uc[a]], [b_psA[x]])
                    n += 1
                act(S, yT[:, a, j::8], psA[x], AF.Gelu_apprx_tanh, [b_psA[x]], [b_yT])
        S.flush()


def stage_merge(K, S, l, yT, b_yT):
    with contextlib.ExitStack() as st:
        w_in = K.inp["w_in"][l].rearrange("(k p) c -> p k c", p=128)
        wglu = sbt(K, st, "wglu", [128, 4, 512], BF16)
        wpc = sbt(K, st, "wpc", [128, 4, 1024], BF16)
        wg2 = sbt(K, st, "wg2", [128, 8, 1024], BF16)
        wo = sbt(K, st, "wo", [128, 8, 1024], BF16)
        bw = Buf()
        bw_pc, bw_g2, bw_o = Buf(), Buf(), Buf()
        for k_ in range(4):
            load_w(K, S, wglu[:, k_, :], K.inp["w_glu"][l].rearrange("(k p) c -> p k c", p=128)[:, k_, :], bw)
        for k_ in range(4):
            load_w(K, S, wpc[:, k_, :], K.inp["w_proj_c"][l].rearrange("(k p) c -> p k c", p=128)[:, k_, :], bw_pc)
        for k in range(8):
            load_w(K, S, wg2[:, k, :], w_in[:, k, 7168:8192], bw_g2)
        for k_ in range(8):
            load_w(K, S, wo[:, k_, :], K.inp["w_o"][l].rearrange("(k p) c -> p k c", p=128)[:, k_, :], bw_o)
        sc = ln_scratch(K, st)
        ln_load_gb(K, S, sc, K.inp["ln1_g"][l], K.inp["ln1_b"][l])
        zs = sbt(K, st, "zs", [128, 512], F32)
        b_zs = Buf()
        ygT = sbt(K, st, "ygT", [128, 4, 512], BF16)
        b_yg = Buf()
        gsb = sbt(K, st, "gsbC", [128, 512], F32)
        b_gsb = Buf()
        mprev = sbt(K, st, "mprevC", [128, 8, 512], BF16)
        b_mprev = Buf()
        mT = sbt(K, st, "mT", [128, 8, 512], BF16)
        b_mT = Buf()
        hprev = [sbt(K, st, "hprev", [128, 1024], F32) for _ in range(2)]
        b_hp = [Buf() for _ in range(2)]
        psM = [K.PS[3][:, 0:512], K.PS[3][:, 512:1024]]
        b_psM = [K.PSh[6], K.PSh[7]]
        mi = 0
        for tb in range(NTB):
            blk = slice(tb * 512, (tb + 1) * 512)
            hb = K.HTb[tb]
            for o2 in range(4):
                S.dma("sp", mprev[:, 2 * o2:2 * o2 + 2, :], K.merged[:, 2 * o2:2 * o2 + 2, blk], [K.b_merged[tb]], [b_mprev])
            for ot in range(4):
                m = mi % 2
                mi += 1
                for k in range(4):
                    mm(S, psM[m], wglu[:, k, ot * 128:(ot + 1) * 128], yT[:, k, blk], k == 0, k == 3, [bw, b_yT], [b_psM[m]])
                act(S, zs[:], psM[m], AF.Sigmoid, [b_psM[m]], [b_zs])
                tt(S, "dve", ygT[:, ot, :], yT[:, ot, blk], zs[:], ALU.mult, [b_yT, b_zs], [b_yg])
            for ot in range(8):
                m = mi % 2
                mi += 1
                for k in range(8):
                    mm(S, psM[m], wg2[:, k, ot * 128:(ot + 1) * 128], K.HT[:, k, blk], k == 0, k == 7, [bw_g2, hb], [b_psM[m]])
                act(S, gsb[:], psM[m], AF.Sigmoid, [b_psM[m]], [b_gsb])
                m = mi % 2
                mi += 1
                for k in range(4):
                    mm(S, psM[m], wpc[:, k, ot * 128:(ot + 1) * 128], ygT[:, k, :], k == 0, k == 3, [bw_pc, b_yg], [b_psM[m]])
                tt(S, "dve", gsb[:], psM[m], gsb[:], ALU.mult, [b_psM[m], b_gsb], [b_gsb])
                tt(S, "dve", mT[:, ot, :], gsb[:], mprev[:, ot, :], ALU.add, [b_gsb, b_mprev], [b_mT])
            for t4 in range(4):
                t = tb * 4 + t4
                i = t % 2
                S.dma("sp", hprev[i][:], K.hres[t * 128:(t + 1) * 128, :], [K.b_hres[tb]], [b_hp[i]])
                pso, bpso = K.PS[i], K.PSb[i]
                for half in range(2):
                    for f in range(8):
                        mm(S, pso[:, half * 512:(half + 1) * 512], mT[:, f, t4 * 128:(t4 + 1) * 128], wo[:, f, half * 512:(half + 1) * 512],
                           f == 0, f == 7, [bw_o, b_mT], [bpso])
                for half in range(2):
                    cs = slice(half * 512, (half + 1) * 512)
                    stt(S, "dve", hprev[i][:, cs], hprev[i][:, cs], ALPHA, pso[:, cs], ALU.mult, ALU.add, [b_hp[i], bpso], [b_hp[i]])
                ln_apply(K, S, sc, hprev[i][:], b_hp[i], t, K.hres, K.b_hres[tb], K.PS[2][:, :], K.PSb[2])
        S.flush()


def stage_ffn(K, S, l, last):
    HF = 1408
    for p in range(2):
        with contextlib.ExitStack() as st:
            w_up = K.inp["w_up"][l].rearrange("(k p) c -> p k c", p=128)
            wup = sbt(K, st, "wup", [128, 8, 2 * HF], BF16)
            wdn = sbt(K, st, "wdn", [128, 11, 1024], BF16)
            bw = Buf()
            bw_d = Buf()
            for k in range(8):
                load_w(K, S, wup[:, k, 0:HF], w_up[:, k, p * HF:(p + 1) * HF], bw)
                load_w(K, S, wup[:, k, HF:2 * HF], w_up[:, k, 2816 + p * HF:2816 + (p + 1) * HF], bw)
            for k_ in range(11):
                load_w(K, S, wdn[:, k_, :], K.inp["w_down"][l][p * HF:(p + 1) * HF, :].rearrange("(k p) c -> p k c", p=128)[:, k_, :], bw_d)
            craw = sbt(K, st, "cwraw", [88, 128], F32)
            cw = sbt(K, st, "cw", [128, 88], F32)
            b_cw = Buf()
            cwl = K.inp["conv_w"][l]
            cbl = K.inp["conv_b"][l]
            for t3 in range(3):
                S.dma("sp", craw[t3 * 22:t3 * 22 + 11, :], cwl[t3, p * HF:(p + 1) * HF].rearrange("(n q) -> n q", q=128), (), [b_cw])
                S.dma("sp", craw[t3 * 22 + 11:t3 * 22 + 22, :], cwl[t3, 2816 + p * HF:2816 + (p + 1) * HF].rearrange("(n q) -> n q", q=128), (), [b_cw])
            S.dma("sp", craw[66:77, :], cbl[p * HF:(p + 1) * HF].rearrange("(n q) -> n q", q=128), (), [b_cw])
            S.dma("sp", craw[77:88, :], cbl[2816 + p * HF:2816 + (p + 1) * HF].rearrange("(n q) -> n q", q=128), (), [b_cw])
            trp(S, K.PS[2][:, 0:88], craw[:], K.identf[0:88, 0:88], [b_cw], [K.PSb[2]])
            cp(S, "dve", cw[:], K.PS[2][:, 0:88], [K.PSb[2]], [b_cw])
            if RET_CUT == 101:
                S.flush()
                return
            sc = ln_scratch(K, st)
            if p == 1:
                ln_load_gb(K, S, sc, K.inp["ln2_g"][l], K.inp["ln2_b"][l])
            diagw = sbt(K, st, "diagw", [128, 22, 3, 128], BF16)
            b_dg = Buf()
            for c in range(22):
                for t3 in range(3):
                    ts1(S, "dve", diagw[:, c, t3, :], K.identf[:], cw[:, t3 * 22 + c:t3 * 22 + c + 1], ALU.mult, [b_cw], [b_dg])
            xbuf = [sbt(K, st, "xbuf", [128, 514], BF16) for _ in range(2)]
            cva = [sbt(K, st, "cva", [128, 512], F32) for _ in range(2)]
            b_xb = [Buf() for _ in range(2)]
            b_cva = [Buf() for _ in range(2)]
            halo = sbt(K, st, "halo", [128, 22, 2], BF16)
            b_halo = Buf()
            mset(S, "pool", halo[:], 0.0, [b_halo])
            actT = sbt(K, st, "actT", [128, 11, 512], BF16)
            b_actT = Buf()
            hprev = [sbt(K, st, "hprevF", [128, 1024], F32) for _ in range(2)]
            b_hp = [Buf() for _ in range(2)]
            psM = [K.PS[3][:, 0:512], K.PS[3][:, 512:1024]]
            b_psM = [K.PSh[6], K.PSh[7]]
            psC = [K.PS[2][:, 0:512], K.PS[2][:, 512:1024]]
            b_psC = [K.PSh[4], K.PSh[5]]
            for i_ in range(2):
                b_psC[i_].w = K.PSb[2].w
                b_psC[i_].r = dict(K.PSb[2].r)
            mi = 0
            cstate = [0]

            def conv_stage(n, wh, c, pa):
                xb, bx = xbuf[wh], b_xb[wh]
                m2 = cstate[0] % 2
                cstate[0] += 1
                for t3 in range(3):
                    mm(S, psC[m2], diagw[:, c, t3, :], xb[:, t3:t3 + 512], t3 == 0, t3 == 2, [b_dg, bx], [b_psC[m2]])
                if wh == 0:
                    S.op("act", lambda e: e.activation(out=cva[pa][:], in_=psC[m2], func=AF.Gelu_apprx_tanh,
                                                       bias=cw[:, 66 + c:67 + c], scale=1.0),
                         [b_psC[m2], b_cw], [b_cva[pa]])
                else:
                    stt(S, "dve", actT[:, n, :], psC[m2], cw[:, 66 + c:67 + c], cva[pa][:], ALU.add, ALU.mult,
                        [b_psC[m2], b_cw, b_cva[pa]], [b_actT])
                cp(S, "dve", halo[:, c, :], xb[:, 512:514], [bx], [b_halo])

            for tb in range(NTB):
                blk = slice(tb * 512, (tb + 1) * 512)
                hb = K.HTb[tb]
                pending = None
                for n in range(11):
                    pa = n % 2
                    for wh in range(2):
                        c = n + 11 * wh
                        m = mi % 2
                        mi += 1
                        xb, bx = xbuf[wh], b_xb[wh]
                        for k in range(8):
                            mm(S, psM[m], wup[:, k, c * 128:(c + 1) * 128], K.HT[:, k, blk], k == 0, k == 7, [bw, hb], [b_psM[m]])
                        if pending is not None and pending[1] == wh:
                            conv_stage(*pending)
                            pending = None
                        cp(S, "dve", xb[:, 0:2], halo[:, c, :], [b_halo], [bx])
                        cp(S, "act", xb[:, 2:514], psM[m], [b_psM[m]], [bx])
                        if pending is not None:
                            conv_stage(*pending)
                        pending = (n, wh, c, pa)
                conv_stage(*pending)
                if RET_CUT == 102:
                    S.flush()
                    return
                for t4 in range(4):
                    t = tb * 4 + t4
                    i = t % 2
                    rows = slice(t * 128, (t + 1) * 128)
                    src = K.hres if p == 0 else K.fpart
                    srcb = K.b_hres[tb] if p == 0 else K.b_fpart[tb]
                    S.dma("sp", hprev[i][:], src[rows, :], [srcb], [b_hp[i]])
                    pso, bpso = K.PS[i], K.PSb[i]
                    for half in range(2):
                        for f in range(11):
                            mm(S, pso[:, half * 512:(half + 1) * 512], actT[:, f, t4 * 128:(t4 + 1) * 128], wdn[:, f, half * 512:(half + 1) * 512],
                               f == 0, f == 10, [bw_d, b_actT], [bpso])
                    for half in range(2):
                        cs = slice(half * 512, (half + 1) * 512)
                        stt(S, "dve", hprev[i][:, cs], hprev[i][:, cs], ALPHA if p == 0 else 1.0, pso[:, cs], ALU.mult, ALU.add,
                            [b_hp[i], bpso], [b_hp[i]])
                    if RET_CUT == 103:
                        S.flush()
                        return
                    if p == 0:
                        S.dma("sp", K.fpart[rows, :], hprev[i][:], [b_hp[i]], [K.b_fpart[tb]])
                    elif last:
                        ln_apply(K, S, sc, hprev[i][:], b_hp[i], t, None, None, pso[:, :], bpso, final_out=K.y, final_buf=K.b_y)
                    else:
                        ln_apply(K, S, sc, hprev[i][:], b_hp[i], t, K.hres, K.b_hres[tb], pso[:, :], bpso)
            S.flush()

IN_SPECS = [
    ("x", [T, D]), ("ln_in_g", [D]), ("ln_in_b", [D]), ("w_in", [NL, D, 8192]), ("rel_bias", [NL, 8, 257]),
    ("w_proj_a", [NL, 512, D]), ("w_proj_b", [NL, 1024, D]), ("w_proj_c", [NL, 512, D]),
    ("lam_re", [NL, 32, 64]), ("lam_im", [NL, 32, 64]), ("log_step", [NL, 32]),
    ("b_re", [NL, 32, 64, 16]), ("b_im", [NL, 32, 64, 16]), ("c_re", [NL, 32, 16, 64]), ("c_im", [NL, 32, 16, 64]),
    ("d_skip", [NL, 512]), ("w_glu", [NL, 512, 512]), ("w_o", [NL, D, D]), ("ln1_g", [NL, D]), ("ln1_b", [NL, D]),
    ("w_up", [NL, D, 5632]), ("conv_w", [NL, 3, 5632]), ("conv_b", [NL, 5632]), ("w_down", [NL, 2816, D]),
    ("ln2_g", [NL, D]), ("ln2_b", [NL, D]),
]


def build_program():
    nc = bass.Bass("TRN2", target_bir_lowering=False)
    K = Ctx()
    K.nc = nc
    K.uid = 0
    K.inp = {n: nc.dram_tensor(n, list(s), F32, kind="ExternalInput").ap() for n, s in IN_SPECS}
    K.y = nc.dram_tensor("y", [T, D], F32, kind="ExternalOutput").ap()
    dbg = "ExternalOutput" if DEBUG else "Internal"
    K.hres = nc.dram_tensor("hres", [T, D], F32, kind=dbg).ap()
    K.merged = nc.dram_tensor("merged", [128, 8, T], BF16, kind=dbg).ap()
    K.Fd = nc.dram_tensor("Fd", [8, 768], F32, kind="Internal").ap()
    K.Wd = nc.dram_tensor("Wd", [16, 128, 2, 8, 128], BF16, kind="Internal").ap()
    K.Vd = nc.dram_tensor("Vd", [16, 128, 2, 8, 128], BF16, kind="Internal").ap()
    K.b_Wd = Buf()
    K.b_Vd = Buf()
    if DEBUG:
        K.dbg2 = nc.dram_tensor("dbg2", [128, 4, T], BF16, kind="ExternalOutput").ap()
        K.dbg1 = nc.dram_tensor("dbg1", [128, 8 * 5 * 128], F32, kind="ExternalOutput").ap()
    K.fpart = nc.dram_tensor("fpart", [T, D], F32, kind="Internal").ap()
    K.b_fpart = [Buf() for _ in range(NTB)]
    K.b_hres = [Buf() for _ in range(NTB)]
    K.b_merged = [Buf() for _ in range(NTB)]
    K.b_y = Buf()
    with contextlib.ExitStack() as gst:
        S = Sched(nc, gst)
        K.HT = gst.enter_context(nc.sbuf_tensor("HT", [128, 8, T], BF16))
        K.HTb = [Buf() for _ in range(NTB)]
        K.identf = gst.enter_context(nc.sbuf_tensor("identf", [128, 128], F32))
        K.Jf = gst.enter_context(nc.sbuf_tensor("Jf", [128, 128], F32))
        io = gst.enter_context(nc.sbuf_tensor("iota_i", [128, 128], I32))
        K.PS = [gst.enter_context(nc.psum_tensor("PS%d" % i, [128, 1024], F32)) for i in range(4)]
        K.PSb = [Buf(excl=True) for _ in range(4)]
        K.PSh = [Buf(excl=True) for _ in range(8)]
        bc = Buf()
        S.op("pool", lambda e: e.iota(io[:], pattern=[[1, 128]], base=0, channel_multiplier=-1), (), [bc])
        ts1(S, "dve", K.identf[:], io[:], 0.0, ALU.is_equal, [bc], [bc])
        S.op("pool", lambda e: e.iota(io[:], pattern=[[1, 128]], base=0, channel_multiplier=1), [bc], [bc])
        ts1(S, "dve", K.Jf[:], io[:], 127.0, ALU.is_equal, [bc], [bc])
        S.flush()
        stage_entry(K, S)
        if STOP_AFTER == "entry":
            return nc
        for l in LAYERS:
            if STOP_AFTER == "ffnonly":
                stage_ffn(K, S, l, False)
                return nc
            if STOP_AFTER not in ("s5only", "rettab"):
                stage_attn(K, S, l)
            if STOP_AFTER in ("attn", "bias"):
                return nc
            if STOP_AFTER != "s5only":
                stage_ret(K, S, l)
            if STOP_AFTER in ("ret", "rettab"):
                return nc
            with contextlib.ExitStack() as sty:
                Tt = sbt(K, sty, "Tt", [128, 4, 8, 128], BF16)
                ARt = sbt(K, sty, "ARt", [128, NPW, 16], F32)
                AIt = sbt(K, sty, "AIt", [128, NPW, 16], F32)
                AInt = sbt(K, sty, "AInt", [128, NPW, 16], F32)
                b_T = Buf()
                s5_setup(K, S, l, Tt, ARt, AIt, AInt, b_T)
                yT = sbt(K, sty, "yT", [128, 4, T], BF16)
                b_yT = Buf()
                stage_s5(K, S, l, yT, b_yT, Tt, ARt, AIt, AInt, b_T)
                if STOP_AFTER in ("s5", "s5only"):
                    S.dma("sp", K.dbg2[:, :, :], yT[:], [b_yT], [Buf()])
                    S.flush()
                    return nc
                stage_merge(K, S, l, yT, b_yT)
            if STOP_AFTER == "merge":
                return nc
            stage_ffn(K, S, l, l == NL - 1)
            if STOP_AFTER == "ffn":
                return nc
    return nc


_PROG = None


def kernel(**inputs):
    global _PROG
    if _PROG is None:
        _PROG = build_program()
    nc = _PROG
    x = np.ascontiguousarray(np.asarray(inputs["x"], dtype=np.float32))
    shared = {n: np.ascontiguousarray(np.asarray(inputs[n], dtype=np.float32)) for n, _ in IN_SPECS if n != "x"}
    in_maps = []
    for c in range(8):
        m = dict(shared)
        m["x"] = x[c]
        in_maps.append(m)
    res = run_bass_kernel_spmd(nc, in_maps, core_ids=list(range(8)))
    return np.stack([r["y"] for r in res.results], axis=0).astype(np.float32)
```

```python
import contextlib
import math
import numpy as np
import concourse.bass as bass
import concourse.mybir as mybir
from concourse.bass_utils import run_bass_kernel_spmd

F32 = mybir.dt.float32
BF16 = mybir.dt.bfloat16
I32 = mybir.dt.int32
AF = mybir.ActivationFunctionType
ALU = mybir.AluOpType
AX = mybir.AxisListType

SAME_ENGINE_SYNC = True
N_DMA_SEMS = 12
N_SW_SEMS = 4
DEBUG = False
STOP_AFTER = None
LIMIT_TB = None
RET_CUT = None
LAYERS = (0, 1)
MM_LAZY_INC = True

T = 4096
D = 1024
NTT = 32
NTB = 8
NL = 2
ALPHA = (2.0 * NL) ** 0.25
LN_EPS = 1e-5
TWO_PI = 2.0 * math.pi


class Buf:
    __slots__ = ("name", "w", "r", "excl")

    def __init__(self, name="", excl=False):
        self.name = name
        self.w = None
        self.r = {}
        self.excl = excl


class Sched:
    ENG = ("pe", "act", "dve", "pool", "sp")

    def __init__(self, nc, stack):
        self.nc = nc
        self.streams = {e: [] for e in self.ENG}
        self.count = {e: 0 for e in self.ENG}
        self.seen = {e: {} for e in self.ENG}
        self.csem = {e: stack.enter_context(nc.semaphore("c_" + e)) for e in self.ENG}
        self.dsem = [stack.enter_context(nc.semaphore("d%d" % i)) for i in range(N_DMA_SEMS)]
        self.duse = [0] * N_DMA_SEMS
        self.dnext = 0
        self.dnext_sw = 0
        self.ninst = 0

    def _sem_of(self, key):
        return self.csem[key[1]] if key[0] == "e" else self.dsem[key[1]]

    def _add_wait(self, eng, k, v):
        seen = self.seen[eng]
        if seen.get(k, 0) >= v:
            return
        seen[k] = v
        sem = self._sem_of(k)
        self.streams[eng].append(lambda e: e.wait_ge(sem, v))
        self.ninst += 1

    def _waits(self, eng, reads, writes):
        deps = {}
        for b in reads:
            if b.w is not None:
                k, v = b.w
                if deps.get(k, 0) < v:
                    deps[k] = v
            if b.excl:
                for k, v in b.r.items():
                    if k != ("e", eng) and deps.get(k, 0) < v:
                        deps[k] = v
        for b in writes:
            if b.w is not None:
                k, v = b.w
                if deps.get(k, 0) < v:
                    deps[k] = v
            for k, v in b.r.items():
                if deps.get(k, 0) < v:
                    deps[k] = v
        for k, v in deps.items():
            if k == ("e", eng) and (eng == "pe" or not SAME_ENGINE_SYNC):
                continue
            self._add_wait(eng, k, v)

    def _commit(self, tok, reads, writes):
        k, v = tok
        for b in reads:
            if b.r.get(k, 0) < v:
                b.r[k] = v
        for b in writes:
            b.w = tok
            b.r = {}

    def op(self, eng, fn, reads=(), writes=(), inc=True):
        self._waits(eng, reads, writes)
        if not inc:
            self.streams[eng].append(lambda e: fn(e))
            self.ninst += 1
            tok = (("e", eng), self.count[eng] + 1)
            self._commit(tok, reads, writes)
            return tok
        self.count[eng] += 1
        n = self.count[eng]
        sem = self.csem[eng]
        self.streams[eng].append(lambda e: fn(e).then_inc(sem, 1))
        self.ninst += 1
        tok = (("e", eng), n)
        self._commit(tok, reads, writes)
        return tok

    def dma(self, eng, out, in_, reads=(), writes=(), **kw):
        if eng == "pool":
            s = N_DMA_SEMS - N_SW_SEMS + self.dnext_sw
            self.dnext_sw = (self.dnext_sw + 1) % N_SW_SEMS
        else:
            s = self.dnext
            self.dnext = (self.dnext + 1) % (N_DMA_SEMS - N_SW_SEMS)
        k = ("d", s)
        prev = 16 * self.duse[s]
        if prev > 0:
            self._add_wait(eng, k, prev)
        self._waits(eng, reads, writes)
        self.duse[s] += 1
        sem = self.dsem[s]
        self.streams[eng].append(lambda e: e.dma_start(out=out, in_=in_, **kw).then_inc(sem, 16))
        self.ninst += 1
        tok = (k, 16 * self.duse[s])
        self._commit(tok, reads, writes)
        return tok

    def flush(self):
        toks = [(("d", s), 16 * self.duse[s]) for s in range(N_DMA_SEMS) if self.duse[s]]
        toks += [(("e", e), self.count[e]) for e in self.ENG if self.count[e]]
        for e in self.ENG:
            for k, v in toks:
                if k == ("e", e):
                    continue
                self._add_wait(e, k, v)
        nc = self.nc
        streams = self.streams
        with nc.Block() as block:
            @block.tensor
            def _(eng):
                for f in streams["pe"]:
                    f(eng)

            @block.scalar
            def _(eng):
                for f in streams["act"]:
                    f(eng)

            @block.vector
            def _(eng):
                for f in streams["dve"]:
                    f(eng)

            @block.gpsimd
            def _(eng):
                for f in streams["pool"]:
                    f(eng)

            @block.sync
            def _(eng):
                for f in streams["sp"]:
                    f(eng)
        self.streams = {e: [] for e in self.ENG}


def mm(S, out, lhsT, rhs, start, stop, rd, wr):
    S.op("pe", lambda e: e.matmul(out, lhsT, rhs, start=start, stop=stop), rd, wr, inc=(stop or not MM_LAZY_INC))


def trp(S, out, in_, ident, rd, wr):
    S.op("pe", lambda e: e.transpose(out, in_, ident), rd, wr)


def act(S, out, in_, func, rd, wr, bias=0.0, scale=1.0):
    S.op("act", lambda e: e.activation(out=out, in_=in_, func=func, bias=bias, scale=scale), rd, wr)


def tt(S, eng, out, in0, in1, op, rd, wr):
    S.op(eng, lambda e: e.tensor_tensor(out=out, in0=in0, in1=in1, op=op), rd, wr)


def ts(S, eng, out, in0, s1, s2, op0, op1, rd, wr):
    S.op(eng, lambda e: e.tensor_scalar(out=out, in0=in0, scalar1=s1, scalar2=s2, op0=op0, op1=op1), rd, wr)


def ts1(S, eng, out, in0, s1, op0, rd, wr):
    S.op(eng, lambda e: e.tensor_scalar(out=out, in0=in0, scalar1=s1, scalar2=None, op0=op0), rd, wr)


def stt(S, eng, out, in0, scalar, in1, op0, op1, rd, wr):
    S.op(eng, lambda e: e.scalar_tensor_tensor(out=out, in0=in0, scalar=scalar, in1=in1, op0=op0, op1=op1), rd, wr)


def cp(S, eng, out, in_, rd, wr):
    if eng == "act":
        S.op("act", lambda e: e.copy(out=out, in_=in_), rd, wr)
    else:
        S.op(eng, lambda e: e.tensor_copy(out=out, in_=in_), rd, wr)


def mset(S, eng, ap, val, wr):
    S.op(eng, lambda e: e.memset(ap, val), (), wr)


def recip(S, out, in_, rd, wr):
    S.op("dve", lambda e: e.reciprocal(out=out, in_=in_), rd, wr)


class Ctx:
    pass


def sbt(K, st, name, shape, dtype):
    K.uid += 1
    return st.enter_context(K.nc.sbuf_tensor("%s_%d" % (name, K.uid), list(shape), dtype))


def sincos(K, S, st, ang, shape, b, want_cos):
    yy = sbt(K, st, "sc_y", shape, F32)
    ki = sbt(K, st, "sc_k", shape, I32)
    kf = sbt(K, st, "sc_kf", shape, F32)
    out = sbt(K, st, "sc_o", shape, F32)
    off = (math.pi / 2.0 if want_cos else 0.0)
    ts1(S, "dve", yy[:], ang, off, ALU.add, [b], [b])
    ts1(S, "dve", ki[:], yy[:], 1.0 / TWO_PI, ALU.mult, [b], [b])
    cp(S, "dve", kf[:], ki[:], [b], [b])
    stt(S, "dve", yy[:], kf[:], -TWO_PI, yy[:], ALU.mult, ALU.add, [b], [b])
    ts(S, "dve", kf[:], yy[:], math.pi, -TWO_PI, ALU.is_gt, ALU.mult, [b], [b])
    tt(S, "dve", yy[:], yy[:], kf[:], ALU.add, [b], [b])
    ts(S, "dve", kf[:], yy[:], -math.pi, TWO_PI, ALU.is_lt, ALU.mult, [b], [b])
    tt(S, "dve", yy[:], yy[:], kf[:], ALU.add, [b], [b])
    ts(S, "dve", yy[:], yy[:], math.pi, -math.pi, ALU.min, ALU.max, [b], [b])
    act(S, out[:], yy[:], AF.Sin, [b], [b])
    return out


def ln_scratch(K, st):
    sc = Ctx()
    sc.i = 0
    sc.st = [sbt(K, st, "ln_st", [128, 12], F32) for _ in range(2)]
    sc.mv = [sbt(K, st, "ln_mv", [128, 2], F32) for _ in range(2)]
    sc.sd = [sbt(K, st, "ln_sd", [128, 1], F32) for _ in range(2)]
    sc.hn = [sbt(K, st, "ln_hn", [128, 1024], F32) for _ in range(2)]
    sc.b = [Buf() for _ in range(2)]
    sc.bh = [Buf() for _ in range(2)]
    sc.gt = sbt(K, st, "ln_g", [128, 1024], F32)
    sc.bt = sbt(K, st, "ln_b", [128, 1024], F32)
    sc.bgb = Buf()
    return sc


def ln_load_gb(K, S, sc, g_ap, b_ap):
    S.dma("sp", sc.gt[:], g_ap.partition_broadcast(128), (), [sc.bgb])
    S.dma("sp", sc.bt[:], b_ap.partition_broadcast(128), (), [sc.bgb])


def ln_apply(K, S, sc, src, src_buf, tt_i, dst_dram, dst_buf, ps, ps_buf, final_out=None, final_buf=None):
    i = sc.i
    sc.i = 1 - i
    stt_, mv, sd, hn, b, bh = sc.st[i], sc.mv[i], sc.sd[i], sc.hn[i], sc.b[i], sc.bh[i]
    S.op("dve", lambda e: e.bn_stats(out=stt_[:, 0:6], in_=src[:, 0:512]), [src_buf], [b])
    S.op("dve", lambda e: e.bn_stats(out=stt_[:, 6:12], in_=src[:, 512:1024]), [src_buf], [b])
    S.op("dve", lambda e: e.bn_aggr(out=mv[:, 0:2], in_=stt_[:, 0:12]), [b], [b])
    ts1(S, "dve", sd[:], mv[:, 1:2], LN_EPS, ALU.add, [b], [b])
    act(S, sd[:], sd[:], AF.Sqrt, [b], [b])
    recip(S, sd[:], sd[:], [b], [b])
    stt(S, "dve", mv[:, 1:2], mv[:, 0:1], -1.0, sd[:, 0:1], ALU.mult, ALU.mult, [b], [b])
    S.op("act", lambda e: e.activation(out=hn[:], in_=src, func=AF.Identity, bias=mv[:, 1:2], scale=sd[:, 0:1]), [src_buf, b], [bh])
    tt(S, "dve", hn[:], hn[:], sc.gt[:], ALU.mult, [bh, sc.bgb], [bh])
    tt(S, "dve", hn[:], hn[:], sc.bt[:], ALU.add, [bh, sc.bgb], [bh])
    rows = slice(tt_i * 128, (tt_i + 1) * 128)
    if dst_dram is not None:
        S.dma("sp", dst_dram[rows, :], hn[:], [bh], [dst_buf])
    if final_out is not None:
        S.dma("sp", final_out[rows, :], hn[:], [bh], [final_buf])
        return
    for k in range(8):
        trp(S, ps[:, k * 128:(k + 1) * 128], hn[:, k * 128:(k + 1) * 128], K.identf[:], [bh], [ps_buf])
    cp(S, "act", K.HT[:, :, rows], ps.rearrange("p (k t) -> p k t", k=8), [ps_buf], [K.HTb[tt_i // 4]])


def stage_entry(K, S):
    with contextlib.ExitStack() as st:
        sc = ln_scratch(K, st)
        ln_load_gb(K, S, sc, K.inp["ln_in_g"], K.inp["ln_in_b"])
        xin = [sbt(K, st, "xin", [128, 1024], F32) for _ in range(2)]
        bx = [Buf() for _ in range(2)]
        for t in range(NTT):
            i = t % 2
            S.dma("sp", xin[i][:], K.inp["x"][t * 128:(t + 1) * 128, :], (), [bx[i]])
            ln_apply(K, S, sc, xin[i][:], bx[i], t, K.hres, K.b_hres[t // 4], K.PS[t % 2][:, :], K.PSb[t % 2])
        S.flush()


def build_bias(K, S, st, l, biasT, b_bias):
    Fd = K.Fd
    bF = Buf()
    rb = K.inp["rel_bias"][l]
    with contextlib.ExitStack() as st2:
        fsb = sbt(K, st2, "fsb", [8, 768], F32)
        bfs = Buf()
        S.dma("sp", fsb[:, 0:256], rb[:, 1:257], (), [bfs])
        cp(S, "dve", fsb[:, 256:768], fsb[:, 255:256].broadcast_to([8, 512]), [bfs], [bfs])
        S.dma("sp", Fd[:, :], fsb[:], [bfs], [bF])
        Tk = sbt(K, st2, "Tk", [128, 8, 5, 128], F32)
        bT = Buf()
        for h in range(8):
            for jr in range(5):
                src = bass.AP(tensor=Fd.tensor, offset=h * 768 + 512 - 128 * jr, ap=[[1, 128], [1, 128]])
                S.dma("sp", Tk[:, h, jr, :], src, [bF], [bT])
        for h in range(8):
            p0 = K.PS[3][:, 0:512]
            p1 = K.PS[3][:, 512:640]
            mm(S, p0, K.Jf[:], Tk[:, h, 0:4, :].rearrange("p j q -> p (j q)"), True, True, [bT], [K.PSb[3]])
            mm(S, p1, K.Jf[:], Tk[:, h, 4, :], True, True, [bT], [K.PSb[3]])
            cp(S, "act", biasT[:, h, :, :].rearrange("p j q -> p (j q)"), K.PS[3][:, 0:640], [K.PSb[3]], [b_bias])
            mset(S, "pool", biasT[64:128, h, 4, 0:64], -30000.0, [b_bias])
            mset(S, "pool", biasT[0:64, h, 0, 64:128], -30000.0, [b_bias])
        S.flush()


def load_w(K, S, dst, src, buf):
    S.dma("pool", dst, src, (), [buf])


def stage_attn(K, S, l):
    with contextlib.ExitStack() as st:
        w_in = K.inp["w_in"][l].rearrange("(k p) c -> p k c", p=128)
        wqkv = sbt(K, st, "wqkv", [128, 8, 1536], BF16)
        wg0 = sbt(K, st, "wg0", [128, 8, 1024], BF16)
        wpa = sbt(K, st, "wpa", [128, 4, 1024], BF16)
        bw = Buf()
        for k in range(8):
            load_w(K, S, wqkv[:, k, :], w_in[:, k, 0:1536], bw)
        bw_g = Buf()
        bw_p = Buf()
        for k in range(8):
            load_w(K, S, wg0[:, k, :], w_in[:, k, 5120:6144], bw_g)
        for k_ in range(4):
            load_w(K, S, wpa[:, k_, :], K.inp["w_proj_a"][l].rearrange("(k p) c -> p k c", p=128)[:, k_, :], bw_p)
        biasT = sbt(K, st, "biasT", [128, 8, 5, 128], F32)
        b_bias = Buf()
        build_bias(K, S, st, l, biasT, b_bias)
        if STOP_AFTER == "bias":
            S.dma("sp", K.dbg1[:, :], biasT[:].rearrange("p h j q -> p (h j q)"), [b_bias], [Buf()])
            S.flush()
            return
        qT = [sbt(K, st, "qT", [128, 4, 512], BF16) for _ in range(2)]
        b_qT = [Buf() for _ in range(2)]
        kring = sbt(K, st, "kring", [128, 4, 1024], BF16)
        b_kr = [Buf() for _ in range(2)]
        vring = sbt(K, st, "vring", [128, 8, 8, 65], BF16)
        b_vr = [Buf() for _ in range(8)]
        mset(S, "pool", vring[:], 1.0, b_vr)
        sbf = [sbt(K, st, "sbf", [128, 640], F32) for _ in range(2)]
        pT = [sbt(K, st, "pT", [128, 640], BF16) for _ in range(2)]
        b_sbf = [Buf() for _ in range(2)]
        b_pT = [Buf() for _ in range(2)]
        att = [sbt(K, st, "att", [128, 512], F32) for _ in range(2)]
        rs = [sbt(K, st, "rs", [128, 8], F32) for _ in range(2)]
        b_att = [Buf() for _ in range(2)]
        attT = [sbt(K, st, "attT", [128, 4, 512], BF16) for _ in range(2)]
        b_attT = [Buf() for _ in range(2)]
        gsb = [sbt(K, st, "gsb", [128, 512], F32) for _ in range(2)]
        b_gsb = [Buf() for _ in range(2)]
        mrg1 = sbt(K, st, "mrg", [128, 8, 512], BF16)
        mrg = [mrg1, mrg1]
        b_mrg1 = Buf()
        b_mrg = [b_mrg1, b_mrg1]
        psS = [K.PS[0], K.PS[1]]
        b_psS = [K.PSb[0], K.PSb[1]]
        psO = [K.PS[2][:, 0:512], K.PS[2][:, 512:1024]]
        b_psO = [K.PSh[4], K.PSh[5]]
        psM = [K.PS[3][:, 0:512], K.PS[3][:, 512:1024]]
        b_psM = [K.PSh[6], K.PSh[7]]
        K.PSh[6].w = K.PSb[3].w
        K.PSh[7].w = K.PSb[3].w
        K.PSh[6].r = dict(K.PSb[3].r)
        K.PSh[7].r = dict(K.PSb[3].r)
        mi = 0
        for tb in range(NTB if LIMIT_TB is None else LIMIT_TB):
            blk = slice(tb * 512, (tb + 1) * 512)
            qs = tb % 2
            hb = K.HTb[tb]
            for ot in range(8):
                m = mi % 2
                mi += 1
                for k in range(8):
                    mm(S, psM[m], wqkv[:, k, ot * 128:(ot + 1) * 128], K.HT[:, k, blk], k == 0, k == 7, [bw, hb], [b_psM[m]])
                if ot < 4:
                    act(S, qT[qs][:, ot, :], psM[m], AF.Copy, [b_psM[m]], [b_qT[qs]], scale=0.125)
                else:
                    cp(S, "dve", kring[:, ot - 4, qs * 512:(qs + 1) * 512], psM[m], [b_psM[m]], [b_kr[qs]])
            for t4 in range(4):
                t = tb * 4 + t4
                slot = t % 8
                m = mi % 2
                mi += 1
                for k in range(8):
                    mm(S, psM[m], K.HT[:, k, t * 128:(t + 1) * 128], wqkv[:, k, 1024:1536], k == 0, k == 7, [bw, hb], [b_psM[m]])
                cp(S, "act", vring[:, slot, :, 0:64], psM[m].rearrange("p (h d) -> p h d", h=8), [b_psM[m]], [b_vr[slot]])
            for s4 in range(4):
                sc_ = tb * 4 + s4
                nk = min(sc_, 4) + 1
                jr0 = 5 - nk
                ai = sc_ % 2
                def emit_scores(h):
                    hp, tq = h % 2, h // 2
                    x = h % 2
                    prt = slice(hp * 64, (hp + 1) * 64)
                    for j in range(nk):
                        kt = sc_ - (nk - 1) + j
                        slot = kt % 8
                        mm(S, psS[x][:, j * 128:(j + 1) * 128], kring[prt, tq, slot * 128:(slot + 1) * 128],
                           qT[qs][prt, tq, s4 * 128:(s4 + 1) * 128], True, True, [b_kr[slot // 4], b_qT[qs]], [b_psS[x]])

                def emit_rest(h):
                    x = h % 2
                    n = nk * 128
                    tt(S, "dve", sbf[x][:, 0:n], psS[x][:, 0:n],
                       biasT[:, h, jr0:5, :].rearrange("p j q -> p (j q)"), ALU.add, [b_psS[x], b_bias], [b_sbf[x]])
                    act(S, pT[x][:, 0:n], sbf[x][:, 0:n], AF.Exp, [b_sbf[x]], [b_pT[x]])
                    for j in range(nk):
                        kt = sc_ - (nk - 1) + j
                        slot = kt % 8
                        mm(S, psO[h // 4][:, (h % 4) * 65:(h % 4) * 65 + 65], pT[x][:, j * 128:(j + 1) * 128],
                           vring[:, slot, h, :], j == 0, j == nk - 1, [b_pT[x], b_vr[slot]], [b_psO[h // 4]])

                emit_scores(0)
                for h in range(8):
                    if h + 1 < 8:
                        emit_scores(h + 1)
                    emit_rest(h)
                for half in range(2):
                    pv = psO[half][:, 0:260].rearrange("p (h e) -> p h e", h=4)
                    recip(S, rs[ai][:, half * 4:(half + 1) * 4], pv[:, :, 64], [b_psO[half]], [b_att[ai]])
                    tt(S, "dve", att[ai][:, half * 256:(half + 1) * 256].rearrange("p (h d) -> p h d", h=4), pv[:, :, 0:64],
                       rs[ai][:, half * 4:(half + 1) * 4].unsqueeze(2).broadcast_to([128, 4, 64]), ALU.mult,
                       [b_psO[half], b_att[ai]], [b_att[ai]])
                m = mi % 2
                mi += 1
                for f in range(4):
                    trp(S, psM[m][:, f * 128:(f + 1) * 128], att[ai][:, f * 128:(f + 1) * 128], K.identf[:], [b_att[ai]], [b_psM[m]])
                cp(S, "act", attT[qs][:, :, s4 * 128:(s4 + 1) * 128], psM[m].rearrange("p (f t) -> p f t", f=4), [b_psM[m]], [b_attT[qs]])
            for ot in range(8):
                g = ot % 2
                m = mi % 2
                mi += 1
                for k in range(8):
                    mm(S, psM[m], wg0[:, k, ot * 128:(ot + 1) * 128], K.HT[:, k, blk], k == 0, k == 7, [bw_g, hb], [b_psM[m]])
                act(S, gsb[g][:], psM[m], AF.Sigmoid, [b_psM[m]], [b_gsb[g]])
                m = mi % 2
                mi += 1
                for f in range(4):
                    mm(S, psM[m], wpa[:, f, ot * 128:(ot + 1) * 128], attT[qs][:, f, :], f == 0, f == 3, [bw_p, b_attT[qs]], [b_psM[m]])
                tt(S, "dve", mrg[qs][:, ot, :], psM[m], gsb[g][:], ALU.mult, [b_psM[m], b_gsb[g]], [b_mrg[qs]])
            for o2 in range(4):
                S.dma("sp", K.merged[:, 2 * o2:2 * o2 + 2, blk], mrg[qs][:, 2 * o2:2 * o2 + 2, :], [b_mrg[qs]], [K.b_merged[tb]])
        S.flush()


def stage_ret(K, S, l):
    with contextlib.ExitStack() as st:
        w_in = K.inp["w_in"][l].rearrange("(k p) c -> p k c", p=128)
        wqk = sbt(K, st, "wqk", [128, 8, 1024], BF16)
        wv = sbt(K, st, "wv", [128, 8, 1024], BF16)
        wgr = sbt(K, st, "wgr", [128, 8, 1024], BF16)
        wg1 = sbt(K, st, "wg1", [128, 8, 1024], BF16)
        wpb = sbt(K, st, "wpb", [128, 8, 1024], BF16)
        bw = Buf()
        for k in range(8):
            load_w(K, S, wqk[:, k, :], w_in[:, k, 1536:2560], bw)
        bw_v, bw_gr, bw_g1, bw_pb = Buf(), Buf(), Buf(), Buf()
        for k in range(8):
            load_w(K, S, wv[:, k, :], w_in[:, k, 2560:3584], bw_v)
        for k in range(8):
            load_w(K, S, wgr[:, k, :], w_in[:, k, 3584:4608], bw_gr)
        for k in range(8):
            load_w(K, S, wg1[:, k, :], w_in[:, k, 6144:7168], bw_g1)
        for k_ in range(8):
            load_w(K, S, wpb[:, k_, :], K.inp["w_proj_b"][l].rearrange("(k p) c -> p k c", p=128)[:, k_, :], bw_pb)
        cosT = sbt(K, st, "cosT", [128, 32, 32], F32)
        sinT = sbt(K, st, "sinT", [128, 32, 32], F32)
        DT = sbt(K, st, "DT", [128, 8, 128], F32)
        qdecT = sbt(K, st, "qdecT", [128, 4, 128], F32)
        kdec = sbt(K, st, "kdec", [128, 8], F32)
        g128 = sbt(K, st, "g128", [128, 4], F32)
        btab = Buf()
        logg = [math.log(1.0 - 2.0 ** (-5.0 - h)) for h in range(8)]
        with contextlib.ExitStack() as st2:
            ii = sbt(K, st2, "ii", [128, 128], I32)
            ff = sbt(K, st2, "ff", [128, 128], F32)
            posf = sbt(K, st2, "posf", [128, 32], F32)
            invf = sbt(K, st2, "invf", [128, 32], F32)
            ang = sbt(K, st2, "ang", [128, 32, 32], F32)
            bt2 = Buf()
            S.op("pool", lambda e: e.iota(ii[:, 0:32], pattern=[[128, 32]], base=0, channel_multiplier=1), (), [bt2])
            cp(S, "dve", posf[:], ii[:, 0:32], [bt2], [bt2])
            for i in range(32):
                mset(S, "pool", invf[:, i:i + 1], float(np.float32(10000.0 ** (-2.0 * i / 64.0))), [bt2])
            tt(S, "dve", ang[:], posf[:].unsqueeze(2).broadcast_to([128, 32, 32]),
               invf[:].unsqueeze(1).broadcast_to([128, 32, 32]), ALU.mult, [bt2], [bt2])
            c_ = sincos(K, S, st2, ang[:], [128, 32, 32], bt2, True)
            cp(S, "dve", cosT[:], c_[:], [bt2], [btab])
            s_ = sincos(K, S, st2, ang[:], [128, 32, 32], bt2, False)
            cp(S, "dve", sinT[:], s_[:], [bt2], [btab])
            S.op("pool", lambda e: e.iota(ii[:], pattern=[[1, 128]], base=0, channel_multiplier=-1), [bt2], [bt2])
            cp(S, "dve", ff[:], ii[:], [bt2], [bt2])
            ts1(S, "dve", ang[:, 0:4, :].rearrange("p a b -> p (a b)"), ff[:], -1.0, ALU.mult, [bt2], [bt2])
            tt(S, "dve", ff[:], ff[:], ang[:, 0:4, :].rearrange("p a b -> p (a b)"), ALU.max, [bt2], [bt2])
            for h in range(8):
                act(S, DT[:, h, :], ff[:], AF.Exp, [bt2], [btab], scale=logg[h])
            mset(S, "pool", DT[64:128, :, 0:64], 0.0, [btab])
            S.op("pool", lambda e: e.iota(ii[:], pattern=[[1, 128]], base=1, channel_multiplier=0), [bt2], [bt2])
            cp(S, "dve", ff[:], ii[:], [bt2], [bt2])
            for h in range(8):
                prt = slice((h % 2) * 64, (h % 2) * 64 + 64)
                act(S, qdecT[prt, h // 2, :], ff[prt, :], AF.Exp, [bt2], [btab], scale=logg[h])
                mset(S, "pool", g128[prt, h // 2:h // 2 + 1], math.exp(128.0 * logg[h]), [btab])
            S.op("pool", lambda e: e.iota(ii[:, 0:1], pattern=[[0, 1]], base=127, channel_multiplier=-1), [bt2], [bt2])
            cp(S, "dve", ff[:, 0:1], ii[:, 0:1], [bt2], [bt2])
            for h in range(8):
                act(S, kdec[:, h:h + 1], ff[:, 0:1], AF.Exp, [bt2], [btab], scale=logg[h])
            S.flush()
        if STOP_AFTER == "rettab":
            return
        xs1 = sbt(K, st, "xs", [128, 512], F32)
        xs = [xs1, xs1]
        xr = [sbt(K, st, "xr", [128, 512], F32) for _ in range(2)]
        tmp = [sbt(K, st, "rtmp", [128, 8, 32], F32) for _ in range(4)]
        b_x1 = Buf()
        b_x = [b_x1, b_x1]
        kcd = sbt(K, st, "kcd", [128, 512], BF16)
        qcT = sbt(K, st, "qcT", [128, 4, 128], BF16)
        qcdT = sbt(K, st, "qcdT", [128, 4, 128], BF16)
        kcT = sbt(K, st, "kcT", [128, 4, 128], BF16)
        vbf = sbt(K, st, "vbf", [128, 1024], BF16)
        b_qk = Buf()
        b_v = Buf()
        sTb = sbt(K, st, "sTb", [128, 1024], BF16)
        b_sT = Buf()
        state = sbt(K, st, "state", [128, 4, 128], F32)
        state_bf = sbt(K, st, "state_bf", [128, 4, 128], BF16)
        b_state = Buf()
        b_sbf = Buf()
        ret = sbt(K, st, "ret", [128, 1024], F32)
        gs = sbt(K, st, "gs", [128, 1024], F32)
        sq = gs
        stat = sbt(K, st, "stat", [128, 6, 8], F32)
        b_ret = Buf()
        b_gs = Buf()
        b_stat = Buf()
        retT1 = sbt(K, st, "retT", [128, 8, 512], BF16)
        retT = [retT1, retT1]
        b_retT1 = Buf()
        b_retT = [b_retT1, b_retT1]
        gsb1 = sbt(K, st, "gsbB", [128, 512], F32)
        gsb = [gsb1, gsb1]
        b_gsb1 = Buf()
        b_gsb = [b_gsb1, b_gsb1]
        mprev = sbt(K, st, "mprev", [128, 8, 512], BF16)
        mrg = mprev
        b_mprev = Buf()
        b_mrg = b_mprev
        psS, b_pS = K.PS[0], K.PSb[0]
        psOo, b_pO = K.PS[1], K.PSb[1]
        psK, b_pK = K.PS[2], K.PSb[2]
        psM = [K.PS[3][:, 0:512], K.PS[3][:, 512:1024]]
        b_psM = [K.PSh[6], K.PSh[7]]
        mset(S, "pool", state[:], 0.0, [b_state])
        mset(S, "pool", state_bf[:], 0.0, [b_sbf])
        mi = 0
        for tb in range(NTB if LIMIT_TB is None else LIMIT_TB):
            blk = slice(tb * 512, (tb + 1) * 512)
            hb = K.HTb[tb]
            rs_ = tb % 2
            for o2 in range(4):
                S.dma("sp", mprev[:, 2 * o2:2 * o2 + 2, :], K.merged[:, 2 * o2:2 * o2 + 2, blk], [K.b_merged[tb]], [b_mprev])
            for s4 in range(4):
                t = tb * 4 + s4
                tok = slice(t * 128, (t + 1) * 128)
                C = cosT[:, t, :].unsqueeze(1).broadcast_to([128, 8, 32])
                Sn = sinT[:, t, :].unsqueeze(1).broadcast_to([128, 8, 32])
                for qk in range(2):
                    m = mi % 2
                    mi += 1
                    for k in range(8):
                        mm(S, psM[m], K.HT[:, k, tok], wqk[:, k, qk * 512:(qk + 1) * 512], k == 0, k == 7, [bw, hb], [b_psM[m]])
                    cp(S, "act", xs[qk][:], psM[m], [b_psM[m]], [b_x[qk]])
                    x3 = xs[qk][:].rearrange("p (h d) -> p h d", h=8)
                    o3 = xr[qk][:].rearrange("p (h d) -> p h d", h=8)
                    x1, x2 = x3[:, :, 0:32], x3[:, :, 32:64]
                    tt(S, "dve", tmp[0][:], x1, C, ALU.mult, [b_x[qk], btab], [b_x[qk]])
                    tt(S, "dve", tmp[1][:], x2, Sn, ALU.mult, [b_x[qk], btab], [b_x[qk]])
                    tt(S, "dve", o3[:, :, 0:32], tmp[0][:], tmp[1][:], ALU.subtract, [b_x[qk]], [b_x[qk]])
                    tt(S, "dve", tmp[2][:], x1, Sn, ALU.mult, [b_x[qk], btab], [b_x[qk]])
                    tt(S, "dve", tmp[3][:], x2, C, ALU.mult, [b_x[qk], btab], [b_x[qk]])
                    tt(S, "dve", o3[:, :, 32:64], tmp[2][:], tmp[3][:], ALU.add, [b_x[qk]], [b_x[qk]])
                if RET_CUT == 1:
                    S.flush()
                    return
                tt(S, "dve", kcd[:].rearrange("p (h d) -> p h d", h=8), xr[1][:].rearrange("p (h d) -> p h d", h=8),
                   kdec[:].unsqueeze(2).broadcast_to([128, 8, 64]), ALU.mult, [b_x[1], btab], [b_qk])
                m = mi % 2
                mi += 1
                for f in range(4):
                    trp(S, psM[m][:, f * 128:(f + 1) * 128], xr[0][:, f * 128:(f + 1) * 128], K.identf[:], [b_x[0]], [b_psM[m]])
                act(S, qcT[:].rearrange("p f t -> p (f t)"), psM[m], AF.Copy, [b_psM[m]], [b_qk], scale=0.125)
                stt(S, "dve", qcdT[:].rearrange("p f t -> p (f t)"), psM[m], 0.125, qdecT[:].rearrange("p f t -> p (f t)"),
                    ALU.mult, ALU.mult, [b_psM[m], btab], [b_qk])
                m = mi % 2
                mi += 1
                for f in range(4):
                    trp(S, psM[m][:, f * 128:(f + 1) * 128], xr[1][:, f * 128:(f + 1) * 128], K.identf[:], [b_x[1]], [b_psM[m]])
                cp(S, "act", kcT[:].rearrange("p f t -> p (f t)"), psM[m], [b_psM[m]], [b_qk])
                if RET_CUT == 2:
                    S.flush()
                    return
                for half in range(2):
                    m = mi % 2
                    mi += 1
                    for k in range(8):
                        mm(S, psM[m], K.HT[:, k, tok], wv[:, k, half * 512:(half + 1) * 512], k == 0, k == 7, [bw_v, hb], [b_psM[m]])
                    cp(S, "act", vbf[:, half * 512:(half + 1) * 512], psM[m], [b_psM[m]], [b_v])
                if RET_CUT == 3:
                    S.flush()
                    return
                for h in range(8):
                    prt = slice((h % 2) * 64, (h % 2) * 64 + 64)
                    c0 = (h % 2) * 512 + (h // 2) * 128
                    mm(S, psS[:, c0:c0 + 128], kcT[prt, h // 2, :], qcT[prt, h // 2, :], True, True, [b_qk], [b_pS])
                if RET_CUT == 40:
                    S.flush()
                    return
                for half in range(2):
                    cs = slice(half * 512, (half + 1) * 512)
                    tt(S, "dve", sTb[:, cs].rearrange("p (h l) -> p h l", h=4), psS[:, cs].rearrange("p (h l) -> p h l", h=4),
                       DT[:, half::2, :], ALU.mult, [b_pS, btab], [b_sT])
                if RET_CUT == 4:
                    S.flush()
                    return
                for h in range(8):
                    prt = slice((h % 2) * 64, (h % 2) * 64 + 64)
                    hs = slice(h * 128, (h + 1) * 128)
                    c0 = (h % 2) * 512 + (h // 2) * 128
                    mm(S, psOo[:, hs], sTb[:, c0:c0 + 128], vbf[:, hs], True, False, [b_sT, b_v], [b_pO])
                    mm(S, psOo[:, hs], qcdT[prt, h // 2, :], state_bf[prt, h // 2, :], False, True, [b_qk, b_sbf], [b_pO])
                if RET_CUT == 5:
                    S.flush()
                    return
                for h in range(8):
                    hs = slice(h * 128, (h + 1) * 128)
                    pr = (h // 2) * 128
                    mm(S, psK[:, hs], kcd[:, pr:pr + 128], vbf[:, hs], True, True, [b_qk, b_v], [b_pK])
                for h in range(8):
                    prt = slice((h % 2) * 64, (h % 2) * 64 + 64)
                    hs = slice(h * 128, (h + 1) * 128)
                    stt(S, "dve", state[prt, h // 2, :], state[prt, h // 2, :], g128[prt, h // 2:h // 2 + 1], psK[prt, hs],
                        ALU.mult, ALU.add, [b_state, b_pK, btab], [b_state])
                cp(S, "act", state_bf[:], state[:], [b_state], [b_sbf])
                if RET_CUT == 6:
                    S.flush()
                    return
                o3 = psOo[:, :].rearrange("p (h e) -> p h e", h=8)
                S.op("dve", lambda e, o3=o3: e.tensor_reduce(out=stat[:, 0, :], in_=o3, axis=AX.X, op=ALU.add), [b_pO], [b_stat])
                act(S, sq[:], psOo[:, :], AF.Square, [b_pO], [b_gs])
                S.op("dve", lambda e: e.tensor_reduce(out=stat[:, 1, :], in_=sq[:].rearrange("p (h e) -> p h e", h=8), axis=AX.X, op=ALU.add),
                     [b_gs], [b_stat])
                ts1(S, "dve", stat[:, 2, :], stat[:, 0, :], 1.0 / 128.0, ALU.mult, [b_stat], [b_stat])
                tt(S, "dve", stat[:, 3, :], stat[:, 2, :], stat[:, 2, :], ALU.mult, [b_stat], [b_stat])
                stt(S, "dve", stat[:, 4, :], stat[:, 1, :], 1.0 / 128.0, stat[:, 3, :], ALU.mult, ALU.subtract, [b_stat], [b_stat])
                ts1(S, "dve", stat[:, 4, :], stat[:, 4, :], LN_EPS, ALU.add, [b_stat], [b_stat])
                act(S, stat[:, 4, :], stat[:, 4, :], AF.Sqrt, [b_stat], [b_stat])
                recip(S, stat[:, 5, :], stat[:, 4, :], [b_stat], [b_stat])
                r3 = ret[:].rearrange("p (h e) -> p h e", h=8)
                tt(S, "dve", r3, o3, stat[:, 2, :].unsqueeze(2).broadcast_to([128, 8, 128]), ALU.subtract, [b_pO, b_stat, b_ret], [b_ret])
                tt(S, "dve", r3, r3, stat[:, 5, :].unsqueeze(2).broadcast_to([128, 8, 128]), ALU.mult, [b_ret, b_stat], [b_ret])
                if RET_CUT == 7:
                    S.flush()
                    return
                for half in range(2):
                    m = mi % 2
                    mi += 1
                    for k in range(8):
                        mm(S, psM[m], K.HT[:, k, tok], wgr[:, k, half * 512:(half + 1) * 512], k == 0, k == 7, [bw_gr, hb], [b_psM[m]])
                    act(S, gs[:, half * 512:(half + 1) * 512], psM[m], AF.Silu, [b_psM[m]], [b_gs])
                tt(S, "dve", ret[:], ret[:], gs[:], ALU.mult, [b_ret, b_gs], [b_ret])
                for half in range(2):
                    m = mi % 2
                    mi += 1
                    for f in range(4):
                        c0 = (half * 4 + f) * 128
                        trp(S, psM[m][:, f * 128:(f + 1) * 128], ret[:, c0:c0 + 128], K.identf[:], [b_ret], [b_psM[m]])
                    cp(S, "act", retT[rs_][:, half * 4:(half + 1) * 4, s4 * 128:(s4 + 1) * 128],
                       psM[m].rearrange("p (f t) -> p f t", f=4), [b_psM[m]], [b_retT[rs_]])
            for ot in range(8):
                g = ot % 2
                m = mi % 2
                mi += 1
                for k in range(8):
                    mm(S, psM[m], wg1[:, k, ot * 128:(ot + 1) * 128], K.HT[:, k, blk], k == 0, k == 7, [bw_g1, hb], [b_psM[m]])
                act(S, gsb[g][:], psM[m], AF.Sigmoid, [b_psM[m]], [b_gsb[g]])
                m = mi % 2
                mi += 1
                for f in range(8):
                    mm(S, psM[m], wpb[:, f, ot * 128:(ot + 1) * 128], retT[rs_][:, f, :], f == 0, f == 7, [bw_pb, b_retT[rs_]], [b_psM[m]])
                tt(S, "dve", gsb[g][:], psM[m], gsb[g][:], ALU.mult, [b_psM[m], b_gsb[g]], [b_gsb[g]])
                tt(S, "dve", mrg[:, ot, :], gsb[g][:], mprev[:, ot, :], ALU.add, [b_gsb[g], b_mprev], [b_mrg])
            for o2 in range(4):
                S.dma("sp", K.merged[:, 2 * o2:2 * o2 + 2, blk], mrg[:, 2 * o2:2 * o2 + 2, :], [b_mrg], [K.b_merged[tb]])
        S.flush()


PW = list(range(9)) + [16, 32, 64, 128, 256, 512, 1024, 2048]
PWI = {m: i for i, m in enumerate(PW)}
NPW = len(PW)


def s5_setup(K, S, l, Tt, ARt, AIt, AInt, b_T):
    with contextlib.ExitStack() as st:
        b = Buf()
        raw = sbt(K, st, "raw", [16, 3, 128], F32)
        ls2 = sbt(K, st, "ls2", [16, 2], F32)
        S.dma("sp", raw[:, 0, :], K.inp["lam_re"][l].rearrange("(pi g) p -> pi (g p)", g=2), (), [b])
        S.dma("sp", raw[:, 1, :], K.inp["lam_im"][l].rearrange("(pi g) p -> pi (g p)", g=2), (), [b])
        S.dma("sp", ls2[:], K.inp["log_step"][l].rearrange("(pi g) -> pi g", g=2), (), [b])
        cp(S, "dve", raw[:, 2, :].rearrange("q (g p) -> q g p", g=2), ls2[:].unsqueeze(2).broadcast_to([16, 2, 64]), [b], [b])
        draw = sbt(K, st, "draw", [4, 128], F32)
        S.dma("sp", draw[:], K.inp["d_skip"][l].rearrange("(a q) -> a q", q=128), (), [b])
        ps = K.PS[3]
        pb = K.PSb[3]
        for i in range(3):
            trp(S, ps[:, i * 16:(i + 1) * 16], raw[:, i, :], K.identf[0:16, 0:16], [b], [pb])
        trp(S, ps[:, 48:52], draw[:], K.identf[0:4, 0:4], [b], [pb])
        lam = sbt(K, st, "lam", [128, 4, 16], F32)
        dcol = sbt(K, st, "dcol", [128, 4], F32)
        cp(S, "dve", lam[:, 0:3, :].rearrange("p a b -> p (a b)"), ps[:, 0:48], [pb], [b])
        cp(S, "dve", dcol[:], ps[:, 48:52], [pb], [b])
        lr, li, stp = lam[:, 0, :], lam[:, 1, :], lam[:, 2, :]
        act(S, stp, stp, AF.Exp, [b], [b])
        er = sbt(K, st, "er", [128, 16], F32)
        th = sbt(K, st, "th", [128, 16], F32)
        tt(S, "dve", er[:], lr, stp, ALU.mult, [b], [b])
        tt(S, "dve", th[:], li, stp, ALU.mult, [b], [b])
        angM = sbt(K, st, "angM", [128, NPW + 1, 16], F32)
        magM = sbt(K, st, "magM", [128, NPW, 16], F32)
        for i, m in enumerate(PW):
            ts1(S, "dve", angM[:, i, :], th[:], float(m), ALU.mult, [b], [b])
            act(S, magM[:, i, :], er[:], AF.Exp, [b], [b], scale=float(m))
        ts1(S, "dve", angM[:, NPW, :], th[:], 0.5, ALU.mult, [b], [b])
        cosM = sincos(K, S, st, angM[:], [128, NPW + 1, 16], b, True)
        sinM = sincos(K, S, st, angM[:], [128, NPW + 1, 16], b, False)
        tt(S, "dve", ARt[:], magM[:], cosM[:, 0:NPW, :], ALU.mult, [b], [b_T])
        tt(S, "dve", AIt[:], magM[:], sinM[:, 0:NPW, :], ALU.mult, [b], [b_T])
        ts1(S, "dve", AInt[:], AIt[:], -1.0, ALU.mult, [b_T], [b_T])
        w = sbt(K, st, "wk", [128, 8, 16], F32)
        t0, t1, nr, ni, den, cr, ci, t2 = [w[:, i, :] for i in range(8)]
        ts(S, "dve", t0, er[:], 1.0 / 120.0, 1.0 / 24.0, ALU.mult, ALU.add, [b], [b])
        for cst in (1.0 / 6.0, 0.5, 1.0):
            tt(S, "dve", t0, t0, er[:], ALU.mult, [b], [b])
            ts1(S, "dve", t0, t0, cst, ALU.add, [b], [b])
        tt(S, "dve", t0, t0, er[:], ALU.mult, [b], [b])
        tt(S, "dve", nr, t0, cosM[:, 1, :], ALU.mult, [b], [b])
        tt(S, "dve", t1, sinM[:, NPW, :], sinM[:, NPW, :], ALU.mult, [b], [b])
        stt(S, "dve", nr, t1, -2.0, nr, ALU.mult, ALU.add, [b], [b])
        cp(S, "dve", ni, AIt[:, 1, :], [b, b_T], [b])
        tt(S, "dve", den, lr, lr, ALU.mult, [b], [b])
        tt(S, "dve", t1, li, li, ALU.mult, [b], [b])
        tt(S, "dve", den, den, t1, ALU.add, [b], [b])
        recip(S, den, den, [b], [b])
        tt(S, "dve", cr, nr, lr, ALU.mult, [b], [b])
        tt(S, "dve", t1, ni, li, ALU.mult, [b], [b])
        tt(S, "dve", cr, cr, t1, ALU.add, [b], [b])
        tt(S, "dve", cr, cr, den, ALU.mult, [b], [b])
        tt(S, "dve", ci, ni, lr, ALU.mult, [b], [b])
        tt(S, "dve", t1, nr, li, ALU.mult, [b], [b])
        tt(S, "dve", ci, ci, t1, ALU.subtract, [b], [b])
        tt(S, "dve", ci, ci, den, ALU.mult, [b], [b])
        Bn = sbt(K, st, "Bn", [128, 4, 16, 16], F32)
        for ri, nm in enumerate(("b_re", "b_im")):
            srcB = K.inp[nm][l].rearrange("(pi g) p k -> (g p) pi k", g=2)
            for pi_ in range(16):
                S.dma("sp", Bn[:, ri, pi_, :], srcB[:, pi_, :], (), [b])
        tmpB = sbt(K, st, "tmpB", [128, 16, 16], F32)

        def bc(x):
            return x.unsqueeze(2).broadcast_to([128, 16, 16])
        tt(S, "dve", Bn[:, 2], Bn[:, 0], bc(cr), ALU.mult, [b], [b])
        tt(S, "dve", tmpB[:], Bn[:, 1], bc(ci), ALU.mult, [b], [b])
        tt(S, "dve", Bn[:, 2], Bn[:, 2], tmpB[:], ALU.subtract, [b], [b])
        tt(S, "dve", Bn[:, 3], Bn[:, 1], bc(cr), ALU.mult, [b], [b])
        tt(S, "dve", tmpB[:], Bn[:, 0], bc(ci), ALU.mult, [b], [b])
        tt(S, "dve", Bn[:, 3], Bn[:, 3], tmpB[:], ALU.add, [b], [b])
        WA = sbt(K, st, "WA", [128, 2, 8, 16, 16], F32)
        for jp in range(8):
            pi_ = PWI[7 - jp]
            ar, ai = bc(ARt[:, pi_, :]), bc(AIt[:, pi_, :])
            tt(S, "dve", WA[:, 0, jp], Bn[:, 2], ar, ALU.mult, [b, b_T], [b])
            tt(S, "dve", tmpB[:], Bn[:, 3], ai, ALU.mult, [b, b_T], [b])
            tt(S, "dve", WA[:, 0, jp], WA[:, 0, jp], tmpB[:], ALU.subtract, [b], [b])
            tt(S, "dve", WA[:, 1, jp], Bn[:, 3], ar, ALU.mult, [b, b_T], [b])
            tt(S, "dve", tmpB[:], Bn[:, 2], ai, ALU.mult, [b, b_T], [b])
            tt(S, "dve", WA[:, 1, jp], WA[:, 1, jp], tmpB[:], ALU.add, [b], [b])
        craw = sbt(K, st, "craw", [128, 2, 4, 2, 64], F32)
        for ri, nm in enumerate(("c_re", "c_im")):
            src = K.inp[nm][l].rearrange("(a g) k p -> (g k) a p", g=8)
            for a in range(4):
                S.dma("sp", craw[:, ri, a, 0, :], src[:, a, :], (), [b])
                S.dma("sp", craw[:, ri, a, 1, :], src[:, a, :], (), [b])
        CT = sbt(K, st, "CT", [128, 2, 16, 16], F32)
        for ri in range(2):
            for a in range(4):
                pa_ = K.PS[2][:, (a % 2) * 512:(a % 2) * 512 + 128]
                pba = K.PSh[4 + a % 2]
                trp(S, pa_, craw[:, ri, a].rearrange("p d q -> p (d q)"), K.identf[:], [b], [pba])
                p3 = pa_.rearrange("p (g k) -> p g k", k=16)
                cp(S, "dve", CT[0:64, ri, 4 * a:4 * a + 4, :], p3[0:64, 0::2, :], [pba], [b])
                cp(S, "dve", CT[64:128, ri, 4 * a:4 * a + 4, :], p3[64:128, 1::2, :], [pba], [b])
        VA = sbt(K, st, "VA", [128, 2, 8, 16, 16], F32)
        for j in range(8):
            pi_ = PWI[j + 1]
            ar, ai = bc(ARt[:, pi_, :]), bc(AIt[:, pi_, :])
            tt(S, "dve", VA[:, 0, j], CT[:, 0], ar, ALU.mult, [b, b_T], [b])
            tt(S, "dve", tmpB[:], CT[:, 1], ai, ALU.mult, [b, b_T], [b])
            tt(S, "dve", VA[:, 0, j], VA[:, 0, j], tmpB[:], ALU.subtract, [b], [b])
            tt(S, "dve", VA[:, 1, j], CT[:, 0], ai, ALU.mult, [b, b_T], [b])
            tt(S, "dve", tmpB[:], CT[:, 1], ar, ALU.mult, [b, b_T], [b])
            tt(S, "dve", VA[:, 1, j], VA[:, 1, j], tmpB[:], ALU.add, [b], [b])
            ts1(S, "dve", VA[:, 1, j], VA[:, 1, j], -1.0, ALU.mult, [b], [b])
        Vst = [sbt(K, st, "Vst", [128, 2, 8, 128], BF16) for _ in range(4)]
        Nat = [sbt(K, st, "Nat", [128, 2, 8, 128], F32) for _ in range(4)]
        CTp = sbt(K, st, "CTp", [128, 4, 2, 128], F32)
        Wst = [sbt(K, st, "Wst", [128, 2, 8, 128], BF16) for _ in range(2)]
        b_V = [Buf() for _ in range(4)]
        b_N = [Buf() for _ in range(4)]
        b_W = [Buf() for _ in range(2)]
        b_C = Buf()
        for q in range(4):
            mset(S, "pool", Vst[q][:], 0.0, [b_V[q]])
            mset(S, "pool", Nat[q][:], 0.0, [b_N[q]])
        mset(S, "pool", CTp[:], 0.0, [b_C])
        psT = [K.PS[0], K.PS[1]]
        b_psT = [K.PSb[0], K.PSb[1]]
        wi = 0
        for a in range(4):
            for q in range(4):
                pi_ = 4 * a + q
                lo, hi = slice(0, 64), slice(64, 128)
                c0, c1 = slice(32 * q, 32 * q + 16), slice(32 * q + 16, 32 * q + 32)
                for ri in range(2):
                    cp(S, "dve", Vst[q][lo, ri, :, c0], VA[lo, ri, :, pi_, :], [b], [b_V[q]])
                    cp(S, "dve", Vst[q][hi, ri, :, c1], VA[hi, ri, :, pi_, :], [b], [b_V[q]])
                    cp(S, "dve", Nat[q][lo, ri, :, c0], WA[lo, ri, :, pi_, :], [b], [b_N[q]])
                    cp(S, "dve", Nat[q][hi, ri, :, c1], WA[hi, ri, :, pi_, :], [b], [b_N[q]])
                    cp(S, "dve", CTp[lo, q, ri, c0], CT[lo, ri, pi_, :], [b], [b_C])
                    cp(S, "dve", CTp[hi, q, ri, c1], CT[hi, ri, pi_, :], [b], [b_C])
                ts1(S, "dve", CTp[:, q, 1, :], CTp[:, q, 1, :], -1.0, ALU.mult, [b_C], [b_C])
                S.dma("sp", K.Vd[pi_], Vst[q][:], [b_V[q]], [K.b_Vd])
                ws = wi % 2
                wi += 1
                for ri in range(2):
                    for jh in range(2):
                        x = (ri * 2 + jh) % 2
                        for jj in range(4):
                            trp(S, psT[x][:, jj * 128:(jj + 1) * 128], Nat[q][:, ri, jh * 4 + jj, :], K.identf[:], [b_N[q]], [b_psT[x]])
                        cp(S, "act", Wst[ws][:, ri, jh * 4:jh * 4 + 4, :].rearrange("p j c -> p (j c)"), psT[x][:, 0:512], [b_psT[x]], [b_W[ws]])
                S.dma("sp", K.Wd[pi_], Wst[ws][:], [b_W[ws]], [K.b_Wd])
            for dl in range(8):
                x = dl % 2
                pT_ = K.PS[3][:, x * 512:x * 512 + 128]
                pbT = K.PSh[6 + x]
                n = 0
                for q in range(4):
                    for ri in range(2):
                        mm(S, pT_, Nat[q][:, ri, 7 - dl, :], CTp[:, q, ri, :], n == 0, n == 7, [b_N[q], b_C], [pbT])
                        n += 1
                if dl == 0:
                    stt(S, "dve", Tt[:, a, dl, :], K.identf[:], dcol[:, a:a + 1], pT_, ALU.mult, ALU.add, [pbT, b], [b_T])
                else:
                    cp(S, "dve", Tt[:, a, dl, :], pT_, [pbT], [b_T])
        S.flush()


def stage_s5(K, S, l, yT, b_yT, Tt, ARt, AIt, AInt, b_T):
    with contextlib.ExitStack() as st:
        w_in = K.inp["w_in"][l].rearrange("(k p) c -> p k c", p=128)
        wu = sbt(K, st, "wu", [128, 8, 512], BF16)
        bw = Buf()
        for k in range(8):
            load_w(K, S, wu[:, k, :], w_in[:, k, 4608:5120], bw)
        ucT = sbt(K, st, "ucT", [128, 4, T], BF16)
        b_uc = [Buf() for _ in range(4)]
        psM = [K.PS[3][:, 0:512], K.PS[3][:, 512:1024]]
        b_psM = [K.PSh[6], K.PSh[7]]
        mi = 0
        for tb in range(NTB):
            blk = slice(tb * 512, (tb + 1) * 512)
            for a in range(4):
                m = mi % 2
                mi += 1
                for k in range(8):
                    mm(S, psM[m], wu[:, k, a * 128:(a + 1) * 128], K.HT[:, k, blk], k == 0, k == 7, [bw, K.HTb[tb]], [b_psM[m]])
                cp(S, "act", ucT[:, a, blk], psM[m], [b_psM[m]], [b_uc[a]])
        Vsb = sbt(K, st, "Vsb", [128, 4, 2, 8, 128], BF16)
        Wsb = [sbt(K, st, "Wsb", [128, 2, 8, 128], BF16) for _ in range(2)]
        b_Vsb = Buf()
        b_Wsb = [Buf() for _ in range(2)]
        X = [[sbt(K, st, "X", [128, 512], F32) for _ in range(2)] for _ in range(2)]
        b_X = [[Buf() for _ in range(2)] for _ in range(2)]
        Xp = [sbt(K, st, "Xp", [128, 4, 512], BF16) for _ in range(2)]
        ptmp = sbt(K, st, "ptmp", [128, 512], F32)
        b_ptmp = Buf()
        b_Xp = Buf()
        psA = [K.PS[0][:, 0:512], K.PS[0][:, 512:1024], K.PS[1][:, 0:512], K.PS[1][:, 512:1024]]
        b_psA = [K.PSh[0], K.PSh[1], K.PSh[2], K.PSh[3]]
        for i in range(4):
            b_psA[i].w = K.PSb[i // 2].w
            b_psA[i].r = dict(K.PSb[i // 2].r)
        ai_ = 0
        wi = 0
        for a in range(4):
            for q in range(4):
                S.dma("sp", Vsb[:, q], K.Vd[4 * a + q], [K.b_Vd], [b_Vsb])
            mset(S, "pool", Xp[0][:, :, 0:1], 0.0, [b_Xp])
            mset(S, "pool", Xp[1][:, :, 0:1], 0.0, [b_Xp])
            for q in range(4):
                pi_ = 4 * a + q
                ws = wi % 2
                wi += 1
                S.dma("sp", Wsb[ws][:], K.Wd[pi_], [K.b_Wd], [b_Wsb[ws]])
                for ri in range(2):
                    x = ai_ % 4
                    ai_ += 1
                    for jp in range(8):
                        mm(S, psA[x], Wsb[ws][:, ri, jp, :], ucT[:, a, jp::8], jp == 0, jp == 7, [b_Wsb[ws], b_uc[a]], [b_psA[x]])
                    cp(S, "act", X[0][ri][:], psA[x], [b_psA[x]], [b_X[0][ri]])
                cur = 0
                for lv in range(9):
                    sft = 1 << lv
                    pw = PWI[8 * sft]
                    ar = ARt[:, pw, pi_:pi_ + 1]
                    ai = AIt[:, pw, pi_:pi_ + 1]
                    an = AInt[:, pw, pi_:pi_ + 1]
                    o, n_ = X[cur], X[1 - cur]
                    bo, bn = b_X[cur], b_X[1 - cur]
                    hd, tl, bd = slice(0, sft), slice(sft, 512), slice(0, 512 - sft)
                    stt(S, "dve", n_[0][:, tl], o[0][:, bd], ar, o[0][:, tl], ALU.mult, ALU.add, [bo[0], b_T], [bn[0]])
                    stt(S, "dve", n_[0][:, tl], o[1][:, bd], an, n_[0][:, tl], ALU.mult, ALU.add, [bo[1], b_T], [bn[0]])
                    cp(S, "act", n_[0][:, hd], o[0][:, hd], [bo[0]], [bn[0]])
                    stt(S, "dve", n_[1][:, tl], o[0][:, bd], ai, o[1][:, tl], ALU.mult, ALU.add, [bo[0], bo[1], b_T], [bn[1]])
                    stt(S, "dve", n_[1][:, tl], o[1][:, bd], ar, n_[1][:, tl], ALU.mult, ALU.add, [bo[1], b_T], [bn[1]])
                    cp(S, "act", n_[1][:, hd], o[1][:, hd], [bo[1]], [bn[1]])
                    cur = 1 - cur
                for ri in range(2):
                    cp(S, "act", Xp[ri][:, q, 1:512], X[cur][ri][:, 0:511], [b_X[cur][ri]], [b_Xp])
            for j in range(8):
                x = ai_ % 4
                ai_ += 1
                n = 0
                tot = 8 + j + 1
                for q in range(4):
                    for ri in range(2):
                        mm(S, psA[x], Vsb[:, q, ri, j, :], Xp[ri][:, q, :], n == 0, n == tot - 1, [b_Vsb, b_Xp], [b_psA[x]])
                        n += 1
                for jp in range(j + 1):
                    mm(S, psA[x], Tt[:, a, j - jp, :], ucT[:, a, jp::8], n == 0, n == tot - 1, [b_T, b_uc[a]], [b_psA[x]])
                    n += 1
                act(S, yT[:, a, j::8], psA[x], AF.Gelu_apprx_tanh, [b_psA[x]], [b_yT])
        S.flush()


def stage_merge(K, S, l, yT, b_yT):
    with contextlib.ExitStack() as st:
        w_in = K.inp["w_in"][l].rearrange("(k p) c -> p k c", p=128)
        wglu = sbt(K, st, "wglu", [128, 4, 512], BF16)
        wpc = sbt(K, st, "wpc", [128, 4, 1024], BF16)
        wg2 = sbt(K, st, "wg2", [128, 8, 1024], BF16)
        wo = sbt(K, st, "wo", [128, 8, 1024], BF16)
        bw = Buf()
        bw_pc, bw_g2, bw_o = Buf(), Buf(), Buf()
        for k_ in range(4):
            load_w(K, S, wglu[:, k_, :], K.inp["w_glu"][l].rearrange("(k p) c -> p k c", p=128)[:, k_, :], bw)
        for k_ in range(4):
            load_w(K, S, wpc[:, k_, :], K.inp["w_proj_c"][l].rearrange("(k p) c -> p k c", p=128)[:, k_, :], bw_pc)
        for k in range(8):
            load_w(K, S, wg2[:, k, :], w_in[:, k, 7168:8192], bw_g2)
        for k_ in range(8):
            load_w(K, S, wo[:, k_, :], K.inp["w_o"][l].rearrange("(k p) c -> p k c", p=128)[:, k_, :], bw_o)
        sc = ln_scratch(K, st)
        ln_load_gb(K, S, sc, K.inp["ln1_g"][l], K.inp["ln1_b"][l])
        zs = sbt(K, st, "zs", [128, 512], F32)
        b_zs = Buf()
        ygT = sbt(K, st, "ygT", [128, 4, 512], BF16)
        b_yg = Buf()
        gsb = sbt(K, st, "gsbC", [128, 512], F32)
        b_gsb = Buf()
        mprev = sbt(K, st, "mprevC", [128, 8, 512], BF16)
        b_mprev = Buf()
        mT = sbt(K, st, "mT", [128, 8, 512], BF16)
        b_mT = Buf()
        hprev = [sbt(K, st, "hprev", [128, 1024], F32) for _ in range(2)]
        b_hp = [Buf() for _ in range(2)]
        psM = [K.PS[3][:, 0:512], K.PS[3][:, 512:1024]]
        b_psM = [K.PSh[6], K.PSh[7]]
        mi = 0
        for tb in range(NTB):
            blk = slice(tb * 512, (tb + 1) * 512)
            hb = K.HTb[tb]
            for o2 in range(4):
                S.dma("sp", mprev[:, 2 * o2:2 * o2 + 2, :], K.merged[:, 2 * o2:2 * o2 + 2, blk], [K.b_merged[tb]], [b_mprev])
            for ot in range(4):
                m = mi % 2
                mi += 1
                for k in range(4):
                    mm(S, psM[m], wglu[:, k, ot * 128:(ot + 1) * 128], yT[:, k, blk], k == 0, k == 3, [bw, b_yT], [b_psM[m]])
                act(S, zs[:], psM[m], AF.Sigmoid, [b_psM[m]], [b_zs])
                tt(S, "dve", ygT[:, ot, :], yT[:, ot, blk], zs[:], ALU.mult, [b_yT, b_zs], [b_yg])
            for ot in range(8):
                m = mi % 2
                mi += 1
                for k in range(8):
                    mm(S, psM[m], wg2[:, k, ot * 128:(ot + 1) * 128], K.HT[:, k, blk], k == 0, k == 7, [bw_g2, hb], [b_psM[m]])
                act(S, gsb[:], psM[m], AF.Sigmoid, [b_psM[m]], [b_gsb])
                m = mi % 2
                mi += 1
                for k in range(4):
                    mm(S, psM[m], wpc[:, k, ot * 128:(ot + 1) * 128], ygT[:, k, :], k == 0, k == 3, [bw_pc, b_yg], [b_psM[m]])
                tt(S, "dve", gsb[:], psM[m], gsb[:], ALU.mult, [b_psM[m], b_gsb], [b_gsb])
                tt(S, "dve", mT[:, ot, :], gsb[:], mprev[:, ot, :], ALU.add, [b_gsb, b_mprev], [b_mT])
            for t4 in range(4):
                t = tb * 4 + t4
                i = t % 2
                S.dma("sp", hprev[i][:], K.hres[t * 128:(t + 1) * 128, :], [K.b_hres[tb]], [b_hp[i]])
                pso, bpso = K.PS[i], K.PSb[i]
                for half in range(2):
                    for f in range(8):
                        mm(S, pso[:, half * 512:(half + 1) * 512], mT[:, f, t4 * 128:(t4 + 1) * 128], wo[:, f, half * 512:(half + 1) * 512],
                           f == 0, f == 7, [bw_o, b_mT], [bpso])
                for half in range(2):
                    cs = slice(half * 512, (half + 1) * 512)
                    stt(S, "dve", hprev[i][:, cs], hprev[i][:, cs], ALPHA, pso[:, cs], ALU.mult, ALU.add, [b_hp[i], bpso], [b_hp[i]])
                ln_apply(K, S, sc, hprev[i][:], b_hp[i], t, K.hres, K.b_hres[tb], K.PS[2][:, :], K.PSb[2])
        S.flush()


def stage_ffn(K, S, l, last):
    HF = 1408
    for p in range(2):
        with contextlib.ExitStack() as st:
            w_up = K.inp["w_up"][l].rearrange("(k p) c -> p k c", p=128)
            wup = sbt(K, st, "wup", [128, 8, 2 * HF], BF16)
            wdn = sbt(K, st, "wdn", [128, 11, 1024], BF16)
            bw = Buf()
            bw_d = Buf()
            for k in range(8):
                load_w(K, S, wup[:, k, 0:HF], w_up[:, k, p * HF:(p + 1) * HF], bw)
                load_w(K, S, wup[:, k, HF:2 * HF], w_up[:, k, 2816 + p * HF:2816 + (p + 1) * HF], bw)
            for k_ in range(11):
                load_w(K, S, wdn[:, k_, :], K.inp["w_down"][l][p * HF:(p + 1) * HF, :].rearrange("(k p) c -> p k c", p=128)[:, k_, :], bw_d)
            craw = sbt(K, st, "cwraw", [88, 128], F32)
            cw = sbt(K, st, "cw", [128, 88], F32)
            b_cw = Buf()
            cwl = K.inp["conv_w"][l]
            cbl = K.inp["conv_b"][l]
            for t3 in range(3):
                S.dma("sp", craw[t3 * 22:t3 * 22 + 11, :], cwl[t3, p * HF:(p + 1) * HF].rearrange("(n q) -> n q", q=128), (), [b_cw])
                S.dma("sp", craw[t3 * 22 + 11:t3 * 22 + 22, :], cwl[t3, 2816 + p * HF:2816 + (p + 1) * HF].rearrange("(n q) -> n q", q=128), (), [b_cw])
            S.dma("sp", craw[66:77, :], cbl[p * HF:(p + 1) * HF].rearrange("(n q) -> n q", q=128), (), [b_cw])
            S.dma("sp", craw[77:88, :], cbl[2816 + p * HF:2816 + (p + 1) * HF].rearrange("(n q) -> n q", q=128), (), [b_cw])
            trp(S, K.PS[2][:, 0:88], craw[:], K.identf[0:88, 0:88], [b_cw], [K.PSb[2]])
            cp(S, "dve", cw[:], K.PS[2][:, 0:88], [K.PSb[2]], [b_cw])
            if RET_CUT == 101:
                S.flush()
                return
            sc = ln_scratch(K, st)
            if p == 1:
                ln_load_gb(K, S, sc, K.inp["ln2_g"][l], K.inp["ln2_b"][l])
            diagw = sbt(K, st, "diagw", [128, 22, 3, 128], BF16)
            b_dg = Buf()
            for c in range(22):
                for t3 in range(3):
                    ts1(S, "dve", diagw[:, c, t3, :], K.identf[:], cw[:, t3 * 22 + c:t3 * 22 + c + 1], ALU.mult, [b_cw], [b_dg])
            xbuf = [sbt(K, st, "xbuf", [128, 514], BF16) for _ in range(2)]
            cva = [sbt(K, st, "cva", [128, 512], F32) for _ in range(2)]
            b_xb = [Buf() for _ in range(2)]
            b_cva = [Buf() for _ in range(2)]
            halo = sbt(K, st, "halo", [128, 22, 2], BF16)
            b_halo = Buf()
            mset(S, "pool", halo[:], 0.0, [b_halo])
            actT = sbt(K, st, "actT", [128, 11, 512], BF16)
            b_actT = Buf()
            hprev = [sbt(K, st, "hprevF", [128, 1024], F32) for _ in range(2)]
            b_hp = [Buf() for _ in range(2)]
            psM = [K.PS[3][:, 0:512], K.PS[3][:, 512:1024]]
            b_psM = [K.PSh[6], K.PSh[7]]
            psC = [K.PS[2][:, 0:512], K.PS[2][:, 512:1024]]
            b_psC = [K.PSh[4], K.PSh[5]]
            for i_ in range(2):
                b_psC[i_].w = K.PSb[2].w
                b_psC[i_].r = dict(K.PSb[2].r)
            mi = 0
            cstate = [0]

            def conv_stage(n, wh, c, pa):
                xb, bx = xbuf[wh], b_xb[wh]
                m2 = cstate[0] % 2
                cstate[0] += 1
                for t3 in range(3):
                    mm(S, psC[m2], diagw[:, c, t3, :], xb[:, t3:t3 + 512], t3 == 0, t3 == 2, [b_dg, bx], [b_psC[m2]])
                if wh == 0:
                    S.op("act", lambda e: e.activation(out=cva[pa][:], in_=psC[m2], func=AF.Gelu_apprx_tanh,
                                                       bias=cw[:, 66 + c:67 + c], scale=1.0),
                         [b_psC[m2], b_cw], [b_cva[pa]])
                else:
                    stt(S, "dve", actT[:, n, :], psC[m2], cw[:, 66 + c:67 + c], cva[pa][:], ALU.add, ALU.mult,
                        [b_psC[m2], b_cw, b_cva[pa]], [b_actT])
                cp(S, "dve", halo[:, c, :], xb[:, 512:514], [bx], [b_halo])

            for tb in range(NTB):
                blk = slice(tb * 512, (tb + 1) * 512)
                hb = K.HTb[tb]
                pending = None
                for n in range(11):
                    pa = n % 2
                    for wh in range(2):
                        c = n + 11 * wh
                        m = mi % 2
                        mi += 1
                        xb, bx = xbuf[wh], b_xb[wh]
                        for k in range(8):
                            mm(S, psM[m], wup[:, k, c * 128:(c + 1) * 128], K.HT[:, k, blk], k == 0, k == 7, [bw, hb], [b_psM[m]])
                        if pending is not None and pending[1] == wh:
                            conv_stage(*pending)
                            pending = None
                        cp(S, "dve", xb[:, 0:2], halo[:, c, :], [b_halo], [bx])
                        cp(S, "act", xb[:, 2:514], psM[m], [b_psM[m]], [bx])
                        if pending is not None:
                            conv_stage(*pending)
                        pending = (n, wh, c, pa)
                conv_stage(*pending)
                if RET_CUT == 102:
                    S.flush()
                    return
                for t4 in range(4):
                    t = tb * 4 + t4
                    i = t % 2
                    rows = slice(t * 128, (t + 1) * 128)
                    src = K.hres if p == 0 else K.fpart
                    srcb = K.b_hres[tb] if p == 0 else K.b_fpart[tb]
                    S.dma("sp", hprev[i][:], src[rows, :], [srcb], [b_hp[i]])
                    pso, bpso = K.PS[i], K.PSb[i]
                    for half in range(2):
                        for f in range(11):
                            mm(S, pso[:, half * 512:(half + 1) * 512], actT[:, f, t4 * 128:(t4 + 1) * 128], wdn[:, f, half * 512:(half + 1) * 512],
                               f == 0, f == 10, [bw_d, b_actT], [bpso])
                    for half in range(2):
                        cs = slice(half * 512, (half + 1) * 512)
                        stt(S, "dve", hprev[i][:, cs], hprev[i][:, cs], ALPHA if p == 0 else 1.0, pso[:, cs], ALU.mult, ALU.add,
                            [b_hp[i], bpso], [b_hp[i]])
                    if RET_CUT == 103:
                        S.flush()
                        return
                    if p == 0:
                        S.dma("sp", K.fpart[rows, :], hprev[i][:], [b_hp[i]], [K.b_fpart[tb]])
                    elif last:
                        ln_apply(K, S, sc, hprev[i][:], b_hp[i], t, None, None, pso[:, :], bpso, final_out=K.y, final_buf=K.b_y)
                    else:
                        ln_apply(K, S, sc, hprev[i][:], b_hp[i], t, K.hres, K.b_hres[tb], pso[:, :], bpso)
            S.flush()

IN_SPECS = [
    ("x", [T, D]), ("ln_in_g", [D]), ("ln_in_b", [D]), ("w_in", [NL, D, 8192]), ("rel_bias", [NL, 8, 257]),
    ("w_proj_a", [NL, 512, D]), ("w_proj_b", [NL, 1024, D]), ("w_proj_c", [NL, 512, D]),
    ("lam_re", [NL, 32, 64]), ("lam_im", [NL, 32, 64]), ("log_step", [NL, 32]),
    ("b_re", [NL, 32, 64, 16]), ("b_im", [NL, 32, 64, 16]), ("c_re", [NL, 32, 16, 64]), ("c_im", [NL, 32, 16, 64]),
    ("d_skip", [NL, 512]), ("w_glu", [NL, 512, 512]), ("w_o", [NL, D, D]), ("ln1_g", [NL, D]), ("ln1_b", [NL, D]),
    ("w_up", [NL, D, 5632]), ("conv_w", [NL, 3, 5632]), ("conv_b", [NL, 5632]), ("w_down", [NL, 2816, D]),
    ("ln2_g", [NL, D]), ("ln2_b", [NL, D]),
]


def build_program():
    nc = bass.Bass("TRN2", target_bir_lowering=False)
    K = Ctx()
    K.nc = nc
    K.uid = 0
    K.inp = {n: nc.dram_tensor(n, list(s), F32, kind="ExternalInput").ap() for n, s in IN_SPECS}
    K.y = nc.dram_tensor("y", [T, D], F32, kind="ExternalOutput").ap()
    dbg = "ExternalOutput" if DEBUG else "Internal"
    K.hres = nc.dram_tensor("hres", [T, D], F32, kind=dbg).ap()
    K.merged = nc.dram_tensor("merged", [128, 8, T], BF16, kind=dbg).ap()
    K.Fd = nc.dram_tensor("Fd", [8, 768], F32, kind="Internal").ap()
    K.Wd = nc.dram_tensor("Wd", [16, 128, 2, 8, 128], BF16, kind="Internal").ap()
    K.Vd = nc.dram_tensor("Vd", [16, 128, 2, 8, 128], BF16, kind="Internal").ap()
    K.b_Wd = Buf()
    K.b_Vd = Buf()
    if DEBUG:
        K.dbg2 = nc.dram_tensor("dbg2", [128, 4, T], BF16, kind="ExternalOutput").ap()
        K.dbg1 = nc.dram_tensor("dbg1", [128, 8 * 5 * 128], F32, kind="ExternalOutput").ap()
    K.fpart = nc.dram_tensor("fpart", [T, D], F32, kind="Internal").ap()
    K.b_fpart = [Buf() for _ in range(NTB)]
    K.b_hres = [Buf() for _ in range(NTB)]
    K.b_merged = [Buf() for _ in range(NTB)]
    K.b_y = Buf()
    with contextlib.ExitStack() as gst:
        S = Sched(nc, gst)
        K.HT = gst.enter_context(nc.sbuf_tensor("HT", [128, 8, T], BF16))
        K.HTb = [Buf() for _ in range(NTB)]
        K.identf = gst.enter_context(nc.sbuf_tensor("identf", [128, 128], F32))
        K.Jf = gst.enter_context(nc.sbuf_tensor("Jf", [128, 128], F32))
        io = gst.enter_context(nc.sbuf_tensor("iota_i", [128, 128], I32))
        K.PS = [gst.enter_context(nc.psum_tensor("PS%d" % i, [128, 1024], F32)) for i in range(4)]
        K.PSb = [Buf(excl=True) for _ in range(4)]
        K.PSh = [Buf(excl=True) for _ in range(8)]
        bc = Buf()
        S.op("pool", lambda e: e.iota(io[:], pattern=[[1, 128]], base=0, channel_multiplier=-1), (), [bc])
        ts1(S, "dve", K.identf[:], io[:], 0.0, ALU.is_equal, [bc], [bc])
        S.op("pool", lambda e: e.iota(io[:], pattern=[[1, 128]], base=0, channel_multiplier=1), [bc], [bc])
        ts1(S, "dve", K.Jf[:], io[:], 127.0, ALU.is_equal, [bc], [bc])
        S.flush()
        stage_entry(K, S)
        if STOP_AFTER == "entry":
            return nc
        for l in LAYERS:
            if STOP_AFTER == "ffnonly":
                stage_ffn(K, S, l, False)
                return nc
            if STOP_AFTER not in ("s5only", "rettab"):
                stage_attn(K, S, l)
            if STOP_AFTER in ("attn", "bias"):
                return nc
            if STOP_AFTER != "s5only":
                stage_ret(K, S, l)
            if STOP_AFTER in ("ret", "rettab"):
                return nc
            with contextlib.ExitStack() as sty:
                Tt = sbt(K, sty, "Tt", [128, 4, 8, 128], BF16)
                ARt = sbt(K, sty, "ARt", [128, NPW, 16], F32)
                AIt = sbt(K, sty, "AIt", [128, NPW, 16], F32)
                AInt = sbt(K, sty, "AInt", [128, NPW, 16], F32)
                b_T = Buf()
                s5_setup(K, S, l, Tt, ARt, AIt, AInt, b_T)
                yT = sbt(K, sty, "yT", [128, 4, T], BF16)
                b_yT = Buf()
                stage_s5(K, S, l, yT, b_yT, Tt, ARt, AIt, AInt, b_T)
                if STOP_AFTER in ("s5", "s5only"):
                    S.dma("sp", K.dbg2[:, :, :], yT[:], [b_yT], [Buf()])
                    S.flush()
                    return nc
                stage_merge(K, S, l, yT, b_yT)
            if STOP_AFTER == "merge":
                return nc
            stage_ffn(K, S, l, l == NL - 1)
            if STOP_AFTER == "ffn":
                return nc
    return nc


_PROG = None


def kernel(**inputs):
    global _PROG
    if _PROG is None:
        _PROG = build_program()
    nc = _PROG
    x = np.ascontiguousarray(np.asarray(inputs["x"], dtype=np.float32))
    shared = {n: np.ascontiguousarray(np.asarray(inputs[n], dtype=np.float32)) for n, _ in IN_SPECS if n != "x"}
    in_maps = []
    for c in range(8):
        m = dict(shared)
        m["x"] = x[c]
        in_maps.append(m)
    res = run_bass_kernel_spmd(nc, in_maps, core_ids=list(range(8)))
    return np.stack([r["y"] for r in res.results], axis=0).astype(np.float32)
```

```python
import contextlib
import math
import numpy as np
import concourse.bass as bass
import concourse.mybir as mybir
from concourse.bass_utils import run_bass_kernel_spmd

F32 = mybir.dt.float32
BF16 = mybir.dt.bfloat16
I32 = mybir.dt.int32
AF = mybir.ActivationFunctionType
ALU = mybir.AluOpType
AX = mybir.AxisListType

SAME_ENGINE_SYNC = True
N_DMA_SEMS = 12
N_SW_SEMS = 4
DEBUG = False
STOP_AFTER = None
LIMIT_TB = None
RET_CUT = None
LAYERS = (0, 1)
MM_LAZY_INC = True

T = 4096
D = 1024
NTT = 32
NTB = 8
NL = 2
ALPHA = (2.0 * NL) ** 0.25
LN_EPS = 1e-5
TWO_PI = 2.0 * math.pi


class Buf:
    __slots__ = ("name", "w", "r", "excl")

    def __init__(self, name="", excl=False):
        self.name = name
        self.w = None
        self.r = {}
        self.excl = excl


class Sched:
    ENG = ("pe", "act", "dve", "pool", "sp")

    def __init__(self, nc, stack):
        self.nc = nc
        self.streams = {e: [] for e in self.ENG}
        self.count = {e: 0 for e in self.ENG}
        self.seen = {e: {} for e in self.ENG}
        self.csem = {e: stack.enter_context(nc.semaphore("c_" + e)) for e in self.ENG}
        self.dsem = [stack.enter_context(nc.semaphore("d%d" % i)) for i in range(N_DMA_SEMS)]
        self.duse = [0] * N_DMA_SEMS
        self.dnext = 0
        self.dnext_sw = 0
        self.ninst = 0

    def _sem_of(self, key):
        return self.csem[key[1]] if key[0] == "e" else self.dsem[key[1]]

    def _add_wait(self, eng, k, v):
        seen = self.seen[eng]
        if seen.get(k, 0) >= v:
            return
        seen[k] = v
        sem = self._sem_of(k)
        self.streams[eng].append(lambda e: e.wait_ge(sem, v))
        self.ninst += 1

    def _waits(self, eng, reads, writes):
        deps = {}
        for b in reads:
            if b.w is not None:
                k, v = b.w
                if deps.get(k, 0) < v:
                    deps[k] = v
            if b.excl:
                for k, v in b.r.items():
                    if k != ("e", eng) and deps.get(k, 0) < v:
                        deps[k] = v
        for b in writes:
            if b.w is not None:
                k, v = b.w
                if deps.get(k, 0) < v:
                    deps[k] = v
            for k, v in b.r.items():
                if deps.get(k, 0) < v:
                    deps[k] = v
        for k, v in deps.items():
            if k == ("e", eng) and (eng == "pe" or not SAME_ENGINE_SYNC):
                continue
            self._add_wait(eng, k, v)

    def _commit(self, tok, reads, writes):
        k, v = tok
        for b in reads:
            if b.r.get(k, 0) < v:
                b.r[k] = v
        for b in writes:
            b.w = tok
            b.r = {}

    def op(self, eng, fn, reads=(), writes=(), inc=True):
        self._waits(eng, reads, writes)
        if not inc:
            self.streams[eng].append(lambda e: fn(e))
            self.ninst += 1
            tok = (("e", eng), self.count[eng] + 1)
            self._commit(tok, reads, writes)
            return tok
        self.count[eng] += 1
        n = self.count[eng]
        sem = self.csem[eng]
        self.streams[eng].append(lambda e: fn(e).then_inc(sem, 1))
        self.ninst += 1
        tok = (("e", eng), n)
        self._commit(tok, reads, writes)
        return tok

    def dma(self, eng, out, in_, reads=(), writes=(), **kw):
        if eng == "pool":
            s = N_DMA_SEMS - N_SW_SEMS + self.dnext_sw
            self.dnext_sw = (self.dnext_sw + 1) % N_SW_SEMS
        else:
            s = self.dnext
            self.dnext = (self.dnext + 1) % (N_DMA_SEMS - N_SW_SEMS)
        k = ("d", s)
        prev = 16 * self.duse[s]
        if prev > 0:
            self._add_wait(eng, k, prev)
        self._waits(eng, reads, writes)
        self.duse[s] += 1
        sem = self.dsem[s]
        self.streams[eng].append(lambda e: e.dma_start(out=out, in_=in_, **kw).then_inc(sem, 16))
        self.ninst += 1
        tok = (k, 16 * self.duse[s])
        self._commit(tok, reads, writes)
        return tok

    def flush(self):
        toks = [(("d", s), 16 * self.duse[s]) for s in range(N_DMA_SEMS) if self.duse[s]]
        toks += [(("e", e), self.count[e]) for e in self.ENG if self.count[e]]
        for e in self.ENG:
            for k, v in toks:
                if k == ("e", e):
                    continue
                self._add_wait(e, k, v)
        nc = self.nc
        streams = self.streams
        with nc.Block() as block:
            @block.tensor
            def _(eng):
                for f in streams["pe"]:
                    f(eng)

            @block.scalar
            def _(eng):
                for f in streams["act"]:
                    f(eng)

            @block.vector
            def _(eng):
                for f in streams["dve"]:
                    f(eng)

            @block.gpsimd
            def _(eng):
                for f in streams["pool"]:
                    f(eng)

            @block.sync
            def _(eng):
                for f in streams["sp"]:
                    f(eng)
        self.streams = {e: [] for e in self.ENG}


def mm(S, out, lhsT, rhs, start, stop, rd, wr):
    S.op("pe", lambda e: e.matmul(out, lhsT, rhs, start=start, stop=stop), rd, wr, inc=(stop or not MM_LAZY_INC))


def trp(S, out, in_, ident, rd, wr):
    S.op("pe", lambda e: e.transpose(out, in_, ident), rd, wr)


def act(S, out, in_, func, rd, wr, bias=0.0, scale=1.0):
    S.op("act", lambda e: e.activation(out=out, in_=in_, func=func, bias=bias, scale=scale), rd, wr)


def tt(S, eng, out, in0, in1, op, rd, wr):
    S.op(eng, lambda e: e.tensor_tensor(out=out, in0=in0, in1=in1, op=op), rd, wr)


def ts(S, eng, out, in0, s1, s2, op0, op1, rd, wr):
    S.op(eng, lambda e: e.tensor_scalar(out=out, in0=in0, scalar1=s1, scalar2=s2, op0=op0, op1=op1), rd, wr)


def ts1(S, eng, out, in0, s1, op0, rd, wr):
    S.op(eng, lambda e: e.tensor_scalar(out=out, in0=in0, scalar1=s1, scalar2=None, op0=op0), rd, wr)


def stt(S, eng, out, in0, scalar, in1, op0, op1, rd, wr):
    S.op(eng, lambda e: e.scalar_tensor_tensor(out=out, in0=in0, scalar=scalar, in1=in1, op0=op0, op1=op1), rd, wr)


def cp(S, eng, out, in_, rd, wr):
    if eng == "act":
        S.op("act", lambda e: e.copy(out=out, in_=in_), rd, wr)
    else:
        S.op(eng, lambda e: e.tensor_copy(out=out, in_=in_), rd, wr)


def mset(S, eng, ap, val, wr):
    S.op(eng, lambda e: e.memset(ap, val), (), wr)


def recip(S, out, in_, rd, wr):
    S.op("dve", lambda e: e.reciprocal(out=out, in_=in_), rd, wr)


class Ctx:
    pass


def sbt(K, st, name, shape, dtype):
    K.uid += 1
    return st.enter_context(K.nc.sbuf_tensor("%s_%d" % (name, K.uid), list(shape), dtype))


def sincos(K, S, st, ang, shape, b, want_cos):
    yy = sbt(K, st, "sc_y", shape, F32)
    ki = sbt(K, st, "sc_k", shape, I32)
    kf = sbt(K, st, "sc_kf", shape, F32)
    out = sbt(K, st, "sc_o", shape, F32)
    off = (math.pi / 2.0 if want_cos else 0.0)
    ts1(S, "dve", yy[:], ang, off, ALU.add, [b], [b])
    ts1(S, "dve", ki[:], yy[:], 1.0 / TWO_PI, ALU.mult, [b], [b])
    cp(S, "dve", kf[:], ki[:], [b], [b])
    stt(S, "dve", yy[:], kf[:], -TWO_PI, yy[:], ALU.mult, ALU.add, [b], [b])
    ts(S, "dve", kf[:], yy[:], math.pi, -TWO_PI, ALU.is_gt, ALU.mult, [b], [b])
    tt(S, "dve", yy[:], yy[:], kf[:], ALU.add, [b], [b])
    ts(S, "dve", kf[:], yy[:], -math.pi, TWO_PI, ALU.is_lt, ALU.mult, [b], [b])
    tt(S, "dve", yy[:], yy[:], kf[:], ALU.add, [b], [b])
    ts(S, "dve", yy[:], yy[:], math.pi, -math.pi, ALU.min, ALU.max, [b], [b])
    act(S, out[:], yy[:], AF.Sin, [b], [b])
    return out


def ln_scratch(K, st):
    sc = Ctx()
    sc.i = 0
    sc.st = [sbt(K, st, "ln_st", [128, 12], F32) for _ in range(2)]
    sc.mv = [sbt(K, st, "ln_mv", [128, 2], F32) for _ in range(2)]
    sc.sd = [sbt(K, st, "ln_sd", [128, 1], F32) for _ in range(2)]
    sc.hn = [sbt(K, st, "ln_hn", [128, 1024], F32) for _ in range(2)]
    sc.b = [Buf() for _ in range(2)]
    sc.bh = [Buf() for _ in range(2)]
    sc.gt = sbt(K, st, "ln_g", [128, 1024], F32)
    sc.bt = sbt(K, st, "ln_b", [128, 1024], F32)
    sc.bgb = Buf()
    return sc


def ln_load_gb(K, S, sc, g_ap, b_ap):
    S.dma("sp", sc.gt[:], g_ap.partition_broadcast(128), (), [sc.bgb])
    S.dma("sp", sc.bt[:], b_ap.partition_broadcast(128), (), [sc.bgb])


def ln_apply(K, S, sc, src, src_buf, tt_i, dst_dram, dst_buf, ps, ps_buf, final_out=None, final_buf=None, defer=False):
    i = sc.i
    sc.i = (i + 1) % len(sc.hn)
    stt_, mv, sd, hn, b, bh = sc.st[i], sc.mv[i], sc.sd[i], sc.hn[i], sc.b[i], sc.bh[i]
    S.op("dve", lambda e: e.bn_stats(out=stt_[:, 0:6], in_=src[:, 0:512]), [src_buf], [b])
    S.op("dve", lambda e: e.bn_stats(out=stt_[:, 6:12], in_=src[:, 512:1024]), [src_buf], [b])
    S.op("dve", lambda e: e.bn_aggr(out=mv[:, 0:2], in_=stt_[:, 0:12]), [b], [b])
    ts1(S, "dve", sd[:], mv[:, 1:2], LN_EPS, ALU.add, [b], [b])
    act(S, sd[:], sd[:], AF.Sqrt, [b], [b])
    recip(S, sd[:], sd[:], [b], [b])
    stt(S, "dve", mv[:, 1:2], mv[:, 0:1], -1.0, sd[:, 0:1], ALU.mult, ALU.mult, [b], [b])
    S.op("act", lambda e: e.activation(out=hn[:], in_=src, func=AF.Identity, bias=mv[:, 1:2], scale=sd[:, 0:1]), [src_buf, b], [bh])
    tt(S, "dve", hn[:], hn[:], sc.gt[:], ALU.mult, [bh, sc.bgb], [bh])
    tt(S, "dve", hn[:], hn[:], sc.bt[:], ALU.add, [bh, sc.bgb], [bh])
    rows = slice(tt_i * 128, (tt_i + 1) * 128)
    if dst_dram is not None:
        S.dma("sp", dst_dram[rows, :], hn[:], [bh], [dst_buf])
    if final_out is not None:
        S.dma("sp", final_out[rows, :], hn[:], [bh], [final_buf])
        return
    def part2():
        for k in range(8):
            trp(S, ps[:, k * 128:(k + 1) * 128], hn[:, k * 128:(k + 1) * 128], K.identf[:], [bh], [ps_buf])
        cp(S, "act", K.HT[:, :, rows], ps.rearrange("p (k t) -> p k t", k=8), [ps_buf], [K.HTb[tt_i // 4]])
    if defer:
        return part2
    part2()


def stage_entry(K, S):
    with contextlib.ExitStack() as st:
        sc = ln_scratch(K, st)
        ln_load_gb(K, S, sc, K.inp["ln_in_g"], K.inp["ln_in_b"])
        xin = [sbt(K, st, "xin", [128, 1024], F32) for _ in range(2)]
        bx = [Buf() for _ in range(2)]
        for t in range(NTT):
            i = t % 2
            S.dma("sp", xin[i][:], K.inp["x"][t * 128:(t + 1) * 128, :], (), [bx[i]])
            ln_apply(K, S, sc, xin[i][:], bx[i], t, K.hres, K.b_hres[t // 4], K.PS[t % 2][:, :], K.PSb[t % 2])
        S.flush()


def build_bias(K, S, st, l, biasT, b_bias):
    Fd = K.Fd
    bF = Buf()
    rb = K.inp["rel_bias"][l]
    with contextlib.ExitStack() as st2:
        fsb = sbt(K, st2, "fsb", [8, 768], F32)
        bfs = Buf()
        S.dma("sp", fsb[:, 0:256], rb[:, 1:257], (), [bfs])
        cp(S, "dve", fsb[:, 256:768], fsb[:, 255:256].broadcast_to([8, 512]), [bfs], [bfs])
        S.dma("sp", Fd[:, :], fsb[:], [bfs], [bF])
        Tk = sbt(K, st2, "Tk", [128, 8, 5, 128], F32)
        bT = Buf()
        for h in range(8):
            for jr in range(5):
                src = bass.AP(tensor=Fd.tensor, offset=h * 768 + 512 - 128 * jr, ap=[[1, 128], [1, 128]])
                S.dma("sp", Tk[:, h, jr, :], src, [bF], [bT])
        for h in range(8):
            p0 = K.PS[3][:, 0:512]
            p1 = K.PS[3][:, 512:640]
            mm(S, p0, K.Jf[:], Tk[:, h, 0:4, :].rearrange("p j q -> p (j q)"), True, True, [bT], [K.PSb[3]])
            mm(S, p1, K.Jf[:], Tk[:, h, 4, :], True, True, [bT], [K.PSb[3]])
            cp(S, "act", biasT[:, h, :, :].rearrange("p j q -> p (j q)"), K.PS[3][:, 0:640], [K.PSb[3]], [b_bias])
            mset(S, "pool", biasT[64:128, h, 4, 0:64], -30000.0, [b_bias])
            mset(S, "pool", biasT[0:64, h, 0, 64:128], -30000.0, [b_bias])
        S.flush()


def load_w(K, S, dst, src, buf):
    S.dma("pool", dst, src, (), [buf])


def stage_attn(K, S, l):
    with contextlib.ExitStack() as st:
        w_in = K.inp["w_in"][l].rearrange("(k p) c -> p k c", p=128)
        wqkv = sbt(K, st, "wqkv", [128, 8, 1536], BF16)
        wg0 = sbt(K, st, "wg0", [128, 8, 1024], BF16)
        wpa = sbt(K, st, "wpa", [128, 4, 1024], BF16)
        bw = Buf()
        for k in range(8):
            load_w(K, S, wqkv[:, k, :], w_in[:, k, 0:1536], bw)
        bw_g = Buf()
        bw_p = Buf()
        for k in range(8):
            load_w(K, S, wg0[:, k, :], w_in[:, k, 5120:6144], bw_g)
        for k_ in range(4):
            load_w(K, S, wpa[:, k_, :], K.inp["w_proj_a"][l].rearrange("(k p) c -> p k c", p=128)[:, k_, :], bw_p)
        biasT = sbt(K, st, "biasT", [128, 8, 5, 128], F32)
        b_bias = Buf()
        build_bias(K, S, st, l, biasT, b_bias)
        if STOP_AFTER == "bias":
            S.dma("sp", K.dbg1[:, :], biasT[:].rearrange("p h j q -> p (h j q)"), [b_bias], [Buf()])
            S.flush()
            return
        qT = [sbt(K, st, "qT", [128, 4, 512], BF16) for _ in range(2)]
        b_qT = [Buf() for _ in range(2)]
        kring = sbt(K, st, "kring", [128, 4, 1024], BF16)
        b_kr = [Buf() for _ in range(2)]
        vring = sbt(K, st, "vring", [128, 8, 8, 65], BF16)
        b_vr = [Buf() for _ in range(8)]
        mset(S, "pool", vring[:], 1.0, b_vr)
        sbf = [sbt(K, st, "sbf", [128, 640], F32) for _ in range(2)]
        pT = [sbt(K, st, "pT", [128, 640], BF16) for _ in range(2)]
        b_sbf = [Buf() for _ in range(2)]
        b_pT = [Buf() for _ in range(2)]
        att = [sbt(K, st, "att", [128, 512], F32) for _ in range(2)]
        rs = [sbt(K, st, "rs", [128, 8], F32) for _ in range(2)]
        b_att = [Buf() for _ in range(2)]
        attT = [sbt(K, st, "attT", [128, 4, 512], BF16) for _ in range(2)]
        b_attT = [Buf() for _ in range(2)]
        gsb = [sbt(K, st, "gsb", [128, 512], F32) for _ in range(2)]
        b_gsb = [Buf() for _ in range(2)]
        mrg1 = sbt(K, st, "mrg", [128, 8, 512], BF16)
        mrg = [mrg1, mrg1]
        b_mrg1 = Buf()
        b_mrg = [b_mrg1, b_mrg1]
        psS = [K.PS[0], K.PS[1]]
        b_psS = [K.PSb[0], K.PSb[1]]
        psO = [K.PS[2][:, 0:512], K.PS[2][:, 512:1024]]
        b_psO = [K.PSh[4], K.PSh[5]]
        psM = [K.PS[3][:, 0:512], K.PS[3][:, 512:1024]]
        b_psM = [K.PSh[6], K.PSh[7]]
        K.PSh[6].w = K.PSb[3].w
        K.PSh[7].w = K.PSb[3].w
        K.PSh[6].r = dict(K.PSb[3].r)
        K.PSh[7].r = dict(K.PSb[3].r)
        mi = 0
        for tb in range(NTB if LIMIT_TB is None else LIMIT_TB):
            blk = slice(tb * 512, (tb + 1) * 512)
            qs = tb % 2
            hb = K.HTb[tb]
            for ot in range(8):
                m = mi % 2
                mi += 1
                for k in range(8):
                    mm(S, psM[m], wqkv[:, k, ot * 128:(ot + 1) * 128], K.HT[:, k, blk], k == 0, k == 7, [bw, hb], [b_psM[m]])
                if ot < 4:
                    act(S, qT[qs][:, ot, :], psM[m], AF.Copy, [b_psM[m]], [b_qT[qs]], scale=0.125)
                else:
                    cp(S, "dve", kring[:, ot - 4, qs * 512:(qs + 1) * 512], psM[m], [b_psM[m]], [b_kr[qs]])
            for t4 in range(4):
                t = tb * 4 + t4
                slot = t % 8
                m = mi % 2
                mi += 1
                for k in range(8):
                    mm(S, psM[m], K.HT[:, k, t * 128:(t + 1) * 128], wqkv[:, k, 1024:1536], k == 0, k == 7, [bw, hb], [b_psM[m]])
                cp(S, "act", vring[:, slot, :, 0:64], psM[m].rearrange("p (h d) -> p h d", h=8), [b_psM[m]], [b_vr[slot]])
            for s4 in range(4):
                sc_ = tb * 4 + s4
                nk = min(sc_, 4) + 1
                jr0 = 5 - nk
                ai = sc_ % 2
                def emit_scores(h):
                    hp, tq = h % 2, h // 2
                    x = h % 2
                    prt = slice(hp * 64, (hp + 1) * 64)
                    for j in range(nk):
                        kt = sc_ - (nk - 1) + j
                        slot = kt % 8
                        mm(S, psS[x][:, j * 128:(j + 1) * 128], kring[prt, tq, slot * 128:(slot + 1) * 128],
                           qT[qs][prt, tq, s4 * 128:(s4 + 1) * 128], True, True, [b_kr[slot // 4], b_qT[qs]], [b_psS[x]])

                def emit_rest(h):
                    x = h % 2
                    n = nk * 128
                    tt(S, "dve", sbf[x][:, 0:n], psS[x][:, 0:n],
                       biasT[:, h, jr0:5, :].rearrange("p j q -> p (j q)"), ALU.add, [b_psS[x], b_bias], [b_sbf[x]])
                    act(S, pT[x][:, 0:n], sbf[x][:, 0:n], AF.Exp, [b_sbf[x]], [b_pT[x]])
                    for j in range(nk):
                        kt = sc_ - (nk - 1) + j
                        slot = kt % 8
                        mm(S, psO[h // 4][:, (h % 4) * 65:(h % 4) * 65 + 65], pT[x][:, j * 128:(j + 1) * 128],
                           vring[:, slot, h, :], j == 0, j == nk - 1, [b_pT[x], b_vr[slot]], [b_psO[h // 4]])

                emit_scores(0)
                for h in range(8):
                    if h + 1 < 8:
                        emit_scores(h + 1)
                    emit_rest(h)
                for half in range(2):
                    pv = psO[half][:, 0:260].rearrange("p (h e) -> p h e", h=4)
                    recip(S, rs[ai][:, half * 4:(half + 1) * 4], pv[:, :, 64], [b_psO[half]], [b_att[ai]])
                    tt(S, "dve", att[ai][:, half * 256:(half + 1) * 256].rearrange("p (h d) -> p h d", h=4), pv[:, :, 0:64],
                       rs[ai][:, half * 4:(half + 1) * 4].unsqueeze(2).broadcast_to([128, 4, 64]), ALU.mult,
                       [b_psO[half], b_att[ai]], [b_att[ai]])
                m = mi % 2
                mi += 1
                for f in range(4):
                    trp(S, psM[m][:, f * 128:(f + 1) * 128], att[ai][:, f * 128:(f + 1) * 128], K.identf[:], [b_att[ai]], [b_psM[m]])
                cp(S, "act", attT[qs][:, :, s4 * 128:(s4 + 1) * 128], psM[m].rearrange("p (f t) -> p f t", f=4), [b_psM[m]], [b_attT[qs]])
            for ot in range(8):
                g = ot % 2
                m = mi % 2
                mi += 1
                for k in range(8):
                    mm(S, psM[m], wg0[:, k, ot * 128:(ot + 1) * 128], K.HT[:, k, blk], k == 0, k == 7, [bw_g, hb], [b_psM[m]])
                act(S, gsb[g][:], psM[m], AF.Sigmoid, [b_psM[m]], [b_gsb[g]])
                m = mi % 2
                mi += 1
                for f in range(4):
                    mm(S, psM[m], wpa[:, f, ot * 128:(ot + 1) * 128], attT[qs][:, f, :], f == 0, f == 3, [bw_p, b_attT[qs]], [b_psM[m]])
                tt(S, "dve", mrg[qs][:, ot, :], psM[m], gsb[g][:], ALU.mult, [b_psM[m], b_gsb[g]], [b_mrg[qs]])
            for o2 in range(4):
                S.dma("sp", K.merged[:, 2 * o2:2 * o2 + 2, blk], mrg[qs][:, 2 * o2:2 * o2 + 2, :], [b_mrg[qs]], [K.b_merged[tb]])
        S.flush()


def stage_ret(K, S, l):
    with contextlib.ExitStack() as st:
        w_in = K.inp["w_in"][l].rearrange("(k p) c -> p k c", p=128)
        wqk = sbt(K, st, "wqk", [128, 8, 1024], BF16)
        wv = sbt(K, st, "wv", [128, 8, 1024], BF16)
        wgr = sbt(K, st, "wgr", [128, 8, 1024], BF16)
        wg1 = sbt(K, st, "wg1", [128, 8, 1024], BF16)
        wpb = sbt(K, st, "wpb", [128, 8, 1024], BF16)
        bw = Buf()
        for k in range(8):
            load_w(K, S, wqk[:, k, :], w_in[:, k, 1536:2560], bw)
        bw_v, bw_gr, bw_g1, bw_pb = Buf(), Buf(), Buf(), Buf()
        for k in range(8):
            load_w(K, S, wv[:, k, :], w_in[:, k, 2560:3584], bw_v)
        for k in range(8):
            load_w(K, S, wgr[:, k, :], w_in[:, k, 3584:4608], bw_gr)
        for k in range(8):
            load_w(K, S, wg1[:, k, :], w_in[:, k, 6144:7168], bw_g1)
        for k_ in range(8):
            load_w(K, S, wpb[:, k_, :], K.inp["w_proj_b"][l].rearrange("(k p) c -> p k c", p=128)[:, k_, :], bw_pb)
        cosT = sbt(K, st, "cosT", [128, 32, 32], F32)
        sinT = sbt(K, st, "sinT", [128, 32, 32], F32)
        DT = sbt(K, st, "DT", [128, 8, 128], F32)
        qdecT = sbt(K, st, "qdecT", [128, 4, 128], F32)
        kdec = sbt(K, st, "kdec", [128, 8], F32)
        g128 = sbt(K, st, "g128", [128, 4], F32)
        btab = Buf()
        logg = [math.log(1.0 - 2.0 ** (-5.0 - h)) for h in range(8)]
        with contextlib.ExitStack() as st2:
            ii = sbt(K, st2, "ii", [128, 128], I32)
            ff = sbt(K, st2, "ff", [128, 128], F32)
            posf = sbt(K, st2, "posf", [128, 32], F32)
            invf = sbt(K, st2, "invf", [128, 32], F32)
            ang = sbt(K, st2, "ang", [128, 32, 32], F32)
            bt2 = Buf()
            S.op("pool", lambda e: e.iota(ii[:, 0:32], pattern=[[128, 32]], base=0, channel_multiplier=1), (), [bt2])
            cp(S, "dve", posf[:], ii[:, 0:32], [bt2], [bt2])
            for i in range(32):
                mset(S, "pool", invf[:, i:i + 1], float(np.float32(10000.0 ** (-2.0 * i / 64.0))), [bt2])
            tt(S, "dve", ang[:], posf[:].unsqueeze(2).broadcast_to([128, 32, 32]),
               invf[:].unsqueeze(1).broadcast_to([128, 32, 32]), ALU.mult, [bt2], [bt2])
            c_ = sincos(K, S, st2, ang[:], [128, 32, 32], bt2, True)
            cp(S, "dve", cosT[:], c_[:], [bt2], [btab])
            s_ = sincos(K, S, st2, ang[:], [128, 32, 32], bt2, False)
            cp(S, "dve", sinT[:], s_[:], [bt2], [btab])
            S.op("pool", lambda e: e.iota(ii[:], pattern=[[1, 128]], base=0, channel_multiplier=-1), [bt2], [bt2])
            cp(S, "dve", ff[:], ii[:], [bt2], [bt2])
            ts1(S, "dve", ang[:, 0:4, :].rearrange("p a b -> p (a b)"), ff[:], -1.0, ALU.mult, [bt2], [bt2])
            tt(S, "dve", ff[:], ff[:], ang[:, 0:4, :].rearrange("p a b -> p (a b)"), ALU.max, [bt2], [bt2])
            for h in range(8):
                act(S, DT[:, h, :], ff[:], AF.Exp, [bt2], [btab], scale=logg[h])
            mset(S, "pool", DT[64:128, :, 0:64], 0.0, [btab])
            S.op("pool", lambda e: e.iota(ii[:], pattern=[[1, 128]], base=1, channel_multiplier=0), [bt2], [bt2])
            cp(S, "dve", ff[:], ii[:], [bt2], [bt2])
            for h in range(8):
                prt = slice((h % 2) * 64, (h % 2) * 64 + 64)
                act(S, qdecT[prt, h // 2, :], ff[prt, :], AF.Exp, [bt2], [btab], scale=logg[h])
                mset(S, "pool", g128[prt, h // 2:h // 2 + 1], math.exp(128.0 * logg[h]), [btab])
            S.op("pool", lambda e: e.iota(ii[:, 0:1], pattern=[[0, 1]], base=127, channel_multiplier=-1), [bt2], [bt2])
            cp(S, "dve", ff[:, 0:1], ii[:, 0:1], [bt2], [bt2])
            for h in range(8):
                act(S, kdec[:, h:h + 1], ff[:, 0:1], AF.Exp, [bt2], [btab], scale=logg[h])
            S.flush()
        if STOP_AFTER == "rettab":
            return
        xs1 = sbt(K, st, "xs", [128, 512], F32)
        xs = [xs1, xs1]
        xr = [sbt(K, st, "xr", [128, 512], F32) for _ in range(2)]
        tmp = [sbt(K, st, "rtmp", [128, 8, 32], F32) for _ in range(4)]
        b_x1 = Buf()
        b_x = [b_x1, b_x1]
        kcd = sbt(K, st, "kcd", [128, 512], BF16)
        qcT = sbt(K, st, "qcT", [128, 4, 128], BF16)
        qcdT = sbt(K, st, "qcdT", [128, 4, 128], BF16)
        kcT = sbt(K, st, "kcT", [128, 4, 128], BF16)
        vbf = sbt(K, st, "vbf", [128, 1024], BF16)
        b_qk = Buf()
        b_v = Buf()
        sTb = sbt(K, st, "sTb", [128, 1024], BF16)
        b_sT = Buf()
        state = sbt(K, st, "state", [128, 4, 128], F32)
        state_bf = sbt(K, st, "state_bf", [128, 4, 128], BF16)
        b_state = Buf()
        b_sbf = Buf()
        ret = sbt(K, st, "ret", [128, 1024], F32)
        gs = sbt(K, st, "gs", [128, 1024], F32)
        sq = gs
        stat = sbt(K, st, "stat", [128, 6, 8], F32)
        b_ret = Buf()
        b_gs = Buf()
        b_stat = Buf()
        retT1 = sbt(K, st, "retT", [128, 8, 512], BF16)
        retT = [retT1, retT1]
        b_retT1 = Buf()
        b_retT = [b_retT1, b_retT1]
        gsb1 = sbt(K, st, "gsbB", [128, 512], F32)
        gsb = [gsb1, gsb1]
        b_gsb1 = Buf()
        b_gsb = [b_gsb1, b_gsb1]
        mprev = sbt(K, st, "mprev", [128, 8, 512], BF16)
        mrg = mprev
        b_mprev = Buf()
        b_mrg = b_mprev
        psS, b_pS = K.PS[0], K.PSb[0]
        psOo, b_pO = K.PS[1], K.PSb[1]
        psK, b_pK = K.PS[2], K.PSb[2]
        psM = [K.PS[3][:, 0:512], K.PS[3][:, 512:1024]]
        b_psM = [K.PSh[6], K.PSh[7]]
        mset(S, "pool", state[:], 0.0, [b_state])
        mset(S, "pool", state_bf[:], 0.0, [b_sbf])
        mi = 0
        for tb in range(NTB if LIMIT_TB is None else LIMIT_TB):
            blk = slice(tb * 512, (tb + 1) * 512)
            hb = K.HTb[tb]
            rs_ = tb % 2
            for o2 in range(4):
                S.dma("sp", mprev[:, 2 * o2:2 * o2 + 2, :], K.merged[:, 2 * o2:2 * o2 + 2, blk], [K.b_merged[tb]], [b_mprev])
            for s4 in range(4):
                t = tb * 4 + s4
                tok = slice(t * 128, (t + 1) * 128)
                C = cosT[:, t, :].unsqueeze(1).broadcast_to([128, 8, 32])
                Sn = sinT[:, t, :].unsqueeze(1).broadcast_to([128, 8, 32])
                for qk in range(2):
                    m = mi % 2
                    mi += 1
                    for k in range(8):
                        mm(S, psM[m], K.HT[:, k, tok], wqk[:, k, qk * 512:(qk + 1) * 512], k == 0, k == 7, [bw, hb], [b_psM[m]])
                    cp(S, "act", xs[qk][:], psM[m], [b_psM[m]], [b_x[qk]])
                    x3 = xs[qk][:].rearrange("p (h d) -> p h d", h=8)
                    o3 = xr[qk][:].rearrange("p (h d) -> p h d", h=8)
                    x1, x2 = x3[:, :, 0:32], x3[:, :, 32:64]
                    tt(S, "dve", tmp[0][:], x1, C, ALU.mult, [b_x[qk], btab], [b_x[qk]])
                    tt(S, "dve", tmp[1][:], x2, Sn, ALU.mult, [b_x[qk], btab], [b_x[qk]])
                    tt(S, "dve", o3[:, :, 0:32], tmp[0][:], tmp[1][:], ALU.subtract, [b_x[qk]], [b_x[qk]])
                    tt(S, "dve", tmp[2][:], x1, Sn, ALU.mult, [b_x[qk], btab], [b_x[qk]])
                    tt(S, "dve", tmp[3][:], x2, C, ALU.mult, [b_x[qk], btab], [b_x[qk]])
                    tt(S, "dve", o3[:, :, 32:64], tmp[2][:], tmp[3][:], ALU.add, [b_x[qk]], [b_x[qk]])
                if RET_CUT == 1:
                    S.flush()
                    return
                tt(S, "dve", kcd[:].rearrange("p (h d) -> p h d", h=8), xr[1][:].rearrange("p (h d) -> p h d", h=8),
                   kdec[:].unsqueeze(2).broadcast_to([128, 8, 64]), ALU.mult, [b_x[1], btab], [b_qk])
                m = mi % 2
                mi += 1
                for f in range(4):
                    trp(S, psM[m][:, f * 128:(f + 1) * 128], xr[0][:, f * 128:(f + 1) * 128], K.identf[:], [b_x[0]], [b_psM[m]])
                act(S, qcT[:].rearrange("p f t -> p (f t)"), psM[m], AF.Copy, [b_psM[m]], [b_qk], scale=0.125)
                stt(S, "dve", qcdT[:].rearrange("p f t -> p (f t)"), psM[m], 0.125, qdecT[:].rearrange("p f t -> p (f t)"),
                    ALU.mult, ALU.mult, [b_psM[m], btab], [b_qk])
                m = mi % 2
                mi += 1
                for f in range(4):
                    trp(S, psM[m][:, f * 128:(f + 1) * 128], xr[1][:, f * 128:(f + 1) * 128], K.identf[:], [b_x[1]], [b_psM[m]])
                cp(S, "act", kcT[:].rearrange("p f t -> p (f t)"), psM[m], [b_psM[m]], [b_qk])
                if RET_CUT == 2:
                    S.flush()
                    return
                for half in range(2):
                    m = mi % 2
                    mi += 1
                    for k in range(8):
                        mm(S, psM[m], K.HT[:, k, tok], wv[:, k, half * 512:(half + 1) * 512], k == 0, k == 7, [bw_v, hb], [b_psM[m]])
                    cp(S, "act", vbf[:, half * 512:(half + 1) * 512], psM[m], [b_psM[m]], [b_v])
                if RET_CUT == 3:
                    S.flush()
                    return
                for h in range(8):
                    prt = slice((h % 2) * 64, (h % 2) * 64 + 64)
                    c0 = (h % 2) * 512 + (h // 2) * 128
                    mm(S, psS[:, c0:c0 + 128], kcT[prt, h // 2, :], qcT[prt, h // 2, :], True, True, [b_qk], [b_pS])
                if RET_CUT == 40:
                    S.flush()
                    return
                for half in range(2):
                    cs = slice(half * 512, (half + 1) * 512)
                    tt(S, "dve", sTb[:, cs].rearrange("p (h l) -> p h l", h=4), psS[:, cs].rearrange("p (h l) -> p h l", h=4),
                       DT[:, half::2, :], ALU.mult, [b_pS, btab], [b_sT])
                if RET_CUT == 4:
                    S.flush()
                    return
                for h in range(8):
                    prt = slice((h % 2) * 64, (h % 2) * 64 + 64)
                    hs = slice(h * 128, (h + 1) * 128)
                    c0 = (h % 2) * 512 + (h // 2) * 128
                    mm(S, psOo[:, hs], sTb[:, c0:c0 + 128], vbf[:, hs], True, False, [b_sT, b_v], [b_pO])
                    mm(S, psOo[:, hs], qcdT[prt, h // 2, :], state_bf[prt, h // 2, :], False, True, [b_qk, b_sbf], [b_pO])
                if RET_CUT == 5:
                    S.flush()
                    return
                for h in range(8):
                    hs = slice(h * 128, (h + 1) * 128)
                    pr = (h // 2) * 128
                    mm(S, psK[:, hs], kcd[:, pr:pr + 128], vbf[:, hs], True, True, [b_qk, b_v], [b_pK])
                for h in range(8):
                    prt = slice((h % 2) * 64, (h % 2) * 64 + 64)
                    hs = slice(h * 128, (h + 1) * 128)
                    stt(S, "dve", state[prt, h // 2, :], state[prt, h // 2, :], g128[prt, h // 2:h // 2 + 1], psK[prt, hs],
                        ALU.mult, ALU.add, [b_state, b_pK, btab], [b_state])
                cp(S, "act", state_bf[:], state[:], [b_state], [b_sbf])
                if RET_CUT == 6:
                    S.flush()
                    return
                o3 = psOo[:, :].rearrange("p (h e) -> p h e", h=8)
                S.op("dve", lambda e, o3=o3: e.tensor_reduce(out=stat[:, 0, :], in_=o3, axis=AX.X, op=ALU.add), [b_pO], [b_stat])
                act(S, sq[:], psOo[:, :], AF.Square, [b_pO], [b_gs])
                S.op("dve", lambda e: e.tensor_reduce(out=stat[:, 1, :], in_=sq[:].rearrange("p (h e) -> p h e", h=8), axis=AX.X, op=ALU.add),
                     [b_gs], [b_stat])
                ts1(S, "dve", stat[:, 2, :], stat[:, 0, :], 1.0 / 128.0, ALU.mult, [b_stat], [b_stat])
                tt(S, "dve", stat[:, 3, :], stat[:, 2, :], stat[:, 2, :], ALU.mult, [b_stat], [b_stat])
                stt(S, "dve", stat[:, 4, :], stat[:, 1, :], 1.0 / 128.0, stat[:, 3, :], ALU.mult, ALU.subtract, [b_stat], [b_stat])
                ts1(S, "dve", stat[:, 4, :], stat[:, 4, :], LN_EPS, ALU.add, [b_stat], [b_stat])
                act(S, stat[:, 4, :], stat[:, 4, :], AF.Sqrt, [b_stat], [b_stat])
                recip(S, stat[:, 5, :], stat[:, 4, :], [b_stat], [b_stat])
                r3 = ret[:].rearrange("p (h e) -> p h e", h=8)
                tt(S, "dve", r3, o3, stat[:, 2, :].unsqueeze(2).broadcast_to([128, 8, 128]), ALU.subtract, [b_pO, b_stat, b_ret], [b_ret])
                tt(S, "dve", r3, r3, stat[:, 5, :].unsqueeze(2).broadcast_to([128, 8, 128]), ALU.mult, [b_ret, b_stat], [b_ret])
                if RET_CUT == 7:
                    S.flush()
                    return
                for half in range(2):
                    m = mi % 2
                    mi += 1
                    for k in range(8):
                        mm(S, psM[m], K.HT[:, k, tok], wgr[:, k, half * 512:(half + 1) * 512], k == 0, k == 7, [bw_gr, hb], [b_psM[m]])
                    act(S, gs[:, half * 512:(half + 1) * 512], psM[m], AF.Silu, [b_psM[m]], [b_gs])
                tt(S, "dve", ret[:], ret[:], gs[:], ALU.mult, [b_ret, b_gs], [b_ret])
                for half in range(2):
                    m = mi % 2
                    mi += 1
                    for f in range(4):
                        c0 = (half * 4 + f) * 128
                        trp(S, psM[m][:, f * 128:(f + 1) * 128], ret[:, c0:c0 + 128], K.identf[:], [b_ret], [b_psM[m]])
                    cp(S, "act", retT[rs_][:, half * 4:(half + 1) * 4, s4 * 128:(s4 + 1) * 128],
                       psM[m].rearrange("p (f t) -> p f t", f=4), [b_psM[m]], [b_retT[rs_]])
            for ot in range(8):
                g = ot % 2
                m = mi % 2
                mi += 1
                for k in range(8):
                    mm(S, psM[m], wg1[:, k, ot * 128:(ot + 1) * 128], K.HT[:, k, blk], k == 0, k == 7, [bw_g1, hb], [b_psM[m]])
                act(S, gsb[g][:], psM[m], AF.Sigmoid, [b_psM[m]], [b_gsb[g]])
                m = mi % 2
                mi += 1
                for f in range(8):
                    mm(S, psM[m], wpb[:, f, ot * 128:(ot + 1) * 128], retT[rs_][:, f, :], f == 0, f == 7, [bw_pb, b_retT[rs_]], [b_psM[m]])
                tt(S, "dve", gsb[g][:], psM[m], gsb[g][:], ALU.mult, [b_psM[m], b_gsb[g]], [b_gsb[g]])
                tt(S, "dve", mrg[:, ot, :], gsb[g][:], mprev[:, ot, :], ALU.add, [b_gsb[g], b_mprev], [b_mrg])
            for o2 in range(4):
                S.dma("sp", K.merged[:, 2 * o2:2 * o2 + 2, blk], mrg[:, 2 * o2:2 * o2 + 2, :], [b_mrg], [K.b_merged[tb]])
        S.flush()


PW = list(range(9)) + [16, 32, 64, 128, 256, 512, 1024, 2048]
PWI = {m: i for i, m in enumerate(PW)}
NPW = len(PW)


def s5_setup(K, S, l, Tt, ARt, AIt, AInt, b_T):
    with contextlib.ExitStack() as st:
        b = Buf()
        raw = sbt(K, st, "raw", [16, 3, 128], F32)
        ls2 = sbt(K, st, "ls2", [16, 2], F32)
        S.dma("sp", raw[:, 0, :], K.inp["lam_re"][l].rearrange("(pi g) p -> pi (g p)", g=2), (), [b])
        S.dma("sp", raw[:, 1, :], K.inp["lam_im"][l].rearrange("(pi g) p -> pi (g p)", g=2), (), [b])
        S.dma("sp", ls2[:], K.inp["log_step"][l].rearrange("(pi g) -> pi g", g=2), (), [b])
        cp(S, "dve", raw[:, 2, :].rearrange("q (g p) -> q g p", g=2), ls2[:].unsqueeze(2).broadcast_to([16, 2, 64]), [b], [b])
        draw = sbt(K, st, "draw", [4, 128], F32)
        S.dma("sp", draw[:], K.inp["d_skip"][l].rearrange("(a q) -> a q", q=128), (), [b])
        ps = K.PS[3]
        pb = K.PSb[3]
        for i in range(3):
            trp(S, ps[:, i * 16:(i + 1) * 16], raw[:, i, :], K.identf[0:16, 0:16], [b], [pb])
        trp(S, ps[:, 48:52], draw[:], K.identf[0:4, 0:4], [b], [pb])
        lam = sbt(K, st, "lam", [128, 4, 16], F32)
        dcol = sbt(K, st, "dcol", [128, 4], F32)
        cp(S, "dve", lam[:, 0:3, :].rearrange("p a b -> p (a b)"), ps[:, 0:48], [pb], [b])
        cp(S, "dve", dcol[:], ps[:, 48:52], [pb], [b])
        lr, li, stp = lam[:, 0, :], lam[:, 1, :], lam[:, 2, :]
        act(S, stp, stp, AF.Exp, [b], [b])
        er = sbt(K, st, "er", [128, 16], F32)
        th = sbt(K, st, "th", [128, 16], F32)
        tt(S, "dve", er[:], lr, stp, ALU.mult, [b], [b])
        tt(S, "dve", th[:], li, stp, ALU.mult, [b], [b])
        angM = sbt(K, st, "angM", [128, NPW + 1, 16], F32)
        magM = sbt(K, st, "magM", [128, NPW, 16], F32)
        for i, m in enumerate(PW):
            ts1(S, "dve", angM[:, i, :], th[:], float(m), ALU.mult, [b], [b])
            act(S, magM[:, i, :], er[:], AF.Exp, [b], [b], scale=float(m))
        ts1(S, "dve", angM[:, NPW, :], th[:], 0.5, ALU.mult, [b], [b])
        cosM = sincos(K, S, st, angM[:], [128, NPW + 1, 16], b, True)
        sinM = sincos(K, S, st, angM[:], [128, NPW + 1, 16], b, False)
        tt(S, "dve", ARt[:], magM[:], cosM[:, 0:NPW, :], ALU.mult, [b], [b_T])
        tt(S, "dve", AIt[:], magM[:], sinM[:, 0:NPW, :], ALU.mult, [b], [b_T])
        ts1(S, "dve", AInt[:], AIt[:], -1.0, ALU.mult, [b_T], [b_T])
        w = sbt(K, st, "wk", [128, 8, 16], F32)
        t0, t1, nr, ni, den, cr, ci, t2 = [w[:, i, :] for i in range(8)]
        ts(S, "dve", t0, er[:], 1.0 / 120.0, 1.0 / 24.0, ALU.mult, ALU.add, [b], [b])
        for cst in (1.0 / 6.0, 0.5, 1.0):
            tt(S, "dve", t0, t0, er[:], ALU.mult, [b], [b])
            ts1(S, "dve", t0, t0, cst, ALU.add, [b], [b])
        tt(S, "dve", t0, t0, er[:], ALU.mult, [b], [b])
        tt(S, "dve", nr, t0, cosM[:, 1, :], ALU.mult, [b], [b])
        tt(S, "dve", t1, sinM[:, NPW, :], sinM[:, NPW, :], ALU.mult, [b], [b])
        stt(S, "dve", nr, t1, -2.0, nr, ALU.mult, ALU.add, [b], [b])
        cp(S, "dve", ni, AIt[:, 1, :], [b, b_T], [b])
        tt(S, "dve", den, lr, lr, ALU.mult, [b], [b])
        tt(S, "dve", t1, li, li, ALU.mult, [b], [b])
        tt(S, "dve", den, den, t1, ALU.add, [b], [b])
        recip(S, den, den, [b], [b])
        tt(S, "dve", cr, nr, lr, ALU.mult, [b], [b])
        tt(S, "dve", t1, ni, li, ALU.mult, [b], [b])
        tt(S, "dve", cr, cr, t1, ALU.add, [b], [b])
        tt(S, "dve", cr, cr, den, ALU.mult, [b], [b])
        tt(S, "dve", ci, ni, lr, ALU.mult, [b], [b])
        tt(S, "dve", t1, nr, li, ALU.mult, [b], [b])
        tt(S, "dve", ci, ci, t1, ALU.subtract, [b], [b])
        tt(S, "dve", ci, ci, den, ALU.mult, [b], [b])
        Bn = sbt(K, st, "Bn", [128, 4, 16, 16], F32)
        for ri, nm in enumerate(("b_re", "b_im")):
            srcB = K.inp[nm][l].rearrange("(pi g) p k -> (g p) pi k", g=2)
            for pi_ in range(16):
                S.dma("sp", Bn[:, ri, pi_, :], srcB[:, pi_, :], (), [b])
        tmpB = sbt(K, st, "tmpB", [128, 16, 16], F32)

        def bc(x):
            return x.unsqueeze(2).broadcast_to([128, 16, 16])
        tt(S, "dve", Bn[:, 2], Bn[:, 0], bc(cr), ALU.mult, [b], [b])
        tt(S, "dve", tmpB[:], Bn[:, 1], bc(ci), ALU.mult, [b], [b])
        tt(S, "dve", Bn[:, 2], Bn[:, 2], tmpB[:], ALU.subtract, [b], [b])
        tt(S, "dve", Bn[:, 3], Bn[:, 1], bc(cr), ALU.mult, [b], [b])
        tt(S, "dve", tmpB[:], Bn[:, 0], bc(ci), ALU.mult, [b], [b])
        tt(S, "dve", Bn[:, 3], Bn[:, 3], tmpB[:], ALU.add, [b], [b])
        WA = sbt(K, st, "WA", [128, 2, 8, 16, 16], F32)
        for jp in range(8):
            pi_ = PWI[7 - jp]
            ar, ai = bc(ARt[:, pi_, :]), bc(AIt[:, pi_, :])
            tt(S, "dve", WA[:, 0, jp], Bn[:, 2], ar, ALU.mult, [b, b_T], [b])
            tt(S, "dve", tmpB[:], Bn[:, 3], ai, ALU.mult, [b, b_T], [b])
            tt(S, "dve", WA[:, 0, jp], WA[:, 0, jp], tmpB[:], ALU.subtract, [b], [b])
            tt(S, "dve", WA[:, 1, jp], Bn[:, 3], ar, ALU.mult, [b, b_T], [b])
            tt(S, "dve", tmpB[:], Bn[:, 2], ai, ALU.mult, [b, b_T], [b])
            tt(S, "dve", WA[:, 1, jp], WA[:, 1, jp], tmpB[:], ALU.add, [b], [b])
        craw = sbt(K, st, "craw", [128, 2, 4, 2, 64], F32)
        for ri, nm in enumerate(("c_re", "c_im")):
            src = K.inp[nm][l].rearrange("(a g) k p -> (g k) a p", g=8)
            for a in range(4):
                S.dma("sp", craw[:, ri, a, 0, :], src[:, a, :], (), [b])
                S.dma("sp", craw[:, ri, a, 1, :], src[:, a, :], (), [b])
        CT = sbt(K, st, "CT", [128, 2, 16, 16], F32)
        for ri in range(2):
            for a in range(4):
                pa_ = K.PS[2][:, (a % 2) * 512:(a % 2) * 512 + 128]
                pba = K.PSh[4 + a % 2]
                trp(S, pa_, craw[:, ri, a].rearrange("p d q -> p (d q)"), K.identf[:], [b], [pba])
                p3 = pa_.rearrange("p (g k) -> p g k", k=16)
                cp(S, "dve", CT[0:64, ri, 4 * a:4 * a + 4, :], p3[0:64, 0::2, :], [pba], [b])
                cp(S, "dve", CT[64:128, ri, 4 * a:4 * a + 4, :], p3[64:128, 1::2, :], [pba], [b])
        VA = sbt(K, st, "VA", [128, 2, 8, 16, 16], F32)
        for j in range(8):
            pi_ = PWI[j + 1]
            ar, ai = bc(ARt[:, pi_, :]), bc(AIt[:, pi_, :])
            tt(S, "dve", VA[:, 0, j], CT[:, 0], ar, ALU.mult, [b, b_T], [b])
            tt(S, "dve", tmpB[:], CT[:, 1], ai, ALU.mult, [b, b_T], [b])
            tt(S, "dve", VA[:, 0, j], VA[:, 0, j], tmpB[:], ALU.subtract, [b], [b])
            tt(S, "dve", VA[:, 1, j], CT[:, 0], ai, ALU.mult, [b, b_T], [b])
            tt(S, "dve", tmpB[:], CT[:, 1], ar, ALU.mult, [b, b_T], [b])
            tt(S, "dve", VA[:, 1, j], VA[:, 1, j], tmpB[:], ALU.add, [b], [b])
            ts1(S, "dve", VA[:, 1, j], VA[:, 1, j], -1.0, ALU.mult, [b], [b])
        Vst = [sbt(K, st, "Vst", [128, 2, 8, 128], BF16) for _ in range(4)]
        Nat = [sbt(K, st, "Nat", [128, 2, 8, 128], F32) for _ in range(4)]
        CTp = sbt(K, st, "CTp", [128, 4, 2, 128], F32)
        Wst = [sbt(K, st, "Wst", [128, 2, 8, 128], BF16) for _ in range(2)]
        b_V = [Buf() for _ in range(4)]
        b_N = [Buf() for _ in range(4)]
        b_W = [Buf() for _ in range(2)]
        b_C = Buf()
        for q in range(4):
            mset(S, "pool", Vst[q][:], 0.0, [b_V[q]])
            mset(S, "pool", Nat[q][:], 0.0, [b_N[q]])
        mset(S, "pool", CTp[:], 0.0, [b_C])
        psT = [K.PS[0], K.PS[1]]
        b_psT = [K.PSb[0], K.PSb[1]]
        wi = 0
        for a in range(4):
            for q in range(4):
                pi_ = 4 * a + q
                lo, hi = slice(0, 64), slice(64, 128)
                c0, c1 = slice(32 * q, 32 * q + 16), slice(32 * q + 16, 32 * q + 32)
                for ri in range(2):
                    cp(S, "dve", Vst[q][lo, ri, :, c0], VA[lo, ri, :, pi_, :], [b], [b_V[q]])
                    cp(S, "dve", Vst[q][hi, ri, :, c1], VA[hi, ri, :, pi_, :], [b], [b_V[q]])
                    cp(S, "dve", Nat[q][lo, ri, :, c0], WA[lo, ri, :, pi_, :], [b], [b_N[q]])
                    cp(S, "dve", Nat[q][hi, ri, :, c1], WA[hi, ri, :, pi_, :], [b], [b_N[q]])
                    cp(S, "dve", CTp[lo, q, ri, c0], CT[lo, ri, pi_, :], [b], [b_C])
                    cp(S, "dve", CTp[hi, q, ri, c1], CT[hi, ri, pi_, :], [b], [b_C])
                ts1(S, "dve", CTp[:, q, 1, :], CTp[:, q, 1, :], -1.0, ALU.mult, [b_C], [b_C])
                S.dma("sp", K.Vd[pi_], Vst[q][:], [b_V[q]], [K.b_Vd])
                ws = wi % 2
                wi += 1
                for ri in range(2):
                    for jh in range(2):
                        x = (ri * 2 + jh) % 2
                        for jj in range(4):
                            trp(S, psT[x][:, jj * 128:(jj + 1) * 128], Nat[q][:, ri, jh * 4 + jj, :], K.identf[:], [b_N[q]], [b_psT[x]])
                        cp(S, "act", Wst[ws][:, ri, jh * 4:jh * 4 + 4, :].rearrange("p j c -> p (j c)"), psT[x][:, 0:512], [b_psT[x]], [b_W[ws]])
                S.dma("sp", K.Wd[pi_], Wst[ws][:], [b_W[ws]], [K.b_Wd])
            for dl in range(8):
                x = dl % 2
                pT_ = K.PS[3][:, x * 512:x * 512 + 128]
                pbT = K.PSh[6 + x]
                n = 0
                for q in range(4):
                    for ri in range(2):
                        mm(S, pT_, Nat[q][:, ri, 7 - dl, :], CTp[:, q, ri, :], n == 0, n == 7, [b_N[q], b_C], [pbT])
                        n += 1
                if dl == 0:
                    stt(S, "dve", Tt[:, a, dl, :], K.identf[:], dcol[:, a:a + 1], pT_, ALU.mult, ALU.add, [pbT, b], [b_T])
                else:
                    cp(S, "dve", Tt[:, a, dl, :], pT_, [pbT], [b_T])
        S.flush()


def stage_s5(K, S, l, yT, b_yT, Tt, ARt, AIt, AInt, b_T):
    with contextlib.ExitStack() as st:
        w_in = K.inp["w_in"][l].rearrange("(k p) c -> p k c", p=128)
        wu = sbt(K, st, "wu", [128, 8, 512], BF16)
        bw = Buf()
        for k in range(8):
            load_w(K, S, wu[:, k, :], w_in[:, k, 4608:5120], bw)
        ucT = sbt(K, st, "ucT", [128, 4, T], BF16)
        b_uc = [Buf() for _ in range(4)]
        psM = [K.PS[3][:, 0:512], K.PS[3][:, 512:1024]]
        b_psM = [K.PSh[6], K.PSh[7]]
        mi = 0
        for tb in range(NTB):
            blk = slice(tb * 512, (tb + 1) * 512)
            for a in range(4):
                m = mi % 2
                mi += 1
                for k in range(8):
                    mm(S, psM[m], wu[:, k, a * 128:(a + 1) * 128], K.HT[:, k, blk], k == 0, k == 7, [bw, K.HTb[tb]], [b_psM[m]])
                cp(S, "act", ucT[:, a, blk], psM[m], [b_psM[m]], [b_uc[a]])
        Vsb = sbt(K, st, "Vsb", [128, 4, 2, 8, 128], BF16)
        Wsb = [sbt(K, st, "Wsb", [128, 2, 8, 128], BF16) for _ in range(2)]
        b_Vsb = Buf()
        b_Wsb = [Buf() for _ in range(2)]
        X = [[sbt(K, st, "X", [128, 512], F32) for _ in range(2)] for _ in range(2)]
        b_X = [[Buf() for _ in range(2)] for _ in range(2)]
        Xp = [sbt(K, st, "Xp", [128, 4, 512], BF16) for _ in range(2)]
        ptmp = sbt(K, st, "ptmp", [128, 512], F32)
        b_ptmp = Buf()
        b_Xp = Buf()
        psA = [K.PS[0][:, 0:512], K.PS[0][:, 512:1024], K.PS[1][:, 0:512], K.PS[1][:, 512:1024]]
        b_psA = [K.PSh[0], K.PSh[1], K.PSh[2], K.PSh[3]]
        for i in range(4):
            b_psA[i].w = K.PSb[i // 2].w
            b_psA[i].r = dict(K.PSb[i // 2].r)
        ai_ = 0
        wi = 0
        for a in range(4):
            for q in range(4):
                S.dma("sp", Vsb[:, q], K.Vd[4 * a + q], [K.b_Vd], [b_Vsb])
            mset(S, "pool", Xp[0][:, :, 0:1], 0.0, [b_Xp])
            mset(S, "pool", Xp[1][:, :, 0:1], 0.0, [b_Xp])
            for q in range(4):
                pi_ = 4 * a + q
                ws = wi % 2
                wi += 1
                S.dma("sp", Wsb[ws][:], K.Wd[pi_], [K.b_Wd], [b_Wsb[ws]])
                for ri in range(2):
                    x = ai_ % 4
                    ai_ += 1
                    for jp in range(8):
                        mm(S, psA[x], Wsb[ws][:, ri, jp, :], ucT[:, a, jp::8], jp == 0, jp == 7, [b_Wsb[ws], b_uc[a]], [b_psA[x]])
                    cp(S, "act", X[0][ri][:], psA[x], [b_psA[x]], [b_X[0][ri]])
                cur = 0
                for lv in range(9):
                    sft = 1 << lv
                    pw = PWI[8 * sft]
                    ar = ARt[:, pw, pi_:pi_ + 1]
                    ai = AIt[:, pw, pi_:pi_ + 1]
                    an = AInt[:, pw, pi_:pi_ + 1]
                    o, n_ = X[cur], X[1 - cur]
                    bo, bn = b_X[cur], b_X[1 - cur]
                    hd, tl, bd = slice(0, sft), slice(sft, 512), slice(0, 512 - sft)
                    stt(S, "dve", n_[0][:, tl], o[0][:, bd], ar, o[0][:, tl], ALU.mult, ALU.add, [bo[0], b_T], [bn[0]])
                    stt(S, "dve", n_[0][:, tl], o[1][:, bd], an, n_[0][:, tl], ALU.mult, ALU.add, [bo[1], b_T], [bn[0]])
                    cp(S, "act", n_[0][:, hd], o[0][:, hd], [bo[0]], [bn[0]])
                    stt(S, "dve", n_[1][:, tl], o[0][:, bd], ai, o[1][:, tl], ALU.mult, ALU.add, [bo[0], bo[1], b_T], [bn[1]])
                    stt(S, "dve", n_[1][:, tl], o[1][:, bd], ar, n_[1][:, tl], ALU.mult, ALU.add, [bo[1], b_T], [bn[1]])
                    cp(S, "act", n_[1][:, hd], o[1][:, hd], [bo[1]], [bn[1]])
                    cur = 1 - cur
                for ri in range(2):
                    cp(S, "act", Xp[ri][:, q, 1:512], X[cur][ri][:, 0:511], [b_X[cur][ri]], [b_Xp])
            for j in range(8):
                x = ai_ % 4
                ai_ += 1
                n = 0
                tot = 8 + j + 1
                for q in range(4):
                    for ri in range(2):
                        mm(S, psA[x], Vsb[:, q, ri, j, :], Xp[ri][:, q, :], n == 0, n == tot - 1, [b_Vsb, b_Xp], [b_psA[x]])
                        n += 1
                for jp in range(j + 1):
                    mm(S, psA[x], Tt[:, a, j - jp, :], ucT[:, a, jp::8], n == 0, n == tot - 1, [b_T, b_uc[a]], [b_psA[x]])
                    n += 1
                act(S, yT[:, a, j::8], psA[x], AF.Gelu_apprx_tanh, [b_psA[x]], [b_yT])
        S.flush()


def stage_merge(K, S, l, yT, b_yT):
    with contextlib.ExitStack() as st:
        w_in = K.inp["w_in"][l].rearrange("(k p) c -> p k c", p=128)
        wglu = sbt(K, st, "wglu", [128, 4, 512], BF16)
        wpc = sbt(K, st, "wpc", [128, 4, 1024], BF16)
        wg2 = sbt(K, st, "wg2", [128, 8, 1024], BF16)
        wo = sbt(K, st, "wo", [128, 8, 1024], BF16)
        bw = Buf()
        bw_pc, bw_g2, bw_o = Buf(), Buf(), Buf()
        for k_ in range(4):
            load_w(K, S, wglu[:, k_, :], K.inp["w_glu"][l].rearrange("(k p) c -> p k c", p=128)[:, k_, :], bw)
        for k_ in range(4):
            load_w(K, S, wpc[:, k_, :], K.inp["w_proj_c"][l].rearrange("(k p) c -> p k c", p=128)[:, k_, :], bw_pc)
        for k in range(8):
            load_w(K, S, wg2[:, k, :], w_in[:, k, 7168:8192], bw_g2)
        for k_ in range(8):
            load_w(K, S, wo[:, k_, :], K.inp["w_o"][l].rearrange("(k p) c -> p k c", p=128)[:, k_, :], bw_o)
        sc = ln_scratch(K, st)
        ln_load_gb(K, S, sc, K.inp["ln1_g"][l], K.inp["ln1_b"][l])
        zs = sbt(K, st, "zs", [128, 512], F32)
        b_zs = Buf()
        ygT = sbt(K, st, "ygT", [128, 4, 512], BF16)
        b_yg = Buf()
        gsb = sbt(K, st, "gsbC", [128, 512], F32)
        b_gsb = Buf()
        mprev = sbt(K, st, "mprevC", [128, 8, 512], BF16)
        b_mprev = Buf()
        mT = sbt(K, st, "mT", [128, 8, 512], BF16)
        b_mT = Buf()
        hprev = [sbt(K, st, "hprev", [128, 1024], F32) for _ in range(2)]
        b_hp = [Buf() for _ in range(2)]
        psM = [K.PS[3][:, 0:512], K.PS[3][:, 512:1024]]
        b_psM = [K.PSh[6], K.PSh[7]]
        mi = 0
        pend = None
        for tb in range(NTB):
            blk = slice(tb * 512, (tb + 1) * 512)
            hb = K.HTb[tb]
            for o2 in range(4):
                S.dma("sp", mprev[:, 2 * o2:2 * o2 + 2, :], K.merged[:, 2 * o2:2 * o2 + 2, blk], [K.b_merged[tb]], [b_mprev])
            for ot in range(4):
                m = mi % 2
                mi += 1
                for k in range(4):
                    mm(S, psM[m], wglu[:, k, ot * 128:(ot + 1) * 128], yT[:, k, blk], k == 0, k == 3, [bw, b_yT], [b_psM[m]])
                act(S, zs[:], psM[m], AF.Sigmoid, [b_psM[m]], [b_zs])
                tt(S, "dve", ygT[:, ot, :], yT[:, ot, blk], zs[:], ALU.mult, [b_yT, b_zs], [b_yg])
            for ot in range(8):
                m = mi % 2
                mi += 1
                for k in range(8):
                    mm(S, psM[m], wg2[:, k, ot * 128:(ot + 1) * 128], K.HT[:, k, blk], k == 0, k == 7, [bw_g2, hb], [b_psM[m]])
                act(S, gsb[:], psM[m], AF.Sigmoid, [b_psM[m]], [b_gsb])
                m = mi % 2
                mi += 1
                for k in range(4):
                    mm(S, psM[m], wpc[:, k, ot * 128:(ot + 1) * 128], ygT[:, k, :], k == 0, k == 3, [bw_pc, b_yg], [b_psM[m]])
                tt(S, "dve", gsb[:], psM[m], gsb[:], ALU.mult, [b_psM[m], b_gsb], [b_gsb])
                tt(S, "dve", mT[:, ot, :], gsb[:], mprev[:, ot, :], ALU.add, [b_gsb, b_mprev], [b_mT])
            for t4 in range(4):
                t = tb * 4 + t4
                i = t % 2
                S.dma("sp", hprev[i][:], K.hres[t * 128:(t + 1) * 128, :], [K.b_hres[tb]], [b_hp[i]])
                pso, bpso = K.PS[i], K.PSb[i]
                for half in range(2):
                    for f in range(8):
                        mm(S, pso[:, half * 512:(half + 1) * 512], mT[:, f, t4 * 128:(t4 + 1) * 128], wo[:, f, half * 512:(half + 1) * 512],
                           f == 0, f == 7, [bw_o, b_mT], [bpso])
                if pend is not None:
                    pend()
                for half in range(2):
                    cs = slice(half * 512, (half + 1) * 512)
                    stt(S, "dve", hprev[i][:, cs], hprev[i][:, cs], ALPHA, pso[:, cs], ALU.mult, ALU.add, [b_hp[i], bpso], [b_hp[i]])
                pend = ln_apply(K, S, sc, hprev[i][:], b_hp[i], t, K.hres, K.b_hres[tb], K.PS[2][:, :], K.PSb[2], defer=True)
        pend()
        S.flush()


def stage_ffn(K, S, l, last):
    HF = 1408
    for p in range(2):
        with contextlib.ExitStack() as st:
            w_up = K.inp["w_up"][l].rearrange("(k p) c -> p k c", p=128)
            wup = sbt(K, st, "wup", [128, 8, 2 * HF], BF16)
            wdn = sbt(K, st, "wdn", [128, 11, 1024], BF16)
            bw = Buf()
            bw_d = Buf()
            for k in range(8):
                load_w(K, S, wup[:, k, 0:HF], w_up[:, k, p * HF:(p + 1) * HF], bw)
                load_w(K, S, wup[:, k, HF:2 * HF], w_up[:, k, 2816 + p * HF:2816 + (p + 1) * HF], bw)
            for k_ in range(11):
                load_w(K, S, wdn[:, k_, :], K.inp["w_down"][l][p * HF:(p + 1) * HF, :].rearrange("(k p) c -> p k c", p=128)[:, k_, :], bw_d)
            craw = sbt(K, st, "cwraw", [88, 128], F32)
            cw = sbt(K, st, "cw", [128, 88], F32)
            b_cw = Buf()
            cwl = K.inp["conv_w"][l]
            cbl = K.inp["conv_b"][l]
            for t3 in range(3):
                S.dma("sp", craw[t3 * 22:t3 * 22 + 11, :], cwl[t3, p * HF:(p + 1) * HF].rearrange("(n q) -> n q", q=128), (), [b_cw])
                S.dma("sp", craw[t3 * 22 + 11:t3 * 22 + 22, :], cwl[t3, 2816 + p * HF:2816 + (p + 1) * HF].rearrange("(n q) -> n q", q=128), (), [b_cw])
            S.dma("sp", craw[66:77, :], cbl[p * HF:(p + 1) * HF].rearrange("(n q) -> n q", q=128), (), [b_cw])
            S.dma("sp", craw[77:88, :], cbl[2816 + p * HF:2816 + (p + 1) * HF].rearrange("(n q) -> n q", q=128), (), [b_cw])
            trp(S, K.PS[2][:, 0:88], craw[:], K.identf[0:88, 0:88], [b_cw], [K.PSb[2]])
            cp(S, "dve", cw[:], K.PS[2][:, 0:88], [K.PSb[2]], [b_cw])
            if RET_CUT == 101:
                S.flush()
                return
            sc = ln_scratch(K, st)
            if p == 1:
                ln_load_gb(K, S, sc, K.inp["ln2_g"][l], K.inp["ln2_b"][l])
            diagw = sbt(K, st, "diagw", [128, 22, 3, 128], BF16)
            b_dg = Buf()
            for c in range(22):
                for t3 in range(3):
                    ts1(S, "dve", diagw[:, c, t3, :], K.identf[:], cw[:, t3 * 22 + c:t3 * 22 + c + 1], ALU.mult, [b_cw], [b_dg])
            xbuf = [sbt(K, st, "xbuf", [128, 514], BF16) for _ in range(2)]
            cva = [sbt(K, st, "cva", [128, 512], F32) for _ in range(2)]
            b_xb = [Buf() for _ in range(2)]
            b_cva = [Buf() for _ in range(2)]
            halo = sbt(K, st, "halo", [128, 22, 2], BF16)
            b_halo = Buf()
            mset(S, "pool", halo[:], 0.0, [b_halo])
            actT = sbt(K, st, "actT", [128, 11, 512], BF16)
            b_actT = Buf()
            hprev = [sbt(K, st, "hprevF", [128, 1024], F32) for _ in range(2)]
            b_hp = [Buf() for _ in range(2)]
            psM = [K.PS[3][:, 0:512], K.PS[3][:, 512:1024]]
            b_psM = [K.PSh[6], K.PSh[7]]
            psC = [K.PS[2][:, 0:512], K.PS[2][:, 512:1024]]
            b_psC = [K.PSh[4], K.PSh[5]]
            for i_ in range(2):
                b_psC[i_].w = K.PSb[2].w
                b_psC[i_].r = dict(K.PSb[2].r)
            mi = 0
            pend2 = None
            cstate = [0]

            def conv_stage(n, wh, c, pa):
                xb, bx = xbuf[wh], b_xb[wh]
                m2 = cstate[0] % 2
                cstate[0] += 1
                for t3 in range(3):
                    mm(S, psC[m2], diagw[:, c, t3, :], xb[:, t3:t3 + 512], t3 == 0, t3 == 2, [b_dg, bx], [b_psC[m2]])
                if wh == 0:
                    S.op("act", lambda e: e.activation(out=cva[pa][:], in_=psC[m2], func=AF.Gelu_apprx_tanh,
                                                       bias=cw[:, 66 + c:67 + c], scale=1.0),
                         [b_psC[m2], b_cw], [b_cva[pa]])
                else:
                    stt(S, "dve", actT[:, n, :], psC[m2], cw[:, 66 + c:67 + c], cva[pa][:], ALU.add, ALU.mult,
                        [b_psC[m2], b_cw, b_cva[pa]], [b_actT])
                cp(S, "dve", halo[:, c, :], xb[:, 512:514], [bx], [b_halo])

            for tb in range(NTB):
                blk = slice(tb * 512, (tb + 1) * 512)
                hb = K.HTb[tb]
                pending = None
                for n in range(11):
                    pa = n % 2
                    for wh in range(2):
                        c = n + 11 * wh
                        m = mi % 2
                        mi += 1
                        xb, bx = xbuf[wh], b_xb[wh]
                        for k in range(8):
                            mm(S, psM[m], wup[:, k, c * 128:(c + 1) * 128], K.HT[:, k, blk], k == 0, k == 7, [bw, hb], [b_psM[m]])
                        if pending is not None and pending[1] == wh:
                            conv_stage(*pending)
                            pending = None
                        cp(S, "dve", xb[:, 0:2], halo[:, c, :], [b_halo], [bx])
                        cp(S, "act", xb[:, 2:514], psM[m], [b_psM[m]], [bx])
                        if pending is not None:
                            conv_stage(*pending)
                        pending = (n, wh, c, pa)
                conv_stage(*pending)
                if RET_CUT == 102:
                    S.flush()
                    return
                for t4 in range(4):
                    t = tb * 4 + t4
                    i = t % 2
                    rows = slice(t * 128, (t + 1) * 128)
                    src = K.hres if p == 0 else K.fpart
                    srcb = K.b_hres[tb] if p == 0 else K.b_fpart[tb]
                    S.dma("sp", hprev[i][:], src[rows, :], [srcb], [b_hp[i]])
                    pso, bpso = K.PS[i], K.PSb[i]
                    for half in range(2):
                        for f in range(11):
                            mm(S, pso[:, half * 512:(half + 1) * 512], actT[:, f, t4 * 128:(t4 + 1) * 128], wdn[:, f, half * 512:(half + 1) * 512],
                               f == 0, f == 10, [bw_d, b_actT], [bpso])
                    if pend2 is not None:
                        pend2()
                        pend2 = None
                    for half in range(2):
                        cs = slice(half * 512, (half + 1) * 512)
                        stt(S, "dve", hprev[i][:, cs], hprev[i][:, cs], ALPHA if p == 0 else 1.0, pso[:, cs], ALU.mult, ALU.add,
                            [b_hp[i], bpso], [b_hp[i]])
                    if RET_CUT == 103:
                        S.flush()
                        return
                    if p == 0:
                        S.dma("sp", K.fpart[rows, :], hprev[i][:], [b_hp[i]], [K.b_fpart[tb]])
                    elif last:
                        ln_apply(K, S, sc, hprev[i][:], b_hp[i], t, None, None, pso[:, :], bpso, final_out=K.y, final_buf=K.b_y)
                    else:
                        pend2 = ln_apply(K, S, sc, hprev[i][:], b_hp[i], t, K.hres, K.b_hres[tb], pso[:, :], bpso, defer=True)
            if pend2 is not None:
                pend2()
            S.flush()

IN_SPECS = [
    ("x", [T, D]), ("ln_in_g", [D]), ("ln_in_b", [D]), ("w_in", [NL, D, 8192]), ("rel_bias", [NL, 8, 257]),
    ("w_proj_a", [NL, 512, D]), ("w_proj_b", [NL, 1024, D]), ("w_proj_c", [NL, 512, D]),
    ("lam_re", [NL, 32, 64]), ("lam_im", [NL, 32, 64]), ("log_step", [NL, 32]),
    ("b_re", [NL, 32, 64, 16]), ("b_im", [NL, 32, 64, 16]), ("c_re", [NL, 32, 16, 64]), ("c_im", [NL, 32, 16, 64]),
    ("d_skip", [NL, 512]), ("w_glu", [NL, 512, 512]), ("w_o", [NL, D, D]), ("ln1_g", [NL, D]), ("ln1_b", [NL, D]),
    ("w_up", [NL, D, 5632]), ("conv_w", [NL, 3, 5632]), ("conv_b", [NL, 5632]), ("w_down", [NL, 2816, D]),
    ("ln2_g", [NL, D]), ("ln2_b", [NL, D]),
]


def build_program():
    nc = bass.Bass("TRN2", target_bir_lowering=False)
    K = Ctx()
    K.nc = nc
    K.uid = 0
    K.inp = {n: nc.dram_tensor(n, list(s), F32, kind="ExternalInput").ap() for n, s in IN_SPECS}
    K.y = nc.dram_tensor("y", [T, D], F32, kind="ExternalOutput").ap()
    dbg = "ExternalOutput" if DEBUG else "Internal"
    K.hres = nc.dram_tensor("hres", [T, D], F32, kind=dbg).ap()
    K.merged = nc.dram_tensor("merged", [128, 8, T], BF16, kind=dbg).ap()
    K.Fd = nc.dram_tensor("Fd", [8, 768], F32, kind="Internal").ap()
    K.Wd = nc.dram_tensor("Wd", [16, 128, 2, 8, 128], BF16, kind="Internal").ap()
    K.Vd = nc.dram_tensor("Vd", [16, 128, 2, 8, 128], BF16, kind="Internal").ap()
    K.b_Wd = Buf()
    K.b_Vd = Buf()
    if DEBUG:
        K.dbg2 = nc.dram_tensor("dbg2", [128, 4, T], BF16, kind="ExternalOutput").ap()
        K.dbg1 = nc.dram_tensor("dbg1", [128, 8 * 5 * 128], F32, kind="ExternalOutput").ap()
    K.fpart = nc.dram_tensor("fpart", [T, D], F32, kind="Internal").ap()
    K.b_fpart = [Buf() for _ in range(NTB)]
    K.b_hres = [Buf() for _ in range(NTB)]
    K.b_merged = [Buf() for _ in range(NTB)]
    K.b_y = Buf()
    with contextlib.ExitStack() as gst:
        S = Sched(nc, gst)
        K.HT = gst.enter_context(nc.sbuf_tensor("HT", [128, 8, T], BF16))
        K.HTb = [Buf() for _ in range(NTB)]
        K.identf = gst.enter_context(nc.sbuf_tensor("identf", [128, 128], F32))
        K.Jf = gst.enter_context(nc.sbuf_tensor("Jf", [128, 128], F32))
        io = gst.enter_context(nc.sbuf_tensor("iota_i", [128, 128], I32))
        K.PS = [gst.enter_context(nc.psum_tensor("PS%d" % i, [128, 1024], F32)) for i in range(4)]
        K.PSb = [Buf(excl=True) for _ in range(4)]
        K.PSh = [Buf(excl=True) for _ in range(8)]
        bc = Buf()
        S.op("pool", lambda e: e.iota(io[:], pattern=[[1, 128]], base=0, channel_multiplier=-1), (), [bc])
        ts1(S, "dve", K.identf[:], io[:], 0.0, ALU.is_equal, [bc], [bc])
        S.op("pool", lambda e: e.iota(io[:], pattern=[[1, 128]], base=0, channel_multiplier=1), [bc], [bc])
        ts1(S, "dve", K.Jf[:], io[:], 127.0, ALU.is_equal, [bc], [bc])
        S.flush()
        stage_entry(K, S)
        if STOP_AFTER == "entry":
            return nc
        for l in LAYERS:
            if STOP_AFTER == "ffnonly":
                stage_ffn(K, S, l, False)
                return nc
            if STOP_AFTER not in ("s5only", "rettab"):
                stage_attn(K, S, l)
            if STOP_AFTER in ("attn", "bias"):
                return nc
            if STOP_AFTER != "s5only":
                stage_ret(K, S, l)
            if STOP_AFTER in ("ret", "rettab"):
                return nc
            with contextlib.ExitStack() as sty:
                Tt = sbt(K, sty, "Tt", [128, 4, 8, 128], BF16)
                ARt = sbt(K, sty, "ARt", [128, NPW, 16], F32)
                AIt = sbt(K, sty, "AIt", [128, NPW, 16], F32)
                AInt = sbt(K, sty, "AInt", [128, NPW, 16], F32)
                b_T = Buf()
                s5_setup(K, S, l, Tt, ARt, AIt, AInt, b_T)
                yT = sbt(K, sty, "yT", [128, 4, T], BF16)
                b_yT = Buf()
                stage_s5(K, S, l, yT, b_yT, Tt, ARt, AIt, AInt, b_T)
                if STOP_AFTER in ("s5", "s5only"):
                    S.dma("sp", K.dbg2[:, :, :], yT[:], [b_yT], [Buf()])
                    S.flush()
                    return nc
                stage_merge(K, S, l, yT, b_yT)
            if STOP_AFTER == "merge":
                return nc
            stage_ffn(K, S, l, l == NL - 1)
            if STOP_AFTER == "ffn":
                return nc
    return nc


_PROG = None


def kernel(**inputs):
    global _PROG
    if _PROG is None:
        _PROG = build_program()
    nc = _PROG
    x = np.ascontiguousarray(np.asarray(inputs["x"], dtype=np.float32))
    shared = {n: np.ascontiguousarray(np.asarray(inputs[n], dtype=np.float32)) for n, _ in IN_SPECS if n != "x"}
    in_maps = []
    for c in range(8):
        m = dict(shared)
        m["x"] = x[c]
        in_maps.append(m)
    res = run_bass_kernel_spmd(nc, in_maps, core_ids=list(range(8)))
    return np.stack([r["y"] for r in res.results], axis=0).astype(np.float32)
```

```python
import contextlib
import math
import numpy as np
import concourse.bass as bass
import concourse.mybir as mybir
from concourse.bass_utils import run_bass_kernel_spmd

F32 = mybir.dt.float32
BF16 = mybir.dt.bfloat16
I32 = mybir.dt.int32
AF = mybir.ActivationFunctionType
ALU = mybir.AluOpType
AX = mybir.AxisListType

SAME_ENGINE_SYNC = True
N_DMA_SEMS = 12
N_SW_SEMS = 4
DEBUG = False
STOP_AFTER = None
LIMIT_TB = None
RET_CUT = None
LAYERS = (0, 1)
MM_LAZY_INC = True

T = 4096
D = 1024
NTT = 32
NTB = 8
NL = 2
ALPHA = (2.0 * NL) ** 0.25
LN_EPS = 1e-5
TWO_PI = 2.0 * math.pi


class Buf:
    __slots__ = ("name", "w", "r", "excl")

    def __init__(self, name="", excl=False):
        self.name = name
        self.w = None
        self.r = {}
        self.excl = excl


class Sched:
    ENG = ("pe", "act", "dve", "pool", "sp")

    def __init__(self, nc, stack):
        self.nc = nc
        self.streams = {e: [] for e in self.ENG}
        self.count = {e: 0 for e in self.ENG}
        self.seen = {e: {} for e in self.ENG}
        self.csem = {e: stack.enter_context(nc.semaphore("c_" + e)) for e in self.ENG}
        self.dsem = [stack.enter_context(nc.semaphore("d%d" % i)) for i in range(N_DMA_SEMS)]
        self.duse = [0] * N_DMA_SEMS
        self.dnext = 0
        self.dnext_sw = 0
        self.ninst = 0

    def _sem_of(self, key):
        return self.csem[key[1]] if key[0] == "e" else self.dsem[key[1]]

    def _add_wait(self, eng, k, v):
        seen = self.seen[eng]
        if seen.get(k, 0) >= v:
            return
        seen[k] = v
        sem = self._sem_of(k)
        self.streams[eng].append(lambda e: e.wait_ge(sem, v))
        self.ninst += 1

    def _waits(self, eng, reads, writes):
        deps = {}
        for b in reads:
            if b.w is not None:
                k, v = b.w
                if deps.get(k, 0) < v:
                    deps[k] = v
            if b.excl:
                for k, v in b.r.items():
                    if k != ("e", eng) and deps.get(k, 0) < v:
                        deps[k] = v
        for b in writes:
            if b.w is not None:
                k, v = b.w
                if deps.get(k, 0) < v:
                    deps[k] = v
            for k, v in b.r.items():
                if deps.get(k, 0) < v:
                    deps[k] = v
        for k, v in deps.items():
            if k == ("e", eng) and (eng == "pe" or not SAME_ENGINE_SYNC):
                continue
            self._add_wait(eng, k, v)

    def _commit(self, tok, reads, writes):
        k, v = tok
        for b in reads:
            if b.r.get(k, 0) < v:
                b.r[k] = v
        for b in writes:
            b.w = tok
            b.r = {}

    def op(self, eng, fn, reads=(), writes=(), inc=True):
        self._waits(eng, reads, writes)
        if not inc:
            self.streams[eng].append(lambda e: fn(e))
            self.ninst += 1
            tok = (("e", eng), self.count[eng] + 1)
            self._commit(tok, reads, writes)
            return tok
        self.count[eng] += 1
        n = self.count[eng]
        sem = self.csem[eng]
        self.streams[eng].append(lambda e: fn(e).then_inc(sem, 1))
        self.ninst += 1
        tok = (("e", eng), n)
        self._commit(tok, reads, writes)
        return tok

    def dma(self, eng, out, in_, reads=(), writes=(), **kw):
        if eng == "pool":
            s = N_DMA_SEMS - N_SW_SEMS + self.dnext_sw
            self.dnext_sw = (self.dnext_sw + 1) % N_SW_SEMS
        else:
            s = self.dnext
            self.dnext = (self.dnext + 1) % (N_DMA_SEMS - N_SW_SEMS)
        k = ("d", s)
        prev = 16 * self.duse[s]
        if prev > 0:
            self._add_wait(eng, k, prev)
        self._waits(eng, reads, writes)
        self.duse[s] += 1
        sem = self.dsem[s]
        self.streams[eng].append(lambda e: e.dma_start(out=out, in_=in_, **kw).then_inc(sem, 16))
        self.ninst += 1
        tok = (k, 16 * self.duse[s])
        self._commit(tok, reads, writes)
        return tok

    def flush(self):
        toks = [(("d", s), 16 * self.duse[s]) for s in range(N_DMA_SEMS) if self.duse[s]]
        toks += [(("e", e), self.count[e]) for e in self.ENG if self.count[e]]
        for e in self.ENG:
            for k, v in toks:
                if k == ("e", e):
                    continue
                self._add_wait(e, k, v)
        nc = self.nc
        streams = self.streams
        with nc.Block() as block:
            @block.tensor
            def _(eng):
                for f in streams["pe"]:
                    f(eng)

            @block.scalar
            def _(eng):
                for f in streams["act"]:
                    f(eng)

            @block.vector
            def _(eng):
                for f in streams["dve"]:
                    f(eng)

            @block.gpsimd
            def _(eng):
                for f in streams["pool"]:
                    f(eng)

            @block.sync
            def _(eng):
                for f in streams["sp"]:
                    f(eng)
        self.streams = {e: [] for e in self.ENG}


def mm(S, out, lhsT, rhs, start, stop, rd, wr):
    S.op("pe", lambda e: e.matmul(out, lhsT, rhs, start=start, stop=stop), rd, wr, inc=(stop or not MM_LAZY_INC))


def trp(S, out, in_, ident, rd, wr):
    S.op("pe", lambda e: e.transpose(out, in_, ident), rd, wr)


def act(S, out, in_, func, rd, wr, bias=0.0, scale=1.0):
    S.op("act", lambda e: e.activation(out=out, in_=in_, func=func, bias=bias, scale=scale), rd, wr)


def tt(S, eng, out, in0, in1, op, rd, wr):
    S.op(eng, lambda e: e.tensor_tensor(out=out, in0=in0, in1=in1, op=op), rd, wr)


def ts(S, eng, out, in0, s1, s2, op0, op1, rd, wr):
    S.op(eng, lambda e: e.tensor_scalar(out=out, in0=in0, scalar1=s1, scalar2=s2, op0=op0, op1=op1), rd, wr)


def ts1(S, eng, out, in0, s1, op0, rd, wr):
    S.op(eng, lambda e: e.tensor_scalar(out=out, in0=in0, scalar1=s1, scalar2=None, op0=op0), rd, wr)


def stt(S, eng, out, in0, scalar, in1, op0, op1, rd, wr):
    S.op(eng, lambda e: e.scalar_tensor_tensor(out=out, in0=in0, scalar=scalar, in1=in1, op0=op0, op1=op1), rd, wr)


def cp(S, eng, out, in_, rd, wr):
    if eng == "act":
        S.op("act", lambda e: e.copy(out=out, in_=in_), rd, wr)
    else:
        S.op(eng, lambda e: e.tensor_copy(out=out, in_=in_), rd, wr)


def mset(S, eng, ap, val, wr):
    S.op(eng, lambda e: e.memset(ap, val), (), wr)


def recip(S, out, in_, rd, wr):
    S.op("dve", lambda e: e.reciprocal(out=out, in_=in_), rd, wr)


class Ctx:
    pass


def sbt(K, st, name, shape, dtype):
    K.uid += 1
    return st.enter_context(K.nc.sbuf_tensor("%s_%d" % (name, K.uid), list(shape), dtype))


def sincos(K, S, st, ang, shape, b, want_cos):
    yy = sbt(K, st, "sc_y", shape, F32)
    ki = sbt(K, st, "sc_k", shape, I32)
    kf = sbt(K, st, "sc_kf", shape, F32)
    out = sbt(K, st, "sc_o", shape, F32)
    off = (math.pi / 2.0 if want_cos else 0.0)
    ts1(S, "dve", yy[:], ang, off, ALU.add, [b], [b])
    ts1(S, "dve", ki[:], yy[:], 1.0 / TWO_PI, ALU.mult, [b], [b])
    cp(S, "dve", kf[:], ki[:], [b], [b])
    stt(S, "dve", yy[:], kf[:], -TWO_PI, yy[:], ALU.mult, ALU.add, [b], [b])
    ts(S, "dve", kf[:], yy[:], math.pi, -TWO_PI, ALU.is_gt, ALU.mult, [b], [b])
    tt(S, "dve", yy[:], yy[:], kf[:], ALU.add, [b], [b])
    ts(S, "dve", kf[:], yy[:], -math.pi, TWO_PI, ALU.is_lt, ALU.mult, [b], [b])
    tt(S, "dve", yy[:], yy[:], kf[:], ALU.add, [b], [b])
    ts(S, "dve", yy[:], yy[:], math.pi, -math.pi, ALU.min, ALU.max, [b], [b])
    act(S, out[:], yy[:], AF.Sin, [b], [b])
    return out


def ln_scratch(K, st):
    sc = Ctx()
    sc.i = 0
    sc.st = [sbt(K, st, "ln_st", [128, 12], F32) for _ in range(2)]
    sc.mv = [sbt(K, st, "ln_mv", [128, 2], F32) for _ in range(2)]
    sc.sd = [sbt(K, st, "ln_sd", [128, 1], F32) for _ in range(2)]
    sc.hn = [sbt(K, st, "ln_hn", [128, 1024], F32) for _ in range(2)]
    sc.b = [Buf() for _ in range(2)]
    sc.bh = [Buf() for _ in range(2)]
    sc.gt = sbt(K, st, "ln_g", [128, 1024], F32)
    sc.bt = sbt(K, st, "ln_b", [128, 1024], F32)
    sc.bgb = Buf()
    return sc


def ln_load_gb(K, S, sc, g_ap, b_ap):
    S.dma("sp", sc.gt[:], g_ap.partition_broadcast(128), (), [sc.bgb])
    S.dma("sp", sc.bt[:], b_ap.partition_broadcast(128), (), [sc.bgb])


def ln_apply(K, S, sc, src, src_buf, tt_i, dst_dram, dst_buf, ps, ps_buf, final_out=None, final_buf=None, defer=False):
    i = sc.i
    sc.i = (i + 1) % len(sc.hn)
    stt_, mv, sd, hn, b, bh = sc.st[i], sc.mv[i], sc.sd[i], sc.hn[i], sc.b[i], sc.bh[i]
    S.op("dve", lambda e: e.bn_stats(out=stt_[:, 0:6], in_=src[:, 0:512]), [src_buf], [b])
    S.op("dve", lambda e: e.bn_stats(out=stt_[:, 6:12], in_=src[:, 512:1024]), [src_buf], [b])
    S.op("dve", lambda e: e.bn_aggr(out=mv[:, 0:2], in_=stt_[:, 0:12]), [b], [b])
    ts1(S, "dve", sd[:], mv[:, 1:2], LN_EPS, ALU.add, [b], [b])
    act(S, sd[:], sd[:], AF.Sqrt, [b], [b])
    recip(S, sd[:], sd[:], [b], [b])
    stt(S, "dve", mv[:, 1:2], mv[:, 0:1], -1.0, sd[:, 0:1], ALU.mult, ALU.mult, [b], [b])
    S.op("act", lambda e: e.activation(out=hn[:], in_=src, func=AF.Identity, bias=mv[:, 1:2], scale=sd[:, 0:1]), [src_buf, b], [bh])
    tt(S, "dve", hn[:], hn[:], sc.gt[:], ALU.mult, [bh, sc.bgb], [bh])
    tt(S, "dve", hn[:], hn[:], sc.bt[:], ALU.add, [bh, sc.bgb], [bh])
    rows = slice(tt_i * 128, (tt_i + 1) * 128)
    if dst_dram is not None:
        S.dma("sp", dst_dram[rows, :], hn[:], [bh], [dst_buf])
    if final_out is not None:
        S.dma("sp", final_out[rows, :], hn[:], [bh], [final_buf])
        return
    def part2():
        for k in range(8):
            trp(S, ps[:, k * 128:(k + 1) * 128], hn[:, k * 128:(k + 1) * 128], K.identf[:], [bh], [ps_buf])
        cp(S, "act", K.HT[:, :, rows], ps.rearrange("p (k t) -> p k t", k=8), [ps_buf], [K.HTb[tt_i // 4]])
    if defer:
        return part2
    part2()


def stage_entry(K, S):
    with contextlib.ExitStack() as st:
        sc = ln_scratch(K, st)
        ln_load_gb(K, S, sc, K.inp["ln_in_g"], K.inp["ln_in_b"])
        xin = [sbt(K, st, "xin", [128, 1024], F32) for _ in range(2)]
        bx = [Buf() for _ in range(2)]
        for t in range(NTT):
            i = t % 2
            S.dma("sp", xin[i][:], K.inp["x"][t * 128:(t + 1) * 128, :], (), [bx[i]])
            ln_apply(K, S, sc, xin[i][:], bx[i], t, K.hres, K.b_hres[t // 4], K.PS[t % 2][:, :], K.PSb[t % 2])
        S.flush()


def build_bias(K, S, st, l, biasT, b_bias):
    Fd = K.Fd
    bF = Buf()
    rb = K.inp["rel_bias"][l]
    with contextlib.ExitStack() as st2:
        fsb = sbt(K, st2, "fsb", [8, 768], F32)
        bfs = Buf()
        S.dma("sp", fsb[:, 0:256], rb[:, 1:257], (), [bfs])
        cp(S, "dve", fsb[:, 256:768], fsb[:, 255:256].broadcast_to([8, 512]), [bfs], [bfs])
        S.dma("sp", Fd[:, :], fsb[:], [bfs], [bF])
        Tk = sbt(K, st2, "Tk", [128, 8, 5, 128], F32)
        bT = Buf()
        for h in range(8):
            for jr in range(5):
                src = bass.AP(tensor=Fd.tensor, offset=h * 768 + 512 - 128 * jr, ap=[[1, 128], [1, 128]])
                S.dma("sp", Tk[:, h, jr, :], src, [bF], [bT])
        for h in range(8):
            p0 = K.PS[3][:, 0:512]
            p1 = K.PS[3][:, 512:640]
            mm(S, p0, K.Jf[:], Tk[:, h, 0:4, :].rearrange("p j q -> p (j q)"), True, True, [bT], [K.PSb[3]])
            mm(S, p1, K.Jf[:], Tk[:, h, 4, :], True, True, [bT], [K.PSb[3]])
            cp(S, "act", biasT[:, h, :, :].rearrange("p j q -> p (j q)"), K.PS[3][:, 0:640], [K.PSb[3]], [b_bias])
            mset(S, "pool", biasT[64:128, h, 4, 0:64], -30000.0, [b_bias])
            mset(S, "pool", biasT[0:64, h, 0, 64:128], -30000.0, [b_bias])
        S.flush()


def load_w(K, S, dst, src, buf):
    S.dma("pool", dst, src, (), [buf])


def stage_attn(K, S, l):
    with contextlib.ExitStack() as st:
        w_in = K.inp["w_in"][l].rearrange("(k p) c -> p k c", p=128)
        wqkv = sbt(K, st, "wqkv", [128, 8, 1536], BF16)
        wg0 = sbt(K, st, "wg0", [128, 8, 1024], BF16)
        wpa = sbt(K, st, "wpa", [128, 4, 1024], BF16)
        bw = Buf()
        for k in range(8):
            load_w(K, S, wqkv[:, k, :], w_in[:, k, 0:1536], bw)
        bw_g = Buf()
        bw_p = Buf()
        for k in range(8):
            load_w(K, S, wg0[:, k, :], w_in[:, k, 5120:6144], bw_g)
        for k_ in range(4):
            load_w(K, S, wpa[:, k_, :], K.inp["w_proj_a"][l].rearrange("(k p) c -> p k c", p=128)[:, k_, :], bw_p)
        biasT = sbt(K, st, "biasT", [128, 8, 5, 128], F32)
        b_bias = Buf()
        build_bias(K, S, st, l, biasT, b_bias)
        if STOP_AFTER == "bias":
            S.dma("sp", K.dbg1[:, :], biasT[:].rearrange("p h j q -> p (h j q)"), [b_bias], [Buf()])
            S.flush()
            return
        qT = [sbt(K, st, "qT", [128, 4, 512], BF16) for _ in range(2)]
        b_qT = [Buf() for _ in range(2)]
        kring = sbt(K, st, "kring", [128, 4, 1024], BF16)
        b_kr = [Buf() for _ in range(2)]
        vring = sbt(K, st, "vring", [128, 8, 8, 65], BF16)
        b_vr = [Buf() for _ in range(8)]
        mset(S, "pool", vring[:], 1.0, b_vr)
        sbf = [sbt(K, st, "sbf", [128, 640], F32) for _ in range(2)]
        pT = [sbt(K, st, "pT", [128, 640], BF16) for _ in range(2)]
        b_sbf = [Buf() for _ in range(2)]
        b_pT = [Buf() for _ in range(2)]
        att = [sbt(K, st, "att", [128, 512], F32) for _ in range(2)]
        rs = [sbt(K, st, "rs", [128, 8], F32) for _ in range(2)]
        b_att = [Buf() for _ in range(2)]
        attT = [sbt(K, st, "attT", [128, 4, 512], BF16) for _ in range(2)]
        b_attT = [Buf() for _ in range(2)]
        gsb = [sbt(K, st, "gsb", [128, 512], F32) for _ in range(2)]
        b_gsb = [Buf() for _ in range(2)]
        mrg1 = sbt(K, st, "mrg", [128, 8, 512], BF16)
        mrg = [mrg1, mrg1]
        b_mrg1 = Buf()
        b_mrg = [b_mrg1, b_mrg1]
        psS = [K.PS[0], K.PS[1]]
        b_psS = [K.PSb[0], K.PSb[1]]
        psO = [K.PS[2][:, 0:512], K.PS[2][:, 512:1024]]
        b_psO = [K.PSh[4], K.PSh[5]]
        psM = [K.PS[3][:, 0:512], K.PS[3][:, 512:1024]]
        b_psM = [K.PSh[6], K.PSh[7]]
        K.PSh[6].w = K.PSb[3].w
        K.PSh[7].w = K.PSb[3].w
        K.PSh[6].r = dict(K.PSb[3].r)
        K.PSh[7].r = dict(K.PSb[3].r)
        mi = 0
        for tb in range(NTB if LIMIT_TB is None else LIMIT_TB):
            blk = slice(tb * 512, (tb + 1) * 512)
            qs = tb % 2
            hb = K.HTb[tb]
            for ot in range(8):
                m = mi % 2
                mi += 1
                for k in range(8):
                    mm(S, psM[m], wqkv[:, k, ot * 128:(ot + 1) * 128], K.HT[:, k, blk], k == 0, k == 7, [bw, hb], [b_psM[m]])
                if ot < 4:
                    act(S, qT[qs][:, ot, :], psM[m], AF.Copy, [b_psM[m]], [b_qT[qs]], scale=0.125)
                else:
                    cp(S, "dve", kring[:, ot - 4, qs * 512:(qs + 1) * 512], psM[m], [b_psM[m]], [b_kr[qs]])
            for t4 in range(4):
                t = tb * 4 + t4
                slot = t % 8
                m = mi % 2
                mi += 1
                for k in range(8):
                    mm(S, psM[m], K.HT[:, k, t * 128:(t + 1) * 128], wqkv[:, k, 1024:1536], k == 0, k == 7, [bw, hb], [b_psM[m]])
                cp(S, "act", vring[:, slot, :, 0:64], psM[m].rearrange("p (h d) -> p h d", h=8), [b_psM[m]], [b_vr[slot]])
            for s4 in range(4):
                sc_ = tb * 4 + s4
                nk = min(sc_, 4) + 1
                jr0 = 5 - nk
                ai = sc_ % 2
                def emit_scores(h):
                    hp, tq = h % 2, h // 2
                    x = h % 2
                    prt = slice(hp * 64, (hp + 1) * 64)
                    for j in range(nk):
                        kt = sc_ - (nk - 1) + j
                        slot = kt % 8
                        mm(S, psS[x][:, j * 128:(j + 1) * 128], kring[prt, tq, slot * 128:(slot + 1) * 128],
                           qT[qs][prt, tq, s4 * 128:(s4 + 1) * 128], True, True, [b_kr[slot // 4], b_qT[qs]], [b_psS[x]])

                def emit_rest(h):
                    x = h % 2
                    n = nk * 128
                    tt(S, "dve", sbf[x][:, 0:n], psS[x][:, 0:n],
                       biasT[:, h, jr0:5, :].rearrange("p j q -> p (j q)"), ALU.add, [b_psS[x], b_bias], [b_sbf[x]])
                    act(S, pT[x][:, 0:n], sbf[x][:, 0:n], AF.Exp, [b_sbf[x]], [b_pT[x]])
                    for j in range(nk):
                        kt = sc_ - (nk - 1) + j
                        slot = kt % 8
                        mm(S, psO[h // 4][:, (h % 4) * 65:(h % 4) * 65 + 65], pT[x][:, j * 128:(j + 1) * 128],
                           vring[:, slot, h, :], j == 0, j == nk - 1, [b_pT[x], b_vr[slot]], [b_psO[h // 4]])

                emit_scores(0)
                for h in range(8):
                    if h + 1 < 8:
                        emit_scores(h + 1)
                    emit_rest(h)
                for half in range(2):
                    pv = psO[half][:, 0:260].rearrange("p (h e) -> p h e", h=4)
                    recip(S, rs[ai][:, half * 4:(half + 1) * 4], pv[:, :, 64], [b_psO[half]], [b_att[ai]])
                    tt(S, "dve", att[ai][:, half * 256:(half + 1) * 256].rearrange("p (h d) -> p h d", h=4), pv[:, :, 0:64],
                       rs[ai][:, half * 4:(half + 1) * 4].unsqueeze(2).broadcast_to([128, 4, 64]), ALU.mult,
                       [b_psO[half], b_att[ai]], [b_att[ai]])
                m = mi % 2
                mi += 1
                for f in range(4):
                    trp(S, psM[m][:, f * 128:(f + 1) * 128], att[ai][:, f * 128:(f + 1) * 128], K.identf[:], [b_att[ai]], [b_psM[m]])
                cp(S, "act", attT[qs][:, :, s4 * 128:(s4 + 1) * 128], psM[m].rearrange("p (f t) -> p f t", f=4), [b_psM[m]], [b_attT[qs]])
            for ot in range(8):
                g = ot % 2
                m = mi % 2
                mi += 1
                for k in range(8):
                    mm(S, psM[m], wg0[:, k, ot * 128:(ot + 1) * 128], K.HT[:, k, blk], k == 0, k == 7, [bw_g, hb], [b_psM[m]])
                act(S, gsb[g][:], psM[m], AF.Sigmoid, [b_psM[m]], [b_gsb[g]])
                m = mi % 2
                mi += 1
                for f in range(4):
                    mm(S, psM[m], wpa[:, f, ot * 128:(ot + 1) * 128], attT[qs][:, f, :], f == 0, f == 3, [bw_p, b_attT[qs]], [b_psM[m]])
                tt(S, "dve", mrg[qs][:, ot, :], psM[m], gsb[g][:], ALU.mult, [b_psM[m], b_gsb[g]], [b_mrg[qs]])
            for o2 in range(4):
                S.dma("sp", K.merged[:, 2 * o2:2 * o2 + 2, blk], mrg[qs][:, 2 * o2:2 * o2 + 2, :], [b_mrg[qs]], [K.b_merged[tb]])
        S.flush()


def stage_ret(K, S, l):
    with contextlib.ExitStack() as st:
        w_in = K.inp["w_in"][l].rearrange("(k p) c -> p k c", p=128)
        wqk = sbt(K, st, "wqk", [128, 8, 1024], BF16)
        wv = sbt(K, st, "wv", [128, 8, 1024], BF16)
        wgr = sbt(K, st, "wgr", [128, 8, 1024], BF16)
        wg1 = sbt(K, st, "wg1", [128, 8, 1024], BF16)
        wpb = sbt(K, st, "wpb", [128, 8, 1024], BF16)
        bw = Buf()
        for k in range(8):
            load_w(K, S, wqk[:, k, :], w_in[:, k, 1536:2560], bw)
        bw_v, bw_gr, bw_g1, bw_pb = Buf(), Buf(), Buf(), Buf()
        for k in range(8):
            load_w(K, S, wv[:, k, :], w_in[:, k, 2560:3584], bw_v)
        for k in range(8):
            load_w(K, S, wgr[:, k, :], w_in[:, k, 3584:4608], bw_gr)
        for k in range(8):
            load_w(K, S, wg1[:, k, :], w_in[:, k, 6144:7168], bw_g1)
        for k_ in range(8):
            load_w(K, S, wpb[:, k_, :], K.inp["w_proj_b"][l].rearrange("(k p) c -> p k c", p=128)[:, k_, :], bw_pb)
        cosT = sbt(K, st, "cosT", [128, 32, 32], F32)
        sinT = sbt(K, st, "sinT", [128, 32, 32], F32)
        DT = sbt(K, st, "DT", [128, 8, 128], F32)
        qdecT = sbt(K, st, "qdecT", [128, 4, 128], F32)
        kdec = sbt(K, st, "kdec", [128, 8], F32)
        g128 = sbt(K, st, "g128", [128, 4], F32)
        btab = Buf()
        logg = [math.log(1.0 - 2.0 ** (-5.0 - h)) for h in range(8)]
        with contextlib.ExitStack() as st2:
            ii = sbt(K, st2, "ii", [128, 128], I32)
            ff = sbt(K, st2, "ff", [128, 128], F32)
            posf = sbt(K, st2, "posf", [128, 32], F32)
            invf = sbt(K, st2, "invf", [128, 32], F32)
            ang = sbt(K, st2, "ang", [128, 32, 32], F32)
            bt2 = Buf()
            S.op("pool", lambda e: e.iota(ii[:, 0:32], pattern=[[128, 32]], base=0, channel_multiplier=1), (), [bt2])
            cp(S, "dve", posf[:], ii[:, 0:32], [bt2], [bt2])
            for i in range(32):
                mset(S, "pool", invf[:, i:i + 1], float(np.float32(10000.0 ** (-2.0 * i / 64.0))), [bt2])
            tt(S, "dve", ang[:], posf[:].unsqueeze(2).broadcast_to([128, 32, 32]),
               invf[:].unsqueeze(1).broadcast_to([128, 32, 32]), ALU.mult, [bt2], [bt2])
            c_ = sincos(K, S, st2, ang[:], [128, 32, 32], bt2, True)
            cp(S, "dve", cosT[:], c_[:], [bt2], [btab])
            s_ = sincos(K, S, st2, ang[:], [128, 32, 32], bt2, False)
            cp(S, "dve", sinT[:], s_[:], [bt2], [btab])
            S.op("pool", lambda e: e.iota(ii[:], pattern=[[1, 128]], base=0, channel_multiplier=-1), [bt2], [bt2])
            cp(S, "dve", ff[:], ii[:], [bt2], [bt2])
            ts1(S, "dve", ang[:, 0:4, :].rearrange("p a b -> p (a b)"), ff[:], -1.0, ALU.mult, [bt2], [bt2])
            tt(S, "dve", ff[:], ff[:], ang[:, 0:4, :].rearrange("p a b -> p (a b)"), ALU.max, [bt2], [bt2])
            for h in range(8):
                act(S, DT[:, h, :], ff[:], AF.Exp, [bt2], [btab], scale=logg[h])
            mset(S, "pool", DT[64:128, :, 0:64], 0.0, [btab])
            S.op("pool", lambda e: e.iota(ii[:], pattern=[[1, 128]], base=1, channel_multiplier=0), [bt2], [bt2])
            cp(S, "dve", ff[:], ii[:], [bt2], [bt2])
            for h in range(8):
                prt = slice((h % 2) * 64, (h % 2) * 64 + 64)
                act(S, qdecT[prt, h // 2, :], ff[prt, :], AF.Exp, [bt2], [btab], scale=logg[h])
                mset(S, "pool", g128[prt, h // 2:h // 2 + 1], math.exp(128.0 * logg[h]), [btab])
            S.op("pool", lambda e: e.iota(ii[:, 0:1], pattern=[[0, 1]], base=127, channel_multiplier=-1), [bt2], [bt2])
            cp(S, "dve", ff[:, 0:1], ii[:, 0:1], [bt2], [bt2])
            for h in range(8):
                act(S, kdec[:, h:h + 1], ff[:, 0:1], AF.Exp, [bt2], [btab], scale=logg[h])
            S.flush()
        if STOP_AFTER == "rettab":
            return
        xs1 = sbt(K, st, "xs", [128, 512], F32)
        xs = [xs1, xs1]
        xr = [sbt(K, st, "xr", [128, 512], F32) for _ in range(2)]
        tmp = [sbt(K, st, "rtmp", [128, 8, 32], F32) for _ in range(4)]
        b_x1 = Buf()
        b_x = [b_x1, b_x1]
        kcd = sbt(K, st, "kcd", [128, 512], BF16)
        qcT = sbt(K, st, "qcT", [128, 4, 128], BF16)
        qcdT = sbt(K, st, "qcdT", [128, 4, 128], BF16)
        kcT = sbt(K, st, "kcT", [128, 4, 128], BF16)
        vbf = sbt(K, st, "vbf", [128, 1024], BF16)
        b_qk = Buf()
        b_v = Buf()
        sTb = sbt(K, st, "sTb", [128, 1024], BF16)
        b_sT = Buf()
        state = sbt(K, st, "state", [128, 4, 128], F32)
        state_bf = sbt(K, st, "state_bf", [128, 4, 128], BF16)
        b_state = Buf()
        b_sbf = Buf()
        ret = sbt(K, st, "ret", [128, 1024], F32)
        gs = sbt(K, st, "gs", [128, 1024], F32)
        sq = gs
        stat = sbt(K, st, "stat", [128, 6, 8], F32)
        b_ret = Buf()
        b_gs = Buf()
        b_stat = Buf()
        retT1 = sbt(K, st, "retT", [128, 8, 512], BF16)
        retT = [retT1, retT1]
        b_retT1 = Buf()
        b_retT = [b_retT1, b_retT1]
        gsb1 = sbt(K, st, "gsbB", [128, 512], F32)
        gsb = [gsb1, gsb1]
        b_gsb1 = Buf()
        b_gsb = [b_gsb1, b_gsb1]
        mprev = sbt(K, st, "mprev", [128, 8, 512], BF16)
        mrg = mprev
        b_mprev = Buf()
        b_mrg = b_mprev
        psS, b_pS = K.PS[0], K.PSb[0]
        psOo, b_pO = K.PS[1], K.PSb[1]
        psK, b_pK = K.PS[2], K.PSb[2]
        psM = [K.PS[3][:, 0:512], K.PS[3][:, 512:1024]]
        b_psM = [K.PSh[6], K.PSh[7]]
        mset(S, "pool", state[:], 0.0, [b_state])
        mset(S, "pool", state_bf[:], 0.0, [b_sbf])
        mi = 0
        for tb in range(NTB if LIMIT_TB is None else LIMIT_TB):
            blk = slice(tb * 512, (tb + 1) * 512)
            hb = K.HTb[tb]
            rs_ = tb % 2
            for o2 in range(4):
                S.dma("sp", mprev[:, 2 * o2:2 * o2 + 2, :], K.merged[:, 2 * o2:2 * o2 + 2, blk], [K.b_merged[tb]], [b_mprev])
            for s4 in range(4):
                t = tb * 4 + s4
                tok = slice(t * 128, (t + 1) * 128)
                C = cosT[:, t, :].unsqueeze(1).broadcast_to([128, 8, 32])
                Sn = sinT[:, t, :].unsqueeze(1).broadcast_to([128, 8, 32])
                for qk in range(2):
                    m = mi % 2
                    mi += 1
                    for k in range(8):
                        mm(S, psM[m], K.HT[:, k, tok], wqk[:, k, qk * 512:(qk + 1) * 512], k == 0, k == 7, [bw, hb], [b_psM[m]])
                    cp(S, "act", xs[qk][:], psM[m], [b_psM[m]], [b_x[qk]])
                    x3 = xs[qk][:].rearrange("p (h d) -> p h d", h=8)
                    o3 = xr[qk][:].rearrange("p (h d) -> p h d", h=8)
                    x1, x2 = x3[:, :, 0:32], x3[:, :, 32:64]
                    tt(S, "dve", tmp[0][:], x1, C, ALU.mult, [b_x[qk], btab], [b_x[qk]])
                    tt(S, "dve", tmp[1][:], x2, Sn, ALU.mult, [b_x[qk], btab], [b_x[qk]])
                    tt(S, "dve", o3[:, :, 0:32], tmp[0][:], tmp[1][:], ALU.subtract, [b_x[qk]], [b_x[qk]])
                    tt(S, "dve", tmp[2][:], x1, Sn, ALU.mult, [b_x[qk], btab], [b_x[qk]])
                    tt(S, "dve", tmp[3][:], x2, C, ALU.mult, [b_x[qk], btab], [b_x[qk]])
                    tt(S, "dve", o3[:, :, 32:64], tmp[2][:], tmp[3][:], ALU.add, [b_x[qk]], [b_x[qk]])
                if RET_CUT == 2:
                    S.flush()
                    return
                for half in range(2):
                    m = mi % 2
                    mi += 1
                    for k in range(8):
                        mm(S, psM[m], K.HT[:, k, tok], wv[:, k, half * 512:(half + 1) * 512], k == 0, k == 7, [bw_v, hb], [b_psM[m]])
                    cp(S, "act", vbf[:, half * 512:(half + 1) * 512], psM[m], [b_psM[m]], [b_v])
                if RET_CUT == 1:
                    S.flush()
                    return
                tt(S, "dve", kcd[:].rearrange("p (h d) -> p h d", h=8), xr[1][:].rearrange("p (h d) -> p h d", h=8),
                   kdec[:].unsqueeze(2).broadcast_to([128, 8, 64]), ALU.mult, [b_x[1], btab], [b_qk])
                m = mi % 2
                mi += 1
                for f in range(4):
                    trp(S, psM[m][:, f * 128:(f + 1) * 128], xr[0][:, f * 128:(f + 1) * 128], K.identf[:], [b_x[0]], [b_psM[m]])
                act(S, qcT[:].rearrange("p f t -> p (f t)"), psM[m], AF.Copy, [b_psM[m]], [b_qk], scale=0.125)
                stt(S, "dve", qcdT[:].rearrange("p f t -> p (f t)"), psM[m], 0.125, qdecT[:].rearrange("p f t -> p (f t)"),
                    ALU.mult, ALU.mult, [b_psM[m], btab], [b_qk])
                m = mi % 2
                mi += 1
                for f in range(4):
                    trp(S, psM[m][:, f * 128:(f + 1) * 128], xr[1][:, f * 128:(f + 1) * 128], K.identf[:], [b_x[1]], [b_psM[m]])
                cp(S, "act", kcT[:].rearrange("p f t -> p (f t)"), psM[m], [b_psM[m]], [b_qk])
                if RET_CUT == 3:
                    S.flush()
                    return
                for h in range(8):
                    prt = slice((h % 2) * 64, (h % 2) * 64 + 64)
                    c0 = (h % 2) * 512 + (h // 2) * 128
                    mm(S, psS[:, c0:c0 + 128], kcT[prt, h // 2, :], qcT[prt, h // 2, :], True, True, [b_qk], [b_pS])
                if RET_CUT == 40:
                    S.flush()
                    return
                for half in range(2):
                    cs = slice(half * 512, (half + 1) * 512)
                    tt(S, "dve", sTb[:, cs].rearrange("p (h l) -> p h l", h=4), psS[:, cs].rearrange("p (h l) -> p h l", h=4),
                       DT[:, half::2, :], ALU.mult, [b_pS, btab], [b_sT])
                if RET_CUT == 4:
                    S.flush()
                    return
                for h in range(8):
                    prt = slice((h % 2) * 64, (h % 2) * 64 + 64)
                    hs = slice(h * 128, (h + 1) * 128)
                    c0 = (h % 2) * 512 + (h // 2) * 128
                    mm(S, psOo[:, hs], sTb[:, c0:c0 + 128], vbf[:, hs], True, False, [b_sT, b_v], [b_pO])
                    mm(S, psOo[:, hs], qcdT[prt, h // 2, :], state_bf[prt, h // 2, :], False, True, [b_qk, b_sbf], [b_pO])
                if RET_CUT == 5:
                    S.flush()
                    return
                for h in range(8):
                    hs = slice(h * 128, (h + 1) * 128)
                    pr = (h // 2) * 128
                    mm(S, psK[:, hs], kcd[:, pr:pr + 128], vbf[:, hs], True, True, [b_qk, b_v], [b_pK])
                for h in range(8):
                    prt = slice((h % 2) * 64, (h % 2) * 64 + 64)
                    hs = slice(h * 128, (h + 1) * 128)
                    stt(S, "dve", state[prt, h // 2, :], state[prt, h // 2, :], g128[prt, h // 2:h // 2 + 1], psK[prt, hs],
                        ALU.mult, ALU.add, [b_state, b_pK, btab], [b_state])
                cp(S, "act", state_bf[:], state[:], [b_state], [b_sbf])
                if RET_CUT == 6:
                    S.flush()
                    return
                o3 = psOo[:, :].rearrange("p (h e) -> p h e", h=8)
                S.op("dve", lambda e, o3=o3: e.tensor_reduce(out=stat[:, 0, :], in_=o3, axis=AX.X, op=ALU.add), [b_pO], [b_stat])
                act(S, sq[:], psOo[:, :], AF.Square, [b_pO], [b_gs])
                S.op("dve", lambda e: e.tensor_reduce(out=stat[:, 1, :], in_=sq[:].rearrange("p (h e) -> p h e", h=8), axis=AX.X, op=ALU.add),
                     [b_gs], [b_stat])
                ts1(S, "dve", stat[:, 2, :], stat[:, 0, :], 1.0 / 128.0, ALU.mult, [b_stat], [b_stat])
                tt(S, "dve", stat[:, 3, :], stat[:, 2, :], stat[:, 2, :], ALU.mult, [b_stat], [b_stat])
                stt(S, "dve", stat[:, 4, :], stat[:, 1, :], 1.0 / 128.0, stat[:, 3, :], ALU.mult, ALU.subtract, [b_stat], [b_stat])
                ts1(S, "dve", stat[:, 4, :], stat[:, 4, :], LN_EPS, ALU.add, [b_stat], [b_stat])
                act(S, stat[:, 4, :], stat[:, 4, :], AF.Sqrt, [b_stat], [b_stat])
                recip(S, stat[:, 5, :], stat[:, 4, :], [b_stat], [b_stat])
                r3 = ret[:].rearrange("p (h e) -> p h e", h=8)
                tt(S, "dve", r3, o3, stat[:, 2, :].unsqueeze(2).broadcast_to([128, 8, 128]), ALU.subtract, [b_pO, b_stat, b_ret], [b_ret])
                tt(S, "dve", r3, r3, stat[:, 5, :].unsqueeze(2).broadcast_to([128, 8, 128]), ALU.mult, [b_ret, b_stat], [b_ret])
                if RET_CUT == 7:
                    S.flush()
                    return
                for half in range(2):
                    m = mi % 2
                    mi += 1
                    for k in range(8):
                        mm(S, psM[m], K.HT[:, k, tok], wgr[:, k, half * 512:(half + 1) * 512], k == 0, k == 7, [bw_gr, hb], [b_psM[m]])
                    act(S, gs[:, half * 512:(half + 1) * 512], psM[m], AF.Silu, [b_psM[m]], [b_gs])
                tt(S, "dve", ret[:], ret[:], gs[:], ALU.mult, [b_ret, b_gs], [b_ret])
                for half in range(2):
                    m = mi % 2
                    mi += 1
                    for f in range(4):
                        c0 = (half * 4 + f) * 128
                        trp(S, psM[m][:, f * 128:(f + 1) * 128], ret[:, c0:c0 + 128], K.identf[:], [b_ret], [b_psM[m]])
                    cp(S, "act", retT[rs_][:, half * 4:(half + 1) * 4, s4 * 128:(s4 + 1) * 128],
                       psM[m].rearrange("p (f t) -> p f t", f=4), [b_psM[m]], [b_retT[rs_]])
            for ot in range(8):
                g = ot % 2
                m = mi % 2
                mi += 1
                for k in range(8):
                    mm(S, psM[m], wg1[:, k, ot * 128:(ot + 1) * 128], K.HT[:, k, blk], k == 0, k == 7, [bw_g1, hb], [b_psM[m]])
                act(S, gsb[g][:], psM[m], AF.Sigmoid, [b_psM[m]], [b_gsb[g]])
                m = mi % 2
                mi += 1
                for f in range(8):
                    mm(S, psM[m], wpb[:, f, ot * 128:(ot + 1) * 128], retT[rs_][:, f, :], f == 0, f == 7, [bw_pb, b_retT[rs_]], [b_psM[m]])
                tt(S, "dve", gsb[g][:], psM[m], gsb[g][:], ALU.mult, [b_psM[m], b_gsb[g]], [b_gsb[g]])
                tt(S, "dve", mrg[:, ot, :], gsb[g][:], mprev[:, ot, :], ALU.add, [b_gsb[g], b_mprev], [b_mrg])
            for o2 in range(4):
                S.dma("sp", K.merged[:, 2 * o2:2 * o2 + 2, blk], mrg[:, 2 * o2:2 * o2 + 2, :], [b_mrg], [K.b_merged[tb]])
        S.flush()


PW = list(range(9)) + [16, 32, 64, 128, 256, 512, 1024, 2048]
PWI = {m: i for i, m in enumerate(PW)}
NPW = len(PW)


def s5_setup(K, S, l, Tt, ARt, AIt, AInt, b_T):
    with contextlib.ExitStack() as st:
        b = Buf()
        raw = sbt(K, st, "raw", [16, 3, 128], F32)
        ls2 = sbt(K, st, "ls2", [16, 2], F32)
        S.dma("sp", raw[:, 0, :], K.inp["lam_re"][l].rearrange("(pi g) p -> pi (g p)", g=2), (), [b])
        S.dma("sp", raw[:, 1, :], K.inp["lam_im"][l].rearrange("(pi g) p -> pi (g p)", g=2), (), [b])
        S.dma("sp", ls2[:], K.inp["log_step"][l].rearrange("(pi g) -> pi g", g=2), (), [b])
        cp(S, "dve", raw[:, 2, :].rearrange("q (g p) -> q g p", g=2), ls2[:].unsqueeze(2).broadcast_to([16, 2, 64]), [b], [b])
        draw = sbt(K, st, "draw", [4, 128], F32)
        S.dma("sp", draw[:], K.inp["d_skip"][l].rearrange("(a q) -> a q", q=128), (), [b])
        ps = K.PS[3]
        pb = K.PSb[3]
        for i in range(3):
            trp(S, ps[:, i * 16:(i + 1) * 16], raw[:, i, :], K.identf[0:16, 0:16], [b], [pb])
        trp(S, ps[:, 48:52], draw[:], K.identf[0:4, 0:4], [b], [pb])
        lam = sbt(K, st, "lam", [128, 4, 16], F32)
        dcol = sbt(K, st, "dcol", [128, 4], F32)
        cp(S, "dve", lam[:, 0:3, :].rearrange("p a b -> p (a b)"), ps[:, 0:48], [pb], [b])
        cp(S, "dve", dcol[:], ps[:, 48:52], [pb], [b])
        lr, li, stp = lam[:, 0, :], lam[:, 1, :], lam[:, 2, :]
        act(S, stp, stp, AF.Exp, [b], [b])
        er = sbt(K, st, "er", [128, 16], F32)
        th = sbt(K, st, "th", [128, 16], F32)
        tt(S, "dve", er[:], lr, stp, ALU.mult, [b], [b])
        tt(S, "dve", th[:], li, stp, ALU.mult, [b], [b])
        angM = sbt(K, st, "angM", [128, NPW + 1, 16], F32)
        magM = sbt(K, st, "magM", [128, NPW, 16], F32)
        for i, m in enumerate(PW):
            ts1(S, "dve", angM[:, i, :], th[:], float(m), ALU.mult, [b], [b])
            act(S, magM[:, i, :], er[:], AF.Exp, [b], [b], scale=float(m))
        ts1(S, "dve", angM[:, NPW, :], th[:], 0.5, ALU.mult, [b], [b])
        cosM = sincos(K, S, st, angM[:], [128, NPW + 1, 16], b, True)
        sinM = sincos(K, S, st, angM[:], [128, NPW + 1, 16], b, False)
        tt(S, "dve", ARt[:], magM[:], cosM[:, 0:NPW, :], ALU.mult, [b], [b_T])
        tt(S, "dve", AIt[:], magM[:], sinM[:, 0:NPW, :], ALU.mult, [b], [b_T])
        ts1(S, "dve", AInt[:], AIt[:], -1.0, ALU.mult, [b_T], [b_T])
        w = sbt(K, st, "wk", [128, 8, 16], F32)
        t0, t1, nr, ni, den, cr, ci, t2 = [w[:, i, :] for i in range(8)]
        ts(S, "dve", t0, er[:], 1.0 / 120.0, 1.0 / 24.0, ALU.mult, ALU.add, [b], [b])
        for cst in (1.0 / 6.0, 0.5, 1.0):
            tt(S, "dve", t0, t0, er[:], ALU.mult, [b], [b])
            ts1(S, "dve", t0, t0, cst, ALU.add, [b], [b])
        tt(S, "dve", t0, t0, er[:], ALU.mult, [b], [b])
        tt(S, "dve", nr, t0, cosM[:, 1, :], ALU.mult, [b], [b])
        tt(S, "dve", t1, sinM[:, NPW, :], sinM[:, NPW, :], ALU.mult, [b], [b])
        stt(S, "dve", nr, t1, -2.0, nr, ALU.mult, ALU.add, [b], [b])
        cp(S, "dve", ni, AIt[:, 1, :], [b, b_T], [b])
        tt(S, "dve", den, lr, lr, ALU.mult, [b], [b])
        tt(S, "dve", t1, li, li, ALU.mult, [b], [b])
        tt(S, "dve", den, den, t1, ALU.add, [b], [b])
        recip(S, den, den, [b], [b])
        tt(S, "dve", cr, nr, lr, ALU.mult, [b], [b])
        tt(S, "dve", t1, ni, li, ALU.mult, [b], [b])
        tt(S, "dve", cr, cr, t1, ALU.add, [b], [b])
        tt(S, "dve", cr, cr, den, ALU.mult, [b], [b])
        tt(S, "dve", ci, ni, lr, ALU.mult, [b], [b])
        tt(S, "dve", t1, nr, li, ALU.mult, [b], [b])
        tt(S, "dve", ci, ci, t1, ALU.subtract, [b], [b])
        tt(S, "dve", ci, ci, den, ALU.mult, [b], [b])
        Bn = sbt(K, st, "Bn", [128, 4, 16, 16], F32)
        for ri, nm in enumerate(("b_re", "b_im")):
            srcB = K.inp[nm][l].rearrange("(pi g) p k -> (g p) pi k", g=2)
            for pi_ in range(16):
                S.dma("sp", Bn[:, ri, pi_, :], srcB[:, pi_, :], (), [b])
        tmpB = sbt(K, st, "tmpB", [128, 16, 16], F32)

        def bc(x):
            return x.unsqueeze(2).broadcast_to([128, 16, 16])
        tt(S, "dve", Bn[:, 2], Bn[:, 0], bc(cr), ALU.mult, [b], [b])
        tt(S, "dve", tmpB[:], Bn[:, 1], bc(ci), ALU.mult, [b], [b])
        tt(S, "dve", Bn[:, 2], Bn[:, 2], tmpB[:], ALU.subtract, [b], [b])
        tt(S, "dve", Bn[:, 3], Bn[:, 1], bc(cr), ALU.mult, [b], [b])
        tt(S, "dve", tmpB[:], Bn[:, 0], bc(ci), ALU.mult, [b], [b])
        tt(S, "dve", Bn[:, 3], Bn[:, 3], tmpB[:], ALU.add, [b], [b])
        WA = sbt(K, st, "WA", [128, 2, 8, 16, 16], F32)
        for jp in range(8):
            pi_ = PWI[7 - jp]
            ar, ai = bc(ARt[:, pi_, :]), bc(AIt[:, pi_, :])
            tt(S, "dve", WA[:, 0, jp], Bn[:, 2], ar, ALU.mult, [b, b_T], [b])
            tt(S, "dve", tmpB[:], Bn[:, 3], ai, ALU.mult, [b, b_T], [b])
            tt(S, "dve", WA[:, 0, jp], WA[:, 0, jp], tmpB[:], ALU.subtract, [b], [b])
            tt(S, "dve", WA[:, 1, jp], Bn[:, 3], ar, ALU.mult, [b, b_T], [b])
            tt(S, "dve", tmpB[:], Bn[:, 2], ai, ALU.mult, [b, b_T], [b])
            tt(S, "dve", WA[:, 1, jp], WA[:, 1, jp], tmpB[:], ALU.add, [b], [b])
        craw = sbt(K, st, "craw", [128, 2, 4, 2, 64], F32)
        for ri, nm in enumerate(("c_re", "c_im")):
            src = K.inp[nm][l].rearrange("(a g) k p -> (g k) a p", g=8)
            for a in range(4):
                S.dma("sp", craw[:, ri, a, 0, :], src[:, a, :], (), [b])
                S.dma("sp", craw[:, ri, a, 1, :], src[:, a, :], (), [b])
        CT = sbt(K, st, "CT", [128, 2, 16, 16], F32)
        for ri in range(2):
            for a in range(4):
                pa_ = K.PS[2][:, (a % 2) * 512:(a % 2) * 512 + 128]
                pba = K.PSh[4 + a % 2]
                trp(S, pa_, craw[:, ri, a].rearrange("p d q -> p (d q)"), K.identf[:], [b], [pba])
                p3 = pa_.rearrange("p (g k) -> p g k", k=16)
                cp(S, "dve", CT[0:64, ri, 4 * a:4 * a + 4, :], p3[0:64, 0::2, :], [pba], [b])
                cp(S, "dve", CT[64:128, ri, 4 * a:4 * a + 4, :], p3[64:128, 1::2, :], [pba], [b])
        VA = sbt(K, st, "VA", [128, 2, 8, 16, 16], F32)
        for j in range(8):
            pi_ = PWI[j + 1]
            ar, ai = bc(ARt[:, pi_, :]), bc(AIt[:, pi_, :])
            tt(S, "dve", VA[:, 0, j], CT[:, 0], ar, ALU.mult, [b, b_T], [b])
            tt(S, "dve", tmpB[:], CT[:, 1], ai, ALU.mult, [b, b_T], [b])
            tt(S, "dve", VA[:, 0, j], VA[:, 0, j], tmpB[:], ALU.subtract, [b], [b])
            tt(S, "dve", VA[:, 1, j], CT[:, 0], ai, ALU.mult, [b, b_T], [b])
            tt(S, "dve", tmpB[:], CT[:, 1], ar, ALU.mult, [b, b_T], [b])
            tt(S, "dve", VA[:, 1, j], VA[:, 1, j], tmpB[:], ALU.add, [b], [b])
            ts1(S, "dve", VA[:, 1, j], VA[:, 1, j], -1.0, ALU.mult, [b], [b])
        Vst = [sbt(K, st, "Vst", [128, 2, 8, 128], BF16) for _ in range(4)]
        Nat = [sbt(K, st, "Nat", [128, 2, 8, 128], F32) for _ in range(4)]
        CTp = sbt(K, st, "CTp", [128, 4, 2, 128], F32)
        Wst = [sbt(K, st, "Wst", [128, 2, 8, 128], BF16) for _ in range(2)]
        b_V = [Buf() for _ in range(4)]
        b_N = [Buf() for _ in range(4)]
        b_W = [Buf() for _ in range(2)]
        b_C = Buf()
        for q in range(4):
            mset(S, "pool", Vst[q][:], 0.0, [b_V[q]])
            mset(S, "pool", Nat[q][:], 0.0, [b_N[q]])
        mset(S, "pool", CTp[:], 0.0, [b_C])
        psT = [K.PS[0], K.PS[1]]
        b_psT = [K.PSb[0], K.PSb[1]]
        wi = 0
        for a in range(4):
            for q in range(4):
                pi_ = 4 * a + q
                lo, hi = slice(0, 64), slice(64, 128)
                c0, c1 = slice(32 * q, 32 * q + 16), slice(32 * q + 16, 32 * q + 32)
                for ri in range(2):
                    cp(S, "dve", Vst[q][lo, ri, :, c0], VA[lo, ri, :, pi_, :], [b], [b_V[q]])
                    cp(S, "dve", Vst[q][hi, ri, :, c1], VA[hi, ri, :, pi_, :], [b], [b_V[q]])
                    cp(S, "dve", Nat[q][lo, ri, :, c0], WA[lo, ri, :, pi_, :], [b], [b_N[q]])
                    cp(S, "dve", Nat[q][hi, ri, :, c1], WA[hi, ri, :, pi_, :], [b], [b_N[q]])
                    cp(S, "dve", CTp[lo, q, ri, c0], CT[lo, ri, pi_, :], [b], [b_C])
                    cp(S, "dve", CTp[hi, q, ri, c1], CT[hi, ri, pi_, :], [b], [b_C])
                ts1(S, "dve", CTp[:, q, 1, :], CTp[:, q, 1, :], -1.0, ALU.mult, [b_C], [b_C])
                S.dma("sp", K.Vd[pi_], Vst[q][:], [b_V[q]], [K.b_Vd])
                ws = wi % 2
                wi += 1
                for ri in range(2):
                    for jh in range(2):
                        x = (ri * 2 + jh) % 2
                        for jj in range(4):
                            trp(S, psT[x][:, jj * 128:(jj + 1) * 128], Nat[q][:, ri, jh * 4 + jj, :], K.identf[:], [b_N[q]], [b_psT[x]])
                        cp(S, "act", Wst[ws][:, ri, jh * 4:jh * 4 + 4, :].rearrange("p j c -> p (j c)"), psT[x][:, 0:512], [b_psT[x]], [b_W[ws]])
                S.dma("sp", K.Wd[pi_], Wst[ws][:], [b_W[ws]], [K.b_Wd])
            for dl in range(8):
                x = dl % 2
                pT_ = K.PS[3][:, x * 512:x * 512 + 128]
                pbT = K.PSh[6 + x]
                n = 0
                for q in range(4):
                    for ri in range(2):
                        mm(S, pT_, Nat[q][:, ri, 7 - dl, :], CTp[:, q, ri, :], n == 0, n == 7, [b_N[q], b_C], [pbT])
                        n += 1
                if dl == 0:
                    stt(S, "dve", Tt[:, a, dl, :], K.identf[:], dcol[:, a:a + 1], pT_, ALU.mult, ALU.add, [pbT, b], [b_T])
                else:
                    cp(S, "dve", Tt[:, a, dl, :], pT_, [pbT], [b_T])
        S.flush()


def stage_s5(K, S, l, yT, b_yT, Tt, ARt, AIt, AInt, b_T):
    with contextlib.ExitStack() as st:
        w_in = K.inp["w_in"][l].rearrange("(k p) c -> p k c", p=128)
        wu = sbt(K, st, "wu", [128, 8, 512], BF16)
        bw = Buf()
        for k in range(8):
            load_w(K, S, wu[:, k, :], w_in[:, k, 4608:5120], bw)
        ucT = sbt(K, st, "ucT", [128, 4, T], BF16)
        b_uc = [Buf() for _ in range(4)]
        psM = [K.PS[3][:, 0:512], K.PS[3][:, 512:1024]]
        b_psM = [K.PSh[6], K.PSh[7]]
        mi = 0
        for tb in range(NTB):
            blk = slice(tb * 512, (tb + 1) * 512)
            for a in range(4):
                m = mi % 2
                mi += 1
                for k in range(8):
                    mm(S, psM[m], wu[:, k, a * 128:(a + 1) * 128], K.HT[:, k, blk], k == 0, k == 7, [bw, K.HTb[tb]], [b_psM[m]])
                cp(S, "act", ucT[:, a, blk], psM[m], [b_psM[m]], [b_uc[a]])
        Vsb = sbt(K, st, "Vsb", [128, 4, 2, 8, 128], BF16)
        Wsb = [sbt(K, st, "Wsb", [128, 2, 8, 128], BF16) for _ in range(2)]
        b_Vsb = Buf()
        b_Wsb = [Buf() for _ in range(2)]
        X = [[sbt(K, st, "X", [128, 512], F32) for _ in range(2)] for _ in range(2)]
        b_X = [[Buf() for _ in range(2)] for _ in range(2)]
        Xp = [sbt(K, st, "Xp", [128, 4, 512], BF16) for _ in range(2)]
        ptmp = sbt(K, st, "ptmp", [128, 512], F32)
        b_ptmp = Buf()
        b_Xp = Buf()
        psA = [K.PS[0][:, 0:512], K.PS[0][:, 512:1024], K.PS[1][:, 0:512], K.PS[1][:, 512:1024]]
        b_psA = [K.PSh[0], K.PSh[1], K.PSh[2], K.PSh[3]]
        for i in range(4):
            b_psA[i].w = K.PSb[i // 2].w
            b_psA[i].r = dict(K.PSb[i // 2].r)
        ai_ = 0
        wi = 0
        for a in range(4):
            for q in range(4):
                S.dma("sp", Vsb[:, q], K.Vd[4 * a + q], [K.b_Vd], [b_Vsb])
            mset(S, "pool", Xp[0][:, :, 0:1], 0.0, [b_Xp])
            mset(S, "pool", Xp[1][:, :, 0:1], 0.0, [b_Xp])
            for q in range(4):
                pi_ = 4 * a + q
                ws = wi % 2
                wi += 1
                S.dma("sp", Wsb[ws][:], K.Wd[pi_], [K.b_Wd], [b_Wsb[ws]])
                for ri in range(2):
                    x = ai_ % 4
                    ai_ += 1
                    for jp in range(8):
                        mm(S, psA[x], Wsb[ws][:, ri, jp, :], ucT[:, a, jp::8], jp == 0, jp == 7, [b_Wsb[ws], b_uc[a]], [b_psA[x]])
                    cp(S, "act", X[0][ri][:], psA[x], [b_psA[x]], [b_X[0][ri]])
                cur = 0
                for lv in range(9):
                    sft = 1 << lv
                    pw = PWI[8 * sft]
                    ar = ARt[:, pw, pi_:pi_ + 1]
                    ai = AIt[:, pw, pi_:pi_ + 1]
                    an = AInt[:, pw, pi_:pi_ + 1]
                    o, n_ = X[cur], X[1 - cur]
                    bo, bn = b_X[cur], b_X[1 - cur]
                    hd, tl, bd = slice(0, sft), slice(sft, 512), slice(0, 512 - sft)
                    stt(S, "dve", n_[0][:, tl], o[0][:, bd], ar, o[0][:, tl], ALU.mult, ALU.add, [bo[0], b_T], [bn[0]])
                    stt(S, "dve", n_[0][:, tl], o[1][:, bd], an, n_[0][:, tl], ALU.mult, ALU.add, [bo[1], b_T], [bn[0]])
                    cp(S, "act", n_[0][:, hd], o[0][:, hd], [bo[0]], [bn[0]])
                    stt(S, "dve", n_[1][:, tl], o[0][:, bd], ai, o[1][:, tl], ALU.mult, ALU.add, [bo[0], bo[1], b_T], [bn[1]])
                    stt(S, "dve", n_[1][:, tl], o[1][:, bd], ar, n_[1][:, tl], ALU.mult, ALU.add, [bo[1], b_T], [bn[1]])
                    cp(S, "act", n_[1][:, hd], o[1][:, hd], [bo[1]], [bn[1]])
                    cur = 1 - cur
                for ri in range(2):
                    cp(S, "act", Xp[ri][:, q, 1:512], X[cur][ri][:, 0:511], [b_X[cur][ri]], [b_Xp])
            for j in range(8):
                x = ai_ % 4
                ai_ += 1
                n = 0
                tot = 8 + j + 1
                for q in range(4):
                    for ri in range(2):
                        mm(S, psA[x], Vsb[:, q, ri, j, :], Xp[ri][:, q, :], n == 0, n == tot - 1, [b_Vsb, b_Xp], [b_psA[x]])
                        n += 1
                for jp in range(j + 1):
                    mm(S, psA[x], Tt[:, a, j - jp, :], ucT[:, a, jp::8], n == 0, n == tot - 1, [b_T, b_uc[a]], [b_psA[x]])
                    n += 1
                act(S, yT[:, a, j::8], psA[x], AF.Gelu_apprx_tanh, [b_psA[x]], [b_yT])
        S.flush()


def stage_merge(K, S, l, yT, b_yT):
    with contextlib.ExitStack() as st:
        w_in = K.inp["w_in"][l].rearrange("(k p) c -> p k c", p=128)
        wglu = sbt(K, st, "wglu", [128, 4, 512], BF16)
        wpc = sbt(K, st, "wpc", [128, 4, 1024], BF16)
        wg2 = sbt(K, st, "wg2", [128, 8, 1024], BF16)
        wo = sbt(K, st, "wo", [128, 8, 1024], BF16)
        bw = Buf()
        bw_pc, bw_g2, bw_o = Buf(), Buf(), Buf()
        for k_ in range(4):
            load_w(K, S, wglu[:, k_, :], K.inp["w_glu"][l].rearrange("(k p) c -> p k c", p=128)[:, k_, :], bw)
        for k_ in range(4):
            load_w(K, S, wpc[:, k_, :], K.inp["w_proj_c"][l].rearrange("(k p) c -> p k c", p=128)[:, k_, :], bw_pc)
        for k in range(8):
            load_w(K, S, wg2[:, k, :], w_in[:, k, 7168:8192], bw_g2)
        for k_ in range(8):
            load_w(K, S, wo[:, k_, :], K.inp["w_o"][l].rearrange("(k p) c -> p k c", p=128)[:, k_, :], bw_o)
        sc = ln_scratch(K, st)
        ln_load_gb(K, S, sc, K.inp["ln1_g"][l], K.inp["ln1_b"][l])
        zs = sbt(K, st, "zs", [128, 512], F32)
        b_zs = Buf()
        ygT = sbt(K, st, "ygT", [128, 4, 512], BF16)
        b_yg = Buf()
        gsb = sbt(K, st, "gsbC", [128, 512], F32)
        b_gsb = Buf()
        mprev = sbt(K, st, "mprevC", [128, 8, 512], BF16)
        b_mprev = Buf()
        mT = sbt(K, st, "mT", [128, 8, 512], BF16)
        b_mT = Buf()
        hprev = [sbt(K, st, "hprev", [128, 1024], F32) for _ in range(2)]
        b_hp = [Buf() for _ in range(2)]
        psM = [K.PS[3][:, 0:512], K.PS[3][:, 512:1024]]
        b_psM = [K.PSh[6], K.PSh[7]]
        mi = 0
        pend = None
        for tb in range(NTB):
            blk = slice(tb * 512, (tb + 1) * 512)
            hb = K.HTb[tb]
            for o2 in range(4):
                S.dma("sp", mprev[:, 2 * o2:2 * o2 + 2, :], K.merged[:, 2 * o2:2 * o2 + 2, blk], [K.b_merged[tb]], [b_mprev])
            for ot in range(4):
                m = mi % 2
                mi += 1
                for k in range(4):
                    mm(S, psM[m], wglu[:, k, ot * 128:(ot + 1) * 128], yT[:, k, blk], k == 0, k == 3, [bw, b_yT], [b_psM[m]])
                act(S, zs[:], psM[m], AF.Sigmoid, [b_psM[m]], [b_zs])
                tt(S, "dve", ygT[:, ot, :], yT[:, ot, blk], zs[:], ALU.mult, [b_yT, b_zs], [b_yg])
            for ot in range(8):
                m = mi % 2
                mi += 1
                for k in range(8):
                    mm(S, psM[m], wg2[:, k, ot * 128:(ot + 1) * 128], K.HT[:, k, blk], k == 0, k == 7, [bw_g2, hb], [b_psM[m]])
                act(S, gsb[:], psM[m], AF.Sigmoid, [b_psM[m]], [b_gsb])
                m = mi % 2
                mi += 1
                for k in range(4):
                    mm(S, psM[m], wpc[:, k, ot * 128:(ot + 1) * 128], ygT[:, k, :], k == 0, k == 3, [bw_pc, b_yg], [b_psM[m]])
                tt(S, "dve", gsb[:], psM[m], gsb[:], ALU.mult, [b_psM[m], b_gsb], [b_gsb])
                tt(S, "dve", mT[:, ot, :], gsb[:], mprev[:, ot, :], ALU.add, [b_gsb, b_mprev], [b_mT])
            for t4 in range(4):
                t = tb * 4 + t4
                i = t % 2
                S.dma("sp", hprev[i][:], K.hres[t * 128:(t + 1) * 128, :], [K.b_hres[tb]], [b_hp[i]])
                pso, bpso = K.PS[i], K.PSb[i]
                for half in range(2):
                    for f in range(8):
                        mm(S, pso[:, half * 512:(half + 1) * 512], mT[:, f, t4 * 128:(t4 + 1) * 128], wo[:, f, half * 512:(half + 1) * 512],
                           f == 0, f == 7, [bw_o, b_mT], [bpso])
                if pend is not None:
                    pend()
                for half in range(2):
                    cs = slice(half * 512, (half + 1) * 512)
                    stt(S, "dve", hprev[i][:, cs], hprev[i][:, cs], ALPHA, pso[:, cs], ALU.mult, ALU.add, [b_hp[i], bpso], [b_hp[i]])
                pend = ln_apply(K, S, sc, hprev[i][:], b_hp[i], t, K.hres, K.b_hres[tb], K.PS[2][:, :], K.PSb[2], defer=True)
        pend()
        S.flush()


def stage_ffn(K, S, l, last):
    HF = 1408
    for p in range(2):
        with contextlib.ExitStack() as st:
            w_up = K.inp["w_up"][l].rearrange("(k p) c -> p k c", p=128)
            wup = sbt(K, st, "wup", [128, 8, 2 * HF], BF16)
            wdn = sbt(K, st, "wdn", [128, 11, 1024], BF16)
            bw = Buf()
            bw_d = Buf()
            for k in range(8):
                load_w(K, S, wup[:, k, 0:HF], w_up[:, k, p * HF:(p + 1) * HF], bw)
                load_w(K, S, wup[:, k, HF:2 * HF], w_up[:, k, 2816 + p * HF:2816 + (p + 1) * HF], bw)
            for k_ in range(11):
                load_w(K, S, wdn[:, k_, :], K.inp["w_down"][l][p * HF:(p + 1) * HF, :].rearrange("(k p) c -> p k c", p=128)[:, k_, :], bw_d)
            craw = sbt(K, st, "cwraw", [88, 128], F32)
            cw = sbt(K, st, "cw", [128, 88], F32)
            b_cw = Buf()
            cwl = K.inp["conv_w"][l]
            cbl = K.inp["conv_b"][l]
            for t3 in range(3):
                S.dma("sp", craw[t3 * 22:t3 * 22 + 11, :], cwl[t3, p * HF:(p + 1) * HF].rearrange("(n q) -> n q", q=128), (), [b_cw])
                S.dma("sp", craw[t3 * 22 + 11:t3 * 22 + 22, :], cwl[t3, 2816 + p * HF:2816 + (p + 1) * HF].rearrange("(n q) -> n q", q=128), (), [b_cw])
            S.dma("sp", craw[66:77, :], cbl[p * HF:(p + 1) * HF].rearrange("(n q) -> n q", q=128), (), [b_cw])
            S.dma("sp", craw[77:88, :], cbl[2816 + p * HF:2816 + (p + 1) * HF].rearrange("(n q) -> n q", q=128), (), [b_cw])
            trp(S, K.PS[2][:, 0:88], craw[:], K.identf[0:88, 0:88], [b_cw], [K.PSb[2]])
            cp(S, "dve", cw[:], K.PS[2][:, 0:88], [K.PSb[2]], [b_cw])
            if RET_CUT == 101:
                S.flush()
                return
            sc = ln_scratch(K, st)
            if p == 1:
                ln_load_gb(K, S, sc, K.inp["ln2_g"][l], K.inp["ln2_b"][l])
            diagw = sbt(K, st, "diagw", [128, 22, 3, 128], BF16)
            b_dg = Buf()
            for c in range(22):
                for t3 in range(3):
                    ts1(S, "dve", diagw[:, c, t3, :], K.identf[:], cw[:, t3 * 22 + c:t3 * 22 + c + 1], ALU.mult, [b_cw], [b_dg])
            xbuf = [sbt(K, st, "xbuf", [128, 514], BF16) for _ in range(2)]
            cva = [sbt(K, st, "cva", [128, 512], F32) for _ in range(2)]
            b_xb = [Buf() for _ in range(2)]
            b_cva = [Buf() for _ in range(2)]
            halo = sbt(K, st, "halo", [128, 22, 2], BF16)
            b_halo = Buf()
            mset(S, "pool", halo[:], 0.0, [b_halo])
            actT = sbt(K, st, "actT", [128, 11, 512], BF16)
            b_actT = Buf()
            hprev = [sbt(K, st, "hprevF", [128, 1024], F32) for _ in range(2)]
            b_hp = [Buf() for _ in range(2)]
            psM = [K.PS[3][:, 0:512], K.PS[3][:, 512:1024]]
            b_psM = [K.PSh[6], K.PSh[7]]
            psC = [K.PS[2][:, 0:512], K.PS[2][:, 512:1024]]
            b_psC = [K.PSh[4], K.PSh[5]]
            for i_ in range(2):
                b_psC[i_].w = K.PSb[2].w
                b_psC[i_].r = dict(K.PSb[2].r)
            mi = 0
            pend2 = None
            cstate = [0]

            def conv_stage(n, wh, c, pa):
                xb, bx = xbuf[wh], b_xb[wh]
                m2 = cstate[0] % 2
                cstate[0] += 1
                for t3 in range(3):
                    mm(S, psC[m2], diagw[:, c, t3, :], xb[:, t3:t3 + 512], t3 == 0, t3 == 2, [b_dg, bx], [b_psC[m2]])
                if wh == 0:
                    S.op("act", lambda e: e.activation(out=cva[pa][:], in_=psC[m2], func=AF.Gelu_apprx_tanh,
                                                       bias=cw[:, 66 + c:67 + c], scale=1.0),
                         [b_psC[m2], b_cw], [b_cva[pa]])
                else:
                    stt(S, "dve", actT[:, n, :], psC[m2], cw[:, 66 + c:67 + c], cva[pa][:], ALU.add, ALU.mult,
                        [b_psC[m2], b_cw, b_cva[pa]], [b_actT])
                cp(S, "dve", halo[:, c, :], xb[:, 512:514], [bx], [b_halo])

            for tb in range(NTB):
                blk = slice(tb * 512, (tb + 1) * 512)
                hb = K.HTb[tb]
                pending = None
                for n in range(11):
                    pa = n % 2
                    for wh in range(2):
                        c = n + 11 * wh
                        m = mi % 2
                        mi += 1
                        xb, bx = xbuf[wh], b_xb[wh]
                        for k in range(8):
                            mm(S, psM[m], wup[:, k, c * 128:(c + 1) * 128], K.HT[:, k, blk], k == 0, k == 7, [bw, hb], [b_psM[m]])
                        if pending is not None and pending[1] == wh:
                            conv_stage(*pending)
                            pending = None
                        cp(S, "dve", xb[:, 0:2], halo[:, c, :], [b_halo], [bx])
                        cp(S, "act", xb[:, 2:514], psM[m], [b_psM[m]], [bx])
                        if pending is not None:
                            conv_stage(*pending)
                        pending = (n, wh, c, pa)
                conv_stage(*pending)
                if RET_CUT == 102:
                    S.flush()
                    return
                for t4 in range(4):
                    t = tb * 4 + t4
                    i = t % 2
                    rows = slice(t * 128, (t + 1) * 128)
                    src = K.hres if p == 0 else K.fpart
                    srcb = K.b_hres[tb] if p == 0 else K.b_fpart[tb]
                    S.dma("sp", hprev[i][:], src[rows, :], [srcb], [b_hp[i]])
                    pso, bpso = K.PS[i], K.PSb[i]
                    for half in range(2):
                        for f in range(11):
                            mm(S, pso[:, half * 512:(half + 1) * 512], actT[:, f, t4 * 128:(t4 + 1) * 128], wdn[:, f, half * 512:(half + 1) * 512],
                               f == 0, f == 10, [bw_d, b_actT], [bpso])
                    if pend2 is not None:
                        pend2()
                        pend2 = None
                    for half in range(2):
                        cs = slice(half * 512, (half + 1) * 512)
                        stt(S, "dve", hprev[i][:, cs], hprev[i][:, cs], ALPHA if p == 0 else 1.0, pso[:, cs], ALU.mult, ALU.add,
                            [b_hp[i], bpso], [b_hp[i]])
                    if RET_CUT == 103:
                        S.flush()
                        return
                    if p == 0:
                        S.dma("sp", K.fpart[rows, :], hprev[i][:], [b_hp[i]], [K.b_fpart[tb]])
                    elif last:
                        ln_apply(K, S, sc, hprev[i][:], b_hp[i], t, None, None, pso[:, :], bpso, final_out=K.y, final_buf=K.b_y)
                    else:
                        pend2 = ln_apply(K, S, sc, hprev[i][:], b_hp[i], t, K.hres, K.b_hres[tb], pso[:, :], bpso, defer=True)
            if pend2 is not None:
                pend2()
            S.flush()

IN_SPECS = [
    ("x", [T, D]), ("ln_in_g", [D]), ("ln_in_b", [D]), ("w_in", [NL, D, 8192]), ("rel_bias", [NL, 8, 257]),
    ("w_proj_a", [NL, 512, D]), ("w_proj_b", [NL, 1024, D]), ("w_proj_c", [NL, 512, D]),
    ("lam_re", [NL, 32, 64]), ("lam_im", [NL, 32, 64]), ("log_step", [NL, 32]),
    ("b_re", [NL, 32, 64, 16]), ("b_im", [NL, 32, 64, 16]), ("c_re", [NL, 32, 16, 64]), ("c_im", [NL, 32, 16, 64]),
    ("d_skip", [NL, 512]), ("w_glu", [NL, 512, 512]), ("w_o", [NL, D, D]), ("ln1_g", [NL, D]), ("ln1_b", [NL, D]),
    ("w_up", [NL, D, 5632]), ("conv_w", [NL, 3, 5632]), ("conv_b", [NL, 5632]), ("w_down", [NL, 2816, D]),
    ("ln2_g", [NL, D]), ("ln2_b", [NL, D]),
]


def build_program():
    nc = bass.Bass("TRN2", target_bir_lowering=False)
    K = Ctx()
    K.nc = nc
    K.uid = 0
    K.inp = {n: nc.dram_tensor(n, list(s), F32, kind="ExternalInput").ap() for n, s in IN_SPECS}
    K.y = nc.dram_tensor("y", [T, D], F32, kind="ExternalOutput").ap()
    dbg = "ExternalOutput" if DEBUG else "Internal"
    K.hres = nc.dram_tensor("hres", [T, D], F32, kind=dbg).ap()
    K.merged = nc.dram_tensor("merged", [128, 8, T], BF16, kind=dbg).ap()
    K.Fd = nc.dram_tensor("Fd", [8, 768], F32, kind="Internal").ap()
    K.Wd = nc.dram_tensor("Wd", [16, 128, 2, 8, 128], BF16, kind="Internal").ap()
    K.Vd = nc.dram_tensor("Vd", [16, 128, 2, 8, 128], BF16, kind="Internal").ap()
    K.b_Wd = Buf()
    K.b_Vd = Buf()
    if DEBUG:
        K.dbg2 = nc.dram_tensor("dbg2", [128, 4, T], BF16, kind="ExternalOutput").ap()
        K.dbg1 = nc.dram_tensor("dbg1", [128, 8 * 5 * 128], F32, kind="ExternalOutput").ap()
    K.fpart = nc.dram_tensor("fpart", [T, D], F32, kind="Internal").ap()
    K.b_fpart = [Buf() for _ in range(NTB)]
    K.b_hres = [Buf() for _ in range(NTB)]
    K.b_merged = [Buf() for _ in range(NTB)]
    K.b_y = Buf()
    with contextlib.ExitStack() as gst:
        S = Sched(nc, gst)
        K.HT = gst.enter_context(nc.sbuf_tensor("HT", [128, 8, T], BF16))
        K.HTb = [Buf() for _ in range(NTB)]
        K.identf = gst.enter_context(nc.sbuf_tensor("identf", [128, 128], F32))
        K.Jf = gst.enter_context(nc.sbuf_tensor("Jf", [128, 128], F32))
        io = gst.enter_context(nc.sbuf_tensor("iota_i", [128, 128], I32))
        K.PS = [gst.enter_context(nc.psum_tensor("PS%d" % i, [128, 1024], F32)) for i in range(4)]
        K.PSb = [Buf(excl=True) for _ in range(4)]
        K.PSh = [Buf(excl=True) for _ in range(8)]
        bc = Buf()
        S.op("pool", lambda e: e.iota(io[:], pattern=[[1, 128]], base=0, channel_multiplier=-1), (), [bc])
        ts1(S, "dve", K.identf[:], io[:], 0.0, ALU.is_equal, [bc], [bc])
        S.op("pool", lambda e: e.iota(io[:], pattern=[[1, 128]], base=0, channel_multiplier=1), [bc], [bc])
        ts1(S, "dve", K.Jf[:], io[:], 127.0, ALU.is_equal, [bc], [bc])
        S.flush()
        stage_entry(K, S)
        if STOP_AFTER == "entry":
            return nc
        for l in LAYERS:
            if STOP_AFTER == "ffnonly":
                stage_ffn(K, S, l, False)
                return nc
            if STOP_AFTER not in ("s5only", "rettab"):
                stage_attn(K, S, l)
            if STOP_AFTER in ("attn", "bias"):
                return nc
            if STOP_AFTER != "s5only":
                stage_ret(K, S, l)
            if STOP_AFTER in ("ret", "rettab"):
                return nc
            with contextlib.ExitStack() as sty:
                Tt = sbt(K, sty, "Tt", [128, 4, 8, 128], BF16)
                ARt = sbt(K, sty, "ARt", [128, NPW, 16], F32)
                AIt = sbt(K, sty, "AIt", [128, NPW, 16], F32)
                AInt = sbt(K, sty, "AInt", [128, NPW, 16], F32)
                b_T = Buf()
                s5_setup(K, S, l, Tt, ARt, AIt, AInt, b_T)
                yT = sbt(K, sty, "yT", [128, 4, T], BF16)
                b_yT = Buf()
                stage_s5(K, S, l, yT, b_yT, Tt, ARt, AIt, AInt, b_T)
                if STOP_AFTER in ("s5", "s5only"):
                    S.dma("sp", K.dbg2[:, :, :], yT[:], [b_yT], [Buf()])
                    S.flush()
                    return nc
                stage_merge(K, S, l, yT, b_yT)
            if STOP_AFTER == "merge":
                return nc
            stage_ffn(K, S, l, l == NL - 1)
            if STOP_AFTER == "ffn":
                return nc
    return nc


_PROG = None


def kernel(**inputs):
    global _PROG
    if _PROG is None:
        _PROG = build_program()
    nc = _PROG
    x = np.ascontiguousarray(np.asarray(inputs["x"], dtype=np.float32))
    shared = {n: np.ascontiguousarray(np.asarray(inputs[n], dtype=np.float32)) for n, _ in IN_SPECS if n != "x"}
    in_maps = []
    for c in range(8):
        m = dict(shared)
        m["x"] = x[c]
        in_maps.append(m)
    res = run_bass_kernel_spmd(nc, in_maps, core_ids=list(range(8)))
    return np.stack([r["y"] for r in res.results], axis=0).astype(np.float32)
```
